# Optimizing a Trainium2 kernel written in Bass

```python
import math
import jax, jax.numpy as jnp
from jax import lax
import numpy as np

D_MODEL = 1024
BATCH = 8
SEQ = 2048
DEPTH = 4
DEC_BATCH = 128
DEC_SEQ = 1
PAST_LEN = 16384
PAGE_SIZE = 128

N_MIXERS = 2
N_A = (DEPTH + 1) // 2
N_B = DEPTH // 2
A_HEADS = 8
A_DK = 128
A_DV = 128
A_QK_DIM = A_HEADS * A_DK
A_V_DIM = A_HEADS * A_DV
A_CONV_DIM = 2 * A_QK_DIM + A_V_DIM
A_IN_DIM = A_CONV_DIM + A_V_DIM + 2 * A_HEADS
CONV_W = 4
CHUNK_A = 64
B_WIDTH = 2 * D_MODEL
B_GROUPS = 8
B_GROUP_DIM = B_WIDTH // B_GROUPS
CHUNK_B = 128
D_FF = 2816
FFN_RES = 0.5
N_SUB = 3
EPS = 1e-6

kernel_name = "hybrid_gdn_gmlp_macaron_adaln_step"


def rms_norm(x, w):
    xf = x.astype(jnp.float32)
    y = xf * lax.rsqrt(jnp.mean(xf * xf, axis=-1, keepdims=True) + EPS)
    return (y * w.astype(jnp.float32)).astype(x.dtype)


def layer_norm(x, w, b):
    xf = x.astype(jnp.float32)
    mu = jnp.mean(xf, axis=-1, keepdims=True)
    xc = xf - mu
    y = xc * lax.rsqrt(jnp.mean(xc * xc, axis=-1, keepdims=True) + EPS)
    return (y * w.astype(jnp.float32) + b.astype(jnp.float32)).astype(x.dtype)


def l2_norm(x):
    xf = x.astype(jnp.float32)
    return xf * lax.rsqrt(jnp.sum(xf * xf, axis=-1, keepdims=True) + EPS)


def swiglu(h, w_gu, w_down):
    gu = h @ w_gu
    return (jax.nn.silu(gu[..., :D_FF]) * gu[..., D_FF:]) @ w_down


def gated_delta_rule(q, k, v, g, beta, S0):
    f32 = jnp.float32
    Bn, L, H, _ = q.shape
    C = min(CHUNK_A, L)
    n = -(-L // C)
    pad = n * C - L

    def blocks(t):
        t = jnp.pad(t.astype(f32), [(0, 0), (0, pad)] + [(0, 0)] * (t.ndim - 2))
        t = t.reshape((Bn, n, C) + t.shape[2:])
        return jnp.moveaxis(t, 3, 1)

    q = blocks(q) * (A_DK ** -0.5)
    k, v, g, beta = blocks(k), blocks(v), blocks(g), blocks(beta)
    decay = jnp.cumsum(g, axis=-1)
    idx = jnp.arange(C)
    causal = idx[:, None] >= idx[None, :]
    strict = idx[:, None] > idx[None, :]
    gamma = jnp.exp(jnp.where(causal, decay[..., :, None] - decay[..., None, :], -jnp.inf))
    k_beta = k * beta[..., None]
    m = jnp.where(strict, jnp.einsum('bhnck,bhnsk->bhncs', k_beta, k) * gamma, 0.0)
    eye = jnp.eye(C, dtype=f32)
    rhs = jnp.concatenate([v * beta[..., None], k_beta * jnp.exp(decay)[..., None]], axis=-1)
    sol = lax.linalg.triangular_solve(eye + m, rhs, left_side=True, lower=True, unit_diagonal=True)
    u_val, w_key = sol[..., :A_DV], sol[..., A_DV:]
    attn = jnp.where(causal, jnp.einsum('bhnck,bhnsk->bhncs', q, k) * gamma, 0.0)

    def step(S, xs):
        q_c, k_c, u_c, w_c, attn_c, d_c = xs
        v_new = u_c - jnp.einsum('bhck,bhkv->bhcv', w_c, S)
        o_c = (jnp.einsum('bhck,bhkv->bhcv', q_c * jnp.exp(d_c)[..., None], S)
               + jnp.einsum('bhcs,bhsv->bhcv', attn_c, v_new))
        d_last = d_c[..., -1]
        S = (S * jnp.exp(d_last)[..., None, None]
             + jnp.einsum('bhck,bhcv->bhkv', k_c * jnp.exp(d_last[..., None] - d_c)[..., None], v_new))
        return S, o_c

    xs = tuple(jnp.moveaxis(t, 2, 0) for t in (q, k, u_val, w_key, attn, decay))
    S, o = lax.scan(step, S0.astype(f32), xs)
    o = jnp.moveaxis(jnp.moveaxis(o, 0, 2), 1, 3)
    o = o.reshape(Bn, n * C, H, A_DV)[:, :L]
    return o, S


def mixer_a(h, conv_buf, S0, w_in, w_conv, a_log, dt_bias, w_onorm, w_out):
    Bn, L, _ = h.shape
    proj = h @ w_in
    qkv_pre = proj[..., :A_CONV_DIM]
    z = proj[..., A_CONV_DIM:A_CONV_DIM + A_V_DIM]
    b_logit = proj[..., A_CONV_DIM + A_V_DIM:A_CONV_DIM + A_V_DIM + A_HEADS]
    a_logit = proj[..., A_CONV_DIM + A_V_DIM + A_HEADS:]
    full = jnp.concatenate([conv_buf.astype(qkv_pre.dtype), qkv_pre], axis=1)
    conv = full[:, 0:L] * w_conv[0]
    for j in range(1, CONV_W):
        conv = conv + full[:, j:j + L] * w_conv[j]
    qkv = jax.nn.silu(conv)
    new_buf = full[:, L:]
    q = l2_norm(qkv[..., :A_QK_DIM].reshape(Bn, L, A_HEADS, A_DK))
    k = l2_norm(qkv[..., A_QK_DIM:2 * A_QK_DIM].reshape(Bn, L, A_HEADS, A_DK))
    v = qkv[..., 2 * A_QK_DIM:].reshape(Bn, L, A_HEADS, A_DV)
    beta = jax.nn.sigmoid(b_logit.astype(jnp.float32))
    g = -jnp.exp(a_log.astype(jnp.float32)) * jax.nn.softplus(a_logit.astype(jnp.float32) + dt_bias.astype(jnp.float32))
    o, S = gated_delta_rule(q, k, v, g, beta, S0)
    o = rms_norm(o.astype(h.dtype), w_onorm) * jax.nn.silu(z.reshape(Bn, L, A_HEADS, A_DV))
    y = o.reshape(Bn, L, A_V_DIM) @ w_out
    return y, new_buf, S.astype(S0.dtype)


def mixer_b(h, w_in, vnorm_w, vnorm_b, w_s, b_s, w_out):
    Bn, L, _ = h.shape
    zz = jax.nn.gelu(h @ w_in)
    u, v = zz[..., :B_WIDTH], zz[..., B_WIDTH:]
    v = layer_norm(v, vnorm_w, vnorm_b)
    n = -(-L // CHUNK_B)
    pad = n * CHUNK_B - L
    vb = jnp.pad(v, ((0, 0), (0, pad), (0, 0))).reshape(Bn, n, CHUNK_B, B_GROUPS, B_GROUP_DIM)
    causal = jnp.tril(jnp.ones((CHUNK_B, CHUNK_B), dtype=bool))
    ws = jnp.where(causal, w_s, 0.0)
    mixed = jnp.einsum('gts,bnsgd->bntgd', ws, vb) + b_s.T[:, :, None]
    mixed = mixed.reshape(Bn, n * CHUNK_B, B_WIDTH)[:, :L]
    y = (u * mixed) @ w_out
    return y, v


def trunk(x, c, conv_bufs, states, w_ada, b_ada, norm_w, ffn_w_gu, ffn_w_down,
          a_w_in, a_w_conv, a_log, a_dt_bias, a_w_onorm, a_w_out,
          b_w_in, b_vnorm_w, b_vnorm_b, b_w_s, b_b_s, b_w_out, final_norm_w):
    Bn = x.shape[0]
    new_conv, new_S, v_rows = [], [], []
    for i in range(DEPTH):
        mod = (jax.nn.silu(c) @ w_ada[i] + b_ada[i]).reshape(Bn, N_SUB, 3, D_MODEL)
        shift, scale, gate = mod[:, :, 0], mod[:, :, 1], mod[:, :, 2]
        h = rms_norm(x, norm_w[i, 0]) * (1.0 + scale[:, 0, None]) + shift[:, 0, None]
        x = x + FFN_RES * (1.0 + gate[:, 0, None]) * swiglu(h, ffn_w_gu[i, 0], ffn_w_down[i, 0])
        h = rms_norm(x, norm_w[i, 1]) * (1.0 + scale[:, 1, None]) + shift[:, 1, None]
        li = i // N_MIXERS
        if i % N_MIXERS == 0:
            y, buf, S = mixer_a(h, conv_bufs[li], states[li], a_w_in[li], a_w_conv[li], a_log[li],
                                a_dt_bias[li], a_w_onorm[li], a_w_out[li])
            new_conv.append(buf)
            new_S.append(S)
        else:
            y, v = mixer_b(h, b_w_in[li], b_vnorm_w[li], b_vnorm_b[li], b_w_s[li], b_b_s[li], b_w_out[li])
            v_rows.append(v)
        x = x + (1.0 + gate[:, 1, None]) * y
        h = rms_norm(x, norm_w[i, 2]) * (1.0 + scale[:, 2, None]) + shift[:, 2, None]
        x = x + FFN_RES * (1.0 + gate[:, 2, None]) * swiglu(h, ffn_w_gu[i, 1], ffn_w_down[i, 1])
    y = rms_norm(x, final_norm_w)
    return y, jnp.stack(new_conv), jnp.stack(new_S), jnp.stack(v_rows)


def setup_inputs(seed: int = 0) -> dict:
    key = jax.random.key(seed)
    ks = jax.random.split(key, 26)
    f32 = jnp.float32

    def nrm(k, shape, s):
        return jax.random.normal(k, shape, f32) * s

    dt = jnp.exp(jax.random.uniform(ks[14], (N_A, A_HEADS), f32, math.log(1e-3), math.log(1e-1)))
    return {
        "x_prompt": nrm(ks[0], (BATCH, SEQ, D_MODEL), 1.0),
        "x_sample": nrm(ks[1], (DEC_BATCH, DEC_SEQ, D_MODEL), 1.0),
        "state_a_conv": nrm(ks[2], (N_A, DEC_BATCH, CONV_W - 1, A_CONV_DIM), 1.0),
        "state_a_S": nrm(ks[3], (N_A, DEC_BATCH, A_HEADS, A_DK, A_DV), 0.5),
        "c_prompt": nrm(ks[4], (BATCH, D_MODEL), 1.0),
        "c_sample": nrm(ks[5], (DEC_BATCH, D_MODEL), 1.0),
        "w_ada": nrm(ks[6], (DEPTH, D_MODEL, N_SUB * 3 * D_MODEL), 0.1 * D_MODEL ** -0.5),
        "b_ada": nrm(ks[7], (DEPTH, N_SUB * 3 * D_MODEL), 0.01),
        "norm_w": 1.0 + nrm(ks[8], (DEPTH, N_SUB, D_MODEL), 0.02),
        "ffn_w_gu": nrm(ks[9], (DEPTH, 2, D_MODEL, 2 * D_FF), D_MODEL ** -0.5),
        "ffn_w_down": nrm(ks[10], (DEPTH, 2, D_FF, D_MODEL), D_FF ** -0.5),
        "a_w_in": nrm(ks[11], (N_A, D_MODEL, A_IN_DIM), D_MODEL ** -0.5),
        "a_w_conv": nrm(ks[12], (N_A, CONV_W, A_CONV_DIM), CONV_W ** -0.5),
        "a_log": jnp.log(jax.random.uniform(ks[13], (N_A, A_HEADS), f32, 1.0, 16.0)),
        "a_dt_bias": dt + jnp.log(-jnp.expm1(-dt)),
        "a_w_onorm": 1.0 + nrm(ks[15], (N_A, A_DV), 0.02),
        "a_w_out": nrm(ks[16], (N_A, A_V_DIM, D_MODEL), A_V_DIM ** -0.5),
        "b_w_in": nrm(ks[17], (N_B, D_MODEL, 2 * B_WIDTH), D_MODEL ** -0.5),
        "b_vnorm_w": 1.0 + nrm(ks[18], (N_B, B_WIDTH), 0.02),
        "b_vnorm_b": nrm(ks[19], (N_B, B_WIDTH), 0.01),
        "b_w_s": nrm(ks[20], (N_B, B_GROUPS, CHUNK_B, CHUNK_B), CHUNK_B ** -0.5),
        "b_b_s": 1.0 + nrm(ks[21], (N_B, B_GROUPS, CHUNK_B), 0.02),
        "b_w_out": nrm(ks[22], (N_B, B_WIDTH, D_MODEL), B_WIDTH ** -0.5),
        "final_norm_w": 1.0 + nrm(ks[23], (D_MODEL,), 0.02),
    }


def reference(x_prompt, x_sample, state_a_conv, state_a_S, c_prompt, c_sample,
              w_ada, b_ada, norm_w, ffn_w_gu, ffn_w_down,
              a_w_in, a_w_conv, a_log, a_dt_bias, a_w_onorm, a_w_out,
              b_w_in, b_vnorm_w, b_vnorm_b, b_w_s, b_b_s, b_w_out, final_norm_w):
    zero_conv = jnp.zeros((N_A, x_prompt.shape[0], CONV_W - 1, A_CONV_DIM), x_prompt.dtype)
    zero_S = jnp.zeros((N_A, x_prompt.shape[0], A_HEADS, A_DK, A_DV), state_a_S.dtype)
    y_prompt, conv_prompt, S_prompt, _ = trunk(
        x_prompt, c_prompt, zero_conv, zero_S, w_ada, b_ada, norm_w, ffn_w_gu, ffn_w_down,
        a_w_in, a_w_conv, a_log, a_dt_bias, a_w_onorm, a_w_out,
        b_w_in, b_vnorm_w, b_vnorm_b, b_w_s, b_b_s, b_w_out, final_norm_w)
    y_sample, conv_sample, S_sample, v_sample = trunk(
        x_sample, c_sample, state_a_conv, state_a_S, w_ada, b_ada, norm_w, ffn_w_gu, ffn_w_down,
        a_w_in, a_w_conv, a_log, a_dt_bias, a_w_onorm, a_w_out,
        b_w_in, b_vnorm_w, b_vnorm_b, b_w_s, b_b_s, b_w_out, final_norm_w)
    return (y_prompt, y_sample, conv_prompt, S_prompt, conv_sample, S_sample, v_sample)
```

```python
import contextlib
import numpy as np
import concourse.bass as bass
import concourse.mybir as mybir
from concourse.bass_utils import run_bass_kernel_spmd

F32 = mybir.dt.float32
BF16 = mybir.dt.bfloat16
AF = mybir.ActivationFunctionType
ALU = mybir.AluOpType
AX = mybir.AxisListType

D = 1024
TP = 2048
NS = 16
T = TP + NS
DEPTH = 4
DFF = 2816
NJ = DFF // 128
EPS = 1e-6
TILES = [(0, 512), (512, 512), (1024, 512), (1536, 512), (2048, 16)]
BIG = 1.0e4


class Res:
    __slots__ = ("name", "w", "rd")

    def __init__(self, name):
        self.name = name
        self.w = None
        self.rd = []


class Eng:
    def __init__(self, name, eng, is_pe=False):
        self.name = name
        self.eng = eng
        self.sem = None
        self.cnt = 0
        self.seen = {}
        self.is_pe = is_pe


class FW:
    def __init__(self, nc, stack):
        self.nc = nc
        self.stack = stack
        self.pe = Eng("pe", nc.tensor, True)
        self.dve = Eng("dve", nc.vector)
        self.act = Eng("act", nc.scalar)
        self.pool = Eng("pool", nc.gpsimd)
        self.sp = Eng("sp", nc.sync)
        self.engs = [self.pe, self.dve, self.act, self.pool, self.sp]
        for e in self.engs:
            e.sem = stack.enter_context(nc.semaphore("s_" + e.name))
        self.dma_sems = {}
        self.res = {}

    def R(self, *key):
        r = self.res.get(key)
        if r is None:
            r = Res(str(key))
            self.res[key] = r
        return r

    def sbuf(self, name, shape, dtype, stack=None):
        self.nsb = getattr(self, "nsb", 0) + 1
        return (stack or self.stack).enter_context(self.nc.sbuf_tensor("sb%d_%s" % (self.nsb, name), list(shape), dtype))

    def dsem(self, name):
        if name not in self.dma_sems:
            h = self.stack.enter_context(self.nc.semaphore("d%d" % len(self.dma_sems)))
            self.dma_sems[name] = [h, 0]
        return self.dma_sems[name]

    def _collect(self, E, reads, writes):
        toks = []
        for r in reads:
            if r.w is not None:
                toks.append(r.w)
        for w in writes:
            if w.w is not None:
                toks.append(w.w)
            toks.extend(w.rd)
        need = {}
        for t in toks:
            kind, key, sem, val = t
            if kind == "eng" and key is E and E.is_pe:
                continue
            if E.seen.get(id(sem), 0) >= val:
                continue
            if need.get(id(sem), (None, 0))[1] < val:
                need[id(sem)] = (sem, val)
        for sem, val in need.values():
            E.seen[id(sem)] = val
            E.eng.wait_ge(sem, val)

    def _mark(self, tok, reads, writes):
        for r in reads:
            if len(r.rd) > 64:
                r.rd = r.rd[-32:]
            r.rd.append(tok)
        for w in writes:
            w.w = tok
            w.rd = []

    def op(self, E, fn, reads=(), writes=()):
        pr = [r for r in reads if r.name.startswith("('ps'")]
        if pr:
            writes = list(writes) + pr
        self._collect(E, reads, writes)
        E.cnt += 1
        fn(E.eng).then_inc(E.sem, 1)
        tok = ("eng", E, E.sem, E.cnt)
        self._mark(tok, reads, writes)
        return tok

    def dma(self, E, stream, fn, reads=(), writes=()):
        self._collect(E, reads, writes)
        ds = self.dsem(stream)
        ds[1] += 16
        fn(E.eng).then_inc(ds[0], 16)
        tok = ("dma", stream, ds[0], ds[1])
        self._mark(tok, reads, writes)
        return tok

    def barrier(self):
        for E in self.engs:
            for O in self.engs:
                if O is not E and O.cnt > 0 and E.seen.get(id(O.sem), 0) < O.cnt:
                    E.seen[id(O.sem)] = O.cnt
                    E.eng.wait_ge(O.sem, O.cnt)
            for name, (h, v) in self.dma_sems.items():
                if v > 0 and E.seen.get(id(h), 0) < v:
                    E.seen[id(h)] = v
                    E.eng.wait_ge(h, v)
        self.res = {}

    def finish(self):
        for name, (h, v) in self.dma_sems.items():
            if v > 0 and self.sp.seen.get(id(h), 0) < v:
                self.sp.eng.wait_ge(h, v)
        for E in self.engs:
            if E is not self.sp and E.cnt > 0:
                self.sp.eng.wait_ge(E.sem, E.cnt)


def build_program(stop_after=None):
    nc = bass.Bass("TRN2", target_bir_lowering=False)

    def din(name, shape):
        return nc.dram_tensor(name, list(shape), F32, kind="ExternalInput").ap()

    def dout(name, shape):
        return nc.dram_tensor(name, list(shape), F32, kind="ExternalOutput").ap()

    xT_in = din("xT", [D, T])
    cT_in = din("cT", [D, 17])
    convT_in = din("convT", [2, 3072, NS, 3])
    S_in = din("S_in", [2, NS, 8, 128, 128])
    w_ada = din("w_ada", [DEPTH, D, 9216])
    b_ada_p = din("b_ada_p", [128, DEPTH * 72])
    norm_w_p = din("norm_w_p", [128, DEPTH * 3 * 8])
    ffn_w_gu = din("ffn_w_gu", [DEPTH, 2, D, 2 * DFF])
    ffn_w_down = din("ffn_w_down", [DEPTH, 2, DFF, D])
    a_w_in = din("a_w_in", [2, D, 4112])
    a_w_conv_p = din("a_w_conv_p", [128, 2 * 4 * 24])
    a_log = din("a_log", [2, 8])
    a_dt_bias = din("a_dt_bias", [2, 8])
    a_w_onorm_p = din("a_w_onorm_p", [128, 2])
    a_w_out = din("a_w_out", [2, D, D])
    b_w_in = din("b_w_in", [2, D, 4096])
    b_vnorm_w = din("b_vnorm_w", [2, 2048])
    b_vnorm_b = din("b_vnorm_b", [2, 2048])
    b_w_sT = din("b_w_sT", [2, 8, 128, 128])
    b_w_s00 = din("b_w_s00", [2, 8])
    b_b_s = din("b_b_s", [2, 8, 128])
    b_b_s0 = din("b_b_s0", [2, 8])
    b_w_out = din("b_w_out", [2, 2048, D])
    fnw_p = din("fnw_p", [128, 8])

    yT_out = dout("yT", [D, T])
    convP_out = dout("convP", [2, 3072, 3])
    SP_out = dout("S_p", [2, 8, 128, 128])
    convS_out = dout("convS", [2, 3072, NS, 3])
    SS_out = dout("S_s", [2, NS, 8, 128, 128])
    vS_out = dout("v_s", [2, NS, 2048])

    with contextlib.ExitStack() as st:
        fw = FW(nc, st)
        R = fw.R
        PE, DVE, ACT, POOL, SP = fw.pe, fw.dve, fw.act, fw.pool, fw.sp

        xT = fw.sbuf("xT", [128, 8, T], F32)
        hT = fw.sbuf("hT", [128, 8, T], BF16)
        mod = fw.sbuf("mod", [128, 72, 17], F32)
        modA = fw.sbuf("modA", [128, 3, 8, 17], F32)
        modG = fw.sbuf("modG", [128, 3, 8, 17], F32)
        stash = fw.sbuf("stash", [128, 3, 8, 17], F32)
        scb = fw.sbuf("scb", [128, 8, 17], BF16)
        ctmp = fw.sbuf("ctmp", [128, 8, 17], F32)
        b_ada_s = fw.sbuf("b_ada_s", [128, DEPTH * 72], F32)
        normw_s = fw.sbuf("normw_s", [128, DEPTH * 3 * 8], F32)
        fnw_s = fw.sbuf("fnw_s", [128, 8], F32)
        ones_b = fw.sbuf("ones_b", [128, 128], BF16)
        ones_f = fw.sbuf("ones_f", [128, 128], F32)
        ident_f = fw.sbuf("ident_f", [128, 128], F32)
        ident_b = fw.sbuf("ident_b", [128, 128], BF16)
        eps_t = fw.sbuf("eps_t", [128, 4], F32)
        ps = [st.enter_context(nc.psum_tensor("ps%d" % i, [128, 512], F32)) for i in range(8)]
        bankctr = [0]

        def bank(group):
            i = group[bankctr[0] % len(group)]
            bankctr[0] += 1
            return i
        held = set()
        bctr = {}

        def bacq(group):
            key = tuple(group)
            c = bctr.get(key, 0)
            for k in range(len(group)):
                i = group[(c + k) % len(group)]
                if i not in held:
                    bctr[key] = c + k + 1
                    held.add(i)
                    return i
            raise RuntimeError("no free psum bank in %s" % (group,))

        def brel(i):
            held.discard(i)

        def acq_wait(group):
            while True:
                for i in group:
                    if i not in held:
                        held.add(i)
                        return i
                yield

        cR = R("const")
        fw.op(POOL, lambda e: e.memset(ones_f[:], 1.0), writes=[cR])
        fw.op(POOL, lambda e: e.memset(ones_b[:], 1.0), writes=[cR])
        fw.op(POOL, lambda e: e.memset(ident_f[:], 1.0), writes=[cR])
        fw.op(POOL, lambda e: e.affine_select(out=ident_f[:], in_=ident_f[:], pattern=[[-1, 128]], compare_op=ALU.is_equal,
                                              fill=0.0, base=0, channel_multiplier=1), reads=[cR], writes=[cR])
        fw.op(POOL, lambda e: e.tensor_copy(ident_b[:], ident_f[:]), reads=[cR], writes=[cR])
        fw.op(POOL, lambda e: e.memset(eps_t[:, 0:1], D * EPS), writes=[cR])
        fw.op(POOL, lambda e: e.memset(eps_t[:, 1:2], EPS), writes=[cR])
        fw.op(POOL, lambda e: e.memset(eps_t[:, 2:3], 128 * EPS), writes=[cR])
        fw.op(POOL, lambda e: e.memset(eps_t[:, 3:4], 1.0), writes=[cR])

        fw.dma(SP, "ld0", lambda e: e.dma_start(out=b_ada_s[:], in_=b_ada_p), writes=[cR])
        fw.dma(SP, "ld1", lambda e: e.dma_start(out=normw_s[:], in_=norm_w_p), writes=[cR])
        fw.dma(SP, "ld2", lambda e: e.dma_start(out=fnw_s[:], in_=fnw_p), writes=[cR])
        fw.dma(SP, "ld3", lambda e: e.dma_start(out=ctmp[:], in_=cT_in.rearrange("(c p) n -> p c n", p=128)), writes=[R("ctmp")])
        for c in range(8):
            fw.dma(SP, "ldx%d" % c, lambda e, c=c: e.dma_start(out=xT[:, c, :], in_=xT_in[c * 128:(c + 1) * 128, :]),
                   writes=[R("x", c, t) for t in range(5)])
        fw.op(ACT, lambda e: e.activation(out=scb[:], in_=ctmp[:], func=AF.Silu), reads=[R("ctmp")], writes=[R("scb")])

        def tcols(t):
            s0, n = TILES[t]
            return slice(s0, s0 + n), n

        def ada_derive(l, s):
            sc = mod[:, (s * 3 + 1) * 8:(s * 3 + 2) * 8, :]
            gt = mod[:, (s * 3 + 2) * 8:(s * 3 + 3) * 8, :]
            fw.op(DVE, lambda e: e.tensor_scalar(out=modA[:, s], in0=sc, scalar1=1.0, scalar2=32.0, op0=ALU.add, op1=ALU.mult),
                  reads=[R("mod")], writes=[R("modA")])
            fw.op(DVE, lambda e: e.tensor_tensor(
                out=modA[:, s], in0=modA[:, s],
                in1=normw_s[:, (l * 3 + s) * 8:(l * 3 + s + 1) * 8].unsqueeze(2).broadcast_to([128, 8, 17]), op=ALU.mult),
                reads=[R("modA"), cR], writes=[R("modA")])
            res = 1.0 if s == 1 else 0.5
            fw.op(DVE, lambda e: e.tensor_scalar(out=modG[:, s], in0=gt, scalar1=1.0, scalar2=res, op0=ALU.add, op1=ALU.mult),
                  reads=[R("mod")], writes=[R("modG")])

        def ada_gen(l, ph, b0=0, b1=18, derive=(0, 1, 2)):
            wa = [fw.sbuf("wa%d" % i, [128, 8, 512], BF16, ph) for i in range(2)]

            def load(blk):
                sl = blk % 2
                fw.dma(POOL, "wa%d" % sl, lambda e: e.dma_start(
                    out=wa[sl][:], in_=w_ada[l, :, blk * 512:(blk + 1) * 512].rearrange("(k p) n -> p k n", p=128)), writes=[R("wa", sl)])
            load(b0)
            for blk in range(b0, b1):
                sl = blk % 2
                wR = R("wa", sl)
                if blk + 1 < b1:
                    load(blk + 1)
                b = bank([6, 7])
                pR = R("ps", b)

                def mm(e):
                    ins = None
                    for j in range(4):
                        for k in range(8):
                            ins = e.matmul(ps[b][:, j * 17:(j + 1) * 17], wa[sl][:, k, j * 128:(j + 1) * 128], scb[:, k, :],
                                           start=(k == 0), stop=(k == 7))
                    return ins
                fw.op(PE, mm, reads=[wR, R("scb")], writes=[pR])
                mc0 = blk * 4
                fw.op(DVE, lambda e: e.tensor_tensor(
                    out=mod[:, mc0:mc0 + 4, :], in0=ps[b][:, 0:68].rearrange("p (a n) -> p a n", a=4),
                    in1=b_ada_s[:, l * 72 + mc0:l * 72 + mc0 + 4].unsqueeze(2).broadcast_to([128, 4, 17]), op=ALU.add),
                    reads=[pR, cR], writes=[R("mod")])
                yield
            for s in derive:
                sc = mod[:, (s * 3 + 1) * 8:(s * 3 + 2) * 8, :]
                gt = mod[:, (s * 3 + 2) * 8:(s * 3 + 3) * 8, :]
                fw.op(DVE, lambda e: e.tensor_scalar(out=modA[:, s], in0=sc, scalar1=1.0, scalar2=32.0, op0=ALU.add, op1=ALU.mult),
                      reads=[R("mod")], writes=[R("modA")])
                fw.op(DVE, lambda e: e.tensor_tensor(
                    out=modA[:, s], in0=modA[:, s],
                    in1=normw_s[:, (l * 3 + s) * 8:(l * 3 + s + 1) * 8].unsqueeze(2).broadcast_to([128, 8, 17]), op=ALU.mult),
                    reads=[R("modA"), cR], writes=[R("modA")])
                res = 1.0 if s == 1 else 0.5
                fw.op(DVE, lambda e: e.tensor_scalar(out=modG[:, s], in0=gt, scalar1=1.0, scalar2=res, op0=ALU.add, op1=ALU.mult),
                      reads=[R("mod")], writes=[R("modG")])

        def ada(l):
            with contextlib.ExitStack() as ph:
                for _ in ada_gen(l, ph):
                    pass
                fw.barrier()

        def norm_to_h(ph_outer, Aview, Bview, dst_fn=None, keep=False):
            if keep:
                _norm_to_h(ph_outer, Aview, Bview, dst_fn)
                return
            with contextlib.ExitStack() as ph:
                _norm_to_h(ph, Aview, Bview, dst_fn)
                fw.barrier()

        def _norm_to_h(ph, Aview, Bview, dst_fn=None):
            sqb = [fw.sbuf("sqb%d" % i, [128, 512], BF16, ph) for i in range(4)]
            rinv = fw.sbuf("rinv", [128, T], F32, ph)
            ntmp = [fw.sbuf("ntmp%d" % i, [128, 512], F32, ph) for i in range(4)]
            stmp = fw.sbuf("stmp", [128, 8, 16], F32, ph)
            ctr = 0
            for t in range(5):
                cs, n = tcols(t)
                b = bank([6, 7])
                pR = R("ps", b)
                for c in range(8):
                    q = ctr % 4
                    ctr += 1
                    if c % 2 == 0:
                        fw.op(ACT, lambda e: e.activation(out=sqb[q][:, :n], in_=xT[:, c, cs], func=AF.Square),
                              reads=[R("x", c, t)], writes=[R("sqb", q)])
                    else:
                        fw.op(DVE, lambda e: e.tensor_tensor(out=sqb[q][:, :n], in0=xT[:, c, cs], in1=xT[:, c, cs], op=ALU.mult),
                              reads=[R("x", c, t)], writes=[R("sqb", q)])
                    fw.op(PE, lambda e: e.matmul(ps[b][:, :n], ones_b[:], sqb[q][:, :n], start=(c == 0), stop=(c == 7)),
                          reads=[R("sqb", q), cR], writes=[pR])
                fw.op(ACT, lambda e: e.activation(out=rinv[:, cs], in_=ps[b][:, :n], func=AF.Ln, bias=eps_t[:, 0:1], scale=1.0),
                      reads=[pR, cR], writes=[R("rinv")])
            fw.op(ACT, lambda e: e.activation(out=rinv[:], in_=rinv[:], func=AF.Exp, scale=-0.5), reads=[R("rinv")], writes=[R("rinv")])
            for t in range(5):
                cs, n = tcols(t)
                if t < 4:
                    for c in range(8):
                        q = ctr % 4
                        ctr += 1
                        fw.op(DVE, lambda e: e.scalar_tensor_tensor(
                            out=ntmp[q][:], in0=xT[:, c, cs], scalar=Aview[:, c, 0:1], in1=rinv[:, cs], op0=ALU.mult, op1=ALU.mult),
                            reads=[R("x", c, t), R("rinv"), R("modA"), cR], writes=[R("ntmp", q)])
                        if Bview is not None:
                            if c % 4 != 3:
                                fw.op(ACT, lambda e: e.activation(out=hT[:, c, cs], in_=ntmp[q][:], func=AF.Identity, bias=Bview[:, c, 0:1], scale=1.0),
                                      reads=[R("ntmp", q), R("mod"), R("stash")], writes=[R("h", c, t)])
                            else:
                                fw.op(DVE, lambda e: e.tensor_scalar(out=hT[:, c, cs], in0=ntmp[q][:], scalar1=Bview[:, c, 0:1], scalar2=None, op0=ALU.add),
                                      reads=[R("ntmp", q), R("mod"), R("stash")], writes=[R("h", c, t)])
                        else:
                            dst_fn(c, t, ntmp[q], R("ntmp", q))
                else:
                    fw.op(DVE, lambda e: e.tensor_tensor(out=stmp[:], in0=xT[:, :, TP:T], in1=rinv[:, TP:T].unsqueeze(1).broadcast_to([128, 8, 16]), op=ALU.mult),
                          reads=[R("x", c, 4) for c in range(8)] + [R("rinv")], writes=[R("stmp")])
                    if Bview is not None:
                        fw.op(DVE, lambda e: e.tensor_tensor(out=stmp[:], in0=stmp[:], in1=Aview[:, :, 1:17], op=ALU.mult),
                              reads=[R("stmp"), R("modA")], writes=[R("stmp")])
                        fw.op(DVE, lambda e: e.tensor_tensor(out=hT[:, :, TP:T], in0=stmp[:], in1=Bview[:, :, 1:17], op=ALU.add),
                              reads=[R("stmp"), R("mod")], writes=[R("h", c, 4) for c in range(8)])
                    else:
                        fw.op(DVE, lambda e: e.tensor_tensor(out=stmp[:], in0=stmp[:], in1=Aview[:, :, 0:1].broadcast_to([128, 8, 16]), op=ALU.mult),
                              reads=[R("stmp"), cR], writes=[R("stmp")])
                        dst_fn(None, 4, stmp, R("stmp"))

        def resid_update(ph_tmp, s, m, t, b, Gv=None, tkey="rtmp"):
            cs, n = tcols(t)
            pR = R("ps", b)
            if Gv is None:
                Gv = modG[:, s]
            if t < 4:
                fw.op(DVE, lambda e: e.scalar_tensor_tensor(out=xT[:, m, cs], in0=ps[b][:, :n], scalar=Gv[:, m, 0:1], in1=xT[:, m, cs],
                                                            op0=ALU.mult, op1=ALU.add),
                      reads=[pR, R("modG"), R("stash"), R("x", m, t)], writes=[R("x", m, t)])
            else:
                fw.op(DVE, lambda e: e.tensor_tensor(out=ph_tmp[:], in0=ps[b][:, :16], in1=Gv[:, m, 1:17], op=ALU.mult),
                      reads=[pR, R("modG"), R("stash")], writes=[R(tkey)])
                fw.op(DVE, lambda e: e.tensor_tensor(out=xT[:, m, cs], in0=xT[:, m, cs], in1=ph_tmp[:], op=ALU.add),
                      reads=[R(tkey), R("x", m, t)], writes=[R("x", m, t)])

        GROUPS = [(0, 6), (6, 6), (12, 6), (18, 4)]

        def ffn(l, f, s, next_ada=None, ada_rest=None):
            with contextlib.ExitStack() as ph:
                Av, Bv, Gv = modA[:, s], mod[:, (s * 3) * 8:(s * 3 + 1) * 8, :], modG[:, s]
                agen = None
                if next_ada is not None:
                    fw.op(DVE, lambda e: e.tensor_copy(stash[:, 0], Av), reads=[R("modA")], writes=[R("stash")])
                    fw.op(DVE, lambda e: e.tensor_copy(stash[:, 1], Bv), reads=[R("mod")], writes=[R("stash")])
                    fw.op(DVE, lambda e: e.tensor_copy(stash[:, 2], Gv), reads=[R("modG")], writes=[R("stash")])
                    Av, Bv, Gv = stash[:, 0], stash[:, 1], stash[:, 2]
                act = fw.sbuf("act", [128, 6, T], BF16, ph)
                wgu = [fw.sbuf("wgu%d" % i, [128, 2, 8, 256], BF16, ph) for i in range(3)]
                wd = [fw.sbuf("wd%d" % i, [128, 6, D], BF16, ph) for i in range(2)]
                sg = [fw.sbuf("sg%d" % i, [128, 512], F32, ph) for i in range(2)]
                rtmp = fw.sbuf("rtmp", [128, 16], F32, ph)
                PAIRS = [j0 + pj * 2 for (j0, gn) in GROUPS for pj in range(gn // 2)]

                def issue_pair(k):
                    if k >= len(PAIRS):
                        return
                    jj_, sl_ = PAIRS[k], k % 3
                    for gu in range(2):
                        c0 = gu * DFF + jj_ * 128
                        fw.dma(POOL, "wgu%d_%d" % (sl_, gu), lambda e: e.dma_start(
                            out=wgu[sl_][:, gu], in_=ffn_w_gu[l, f, :, c0:c0 + 256].rearrange("(k p) n -> p k n", p=128)),
                            writes=[R("wgu", sl_, gu)])

                def issue_wd(gi_):
                    if gi_ >= len(GROUPS):
                        return
                    j0_, gn_ = GROUPS[gi_]
                    ws_ = gi_ % 2
                    fw.dma(POOL, "wd%d" % ws_, lambda e: e.dma_start(
                        out=wd[ws_][:, 0:gn_, :], in_=ffn_w_down[l, f, j0_ * 128:(j0_ + gn_) * 128, :].rearrange("(j p) n -> p j n", p=128)),
                        writes=[R("wd", ws_)])
                issue_pair(0)
                issue_pair(1)
                issue_wd(0)
                norm_to_h(ph, Av, Bv, keep=(next_ada is None and ada_rest is None))
                if next_ada is not None:
                    agen = ada_gen(next_ada, ph)
                if ada_rest is not None:
                    agen = ada_gen(ada_rest, ph, 6, 18, (1, 2))
                pair_ctr = 0
                sgc = 0
                for gi, (j0, gn) in enumerate(GROUPS):
                    ws = gi % 2
                    issue_wd(gi + 1)
                    for pj in range(gn // 2):
                        jj = j0 + pj * 2
                        sl = pair_ctr % 3
                        issue_pair(pair_ctr + 2)
                        pair_ctr += 1
                        for jl in range(2):
                            ja = pj * 2 + jl
                            for t in range(5):
                                cs, n = tcols(t)
                                bg = bank([0, 1])
                                bu = bank([2, 3])

                                def mmg(e, gu, bnk, sl=sl, jl=jl, cs=cs, n=n):
                                    ins = None
                                    for k in range(8):
                                        ins = e.matmul(ps[bnk][:, :n], wgu[sl][:, gu, k, jl * 128:(jl + 1) * 128], hT[:, k, cs], start=(k == 0), stop=(k == 7))
                                    return ins
                                hr = [R("h", k, t) for k in range(8)]
                                fw.op(PE, lambda e, bg=bg: mmg(e, 0, bg), reads=hr + [R("wgu", sl, 0)], writes=[R("ps", bg)])
                                fw.op(PE, lambda e, bu=bu: mmg(e, 1, bu), reads=hr + [R("wgu", sl, 1)], writes=[R("ps", bu)])
                                q = sgc % 2
                                sgc += 1
                                fw.op(ACT, lambda e, q=q, bg=bg, n=n: e.activation(out=sg[q][:, :n], in_=ps[bg][:, :n], func=AF.Silu),
                                      reads=[R("ps", bg)], writes=[R("sg", q)])
                                fw.op(DVE, lambda e, q=q, bu=bu, n=n, ja=ja, cs=cs: e.tensor_tensor(out=act[:, ja, cs], in0=sg[q][:, :n], in1=ps[bu][:, :n], op=ALU.mult),
                                      reads=[R("sg", q), R("ps", bu)], writes=[R("act", ja, t)])
                    for m in range(8):
                        for t in range(5):
                            cs, n = tcols(t)
                            b = bank([4, 5])

                            def mmd(e, b=b, m=m, cs=cs, n=n, ws=ws, gn=gn):
                                ins = None
                                for ja in range(gn):
                                    ins = e.matmul(ps[b][:, :n], wd[ws][:, ja, m * 128:(m + 1) * 128], act[:, ja, cs], start=(ja == 0), stop=(ja == gn - 1))
                                return ins
                            fw.op(PE, mmd, reads=[R("act", ja, t) for ja in range(gn)] + [R("wd", ws)], writes=[R("ps", b)])
                            resid_update(rtmp, s, m, t, b, Gv)
                    if agen is not None:
                        for _ in range(5):
                            next(agen, None)
                if agen is not None:
                    for _ in agen:
                        pass
                fw.barrier()

        def mixer_b(l, li):
            s = 1
            with contextlib.ExitStack() as ph:
                norm_to_h(ph, modA[:, s], mod[:, (s * 3) * 8:(s * 3 + 1) * 8, :])
                wv1 = [fw.sbuf("wv1_%d" % i, [128, 8, 512], BF16, ph) for i in range(2)]
                gv = [fw.sbuf("gv%d" % i, [128, 512], F32, ph) for i in range(2)]
                ssum = fw.sbuf("ssum", [128, 17, 4], F32, ph)
                ssq = fw.sbuf("ssq", [128, 17, 4], F32, ph)
                mean = fw.sbuf("mean", [128, 17], F32, ph)
                rstd = fw.sbuf("rstd", [128, 17], F32, ph)
                nmr = fw.sbuf("nmr", [128, 17], F32, ph)
                vtmp = fw.sbuf("vtmp", [128, 17], F32, ph)
                fw.op(DVE, lambda e: e.memset(ssum[:], 0.0), writes=[R("ssum")])
                fw.op(DVE, lambda e: e.memset(ssq[:], 0.0), writes=[R("ssq")])
                gc = 0
                for vb in range(4):
                    sl = vb % 2
                    fw.dma(POOL, "wv1_%d" % sl, lambda e, sl=sl, vb=vb: e.dma_start(
                        out=wv1[sl][:], in_=b_w_in[li, :, 2048 + vb * 512:2048 + (vb + 1) * 512].rearrange("(k p) n -> p k n", p=128)),
                        writes=[R("wv1", sl)])
                    for ti in range(17):
                        c0 = ti * 128
                        np_ = 128 if ti < 16 else 16
                        t5 = min(ti // 4, 4)
                        b = bank([0, 1, 2, 3])

                        def mm(e, b=b, c0=c0, np_=np_, sl=sl):
                            ins = None
                            for k in range(8):
                                ins = e.matmul(ps[b][0:np_, :], hT[:, k, c0:c0 + np_], wv1[sl][:, k, :], start=(k == 0), stop=(k == 7))
                            return ins
                        fw.op(PE, mm, reads=[R("h", k, t5) for k in range(8)] + [R("wv1", sl)], writes=[R("ps", b)])
                        q = gc % 2
                        gc += 1
                        fw.op(ACT, lambda e, q=q, b=b, np_=np_, ti=ti, vb=vb: e.activation(
                            out=gv[q][0:np_, :], in_=ps[b][0:np_, :], func=AF.Gelu_apprx_tanh, accum_out=ssum[0:np_, ti, vb:vb + 1]),
                            reads=[R("ps", b)], writes=[R("gv", q), R("ssum")])
                        fw.op(ACT, lambda e, q=q, np_=np_, ti=ti, vb=vb: e.activation(
                            out=gv[q][0:np_, :], in_=gv[q][0:np_, :], func=AF.Square, accum_out=ssq[0:np_, ti, vb:vb + 1]),
                            reads=[R("gv", q)], writes=[R("gv", q), R("ssq")])
                stR = R("stats")
                fw.op(DVE, lambda e: e.tensor_reduce(out=mean[:], in_=ssum[:], axis=AX.X, op=ALU.add), reads=[R("ssum")], writes=[stR])
                fw.op(DVE, lambda e: e.tensor_reduce(out=rstd[:], in_=ssq[:], axis=AX.X, op=ALU.add), reads=[R("ssq")], writes=[stR])
                fw.op(DVE, lambda e: e.tensor_scalar(out=mean[:], in0=mean[:], scalar1=1.0 / 2048, scalar2=None, op0=ALU.mult), reads=[stR], writes=[stR])
                fw.op(DVE, lambda e: e.tensor_tensor(out=vtmp[:], in0=mean[:], in1=mean[:], op=ALU.mult), reads=[stR], writes=[stR])
                fw.op(DVE, lambda e: e.scalar_tensor_tensor(out=rstd[:], in0=rstd[:], scalar=1.0 / 2048, in1=vtmp[:], op0=ALU.mult, op1=ALU.subtract),
                      reads=[stR], writes=[stR])
                fw.op(ACT, lambda e: e.activation(out=rstd[:], in_=rstd[:], func=AF.Sqrt, bias=eps_t[:, 1:2], scale=1.0), reads=[stR, cR], writes=[stR])
                fw.op(DVE, lambda e: e.reciprocal(out=rstd[:], in_=rstd[:]), reads=[stR], writes=[stR])
                fw.op(DVE, lambda e: e.scalar_tensor_tensor(out=nmr[:], in0=mean[:], scalar=-1.0, in1=rstd[:], op0=ALU.mult, op1=ALU.mult),
                      reads=[stR], writes=[stR])
                wvg2 = [fw.sbuf("wvg%d" % i, [128, 8, 256], BF16, ph) for i in range(2)]
                wug2 = [fw.sbuf("wug%d" % i, [128, 8, 256], BF16, ph) for i in range(2)]
                wog2 = [fw.sbuf("wog%d" % i, [128, 2, D], BF16, ph) for i in range(2)]

                def load_group_w(g_):
                    if g_ >= 8:
                        return
                    q_ = g_ % 2
                    fw.dma(POOL, "wvg%d" % q_, lambda e: e.dma_start(out=wvg2[q_][:], in_=b_w_in[li, :, 2048 + g_ * 256:2048 + (g_ + 1) * 256].rearrange("(k p) n -> p k n", p=128)),
                           writes=[R("wvg", q_)])
                    fw.dma(POOL, "wug%d" % q_, lambda e: e.dma_start(out=wug2[q_][:], in_=b_w_in[li, :, g_ * 256:(g_ + 1) * 256].rearrange("(k p) n -> p k n", p=128)),
                           writes=[R("wug", q_)])

                def load_group_wo(g_):
                    if g_ >= 8:
                        return
                    q_ = g_ % 2
                    fw.dma(POOL, "wog%d" % q_, lambda e: e.dma_start(out=wog2[q_][:], in_=b_w_out[li, g_ * 256:(g_ + 1) * 256, :].rearrange("(c p) n -> p c n", p=128)),
                           writes=[R("wog", q_)])
                load_group_w(0)
                load_group_wo(0)
                load_group_wo(1)
                wsf = fw.sbuf("wsf", [128, 128], F32, ph)
                wsb = fw.sbuf("wsb", [128, 128], BF16, ph)
                mskU = fw.sbuf("mskU", [128, 128], F32, ph)
                lnw = fw.sbuf("lnw", [128, 256], F32, ph)
                lnb = fw.sbuf("lnb", [128, 256], F32, ph)
                bsbc = fw.sbuf("bsbc", [128, 128], F32, ph)
                lncol = fw.sbuf("lncol", [128, 2, 2], F32, ph)
                bias2 = fw.sbuf("bias2", [128, 2, 128], F32, ph)
                bs0 = fw.sbuf("bs0", [128, 8], F32, ph)
                ws00 = fw.sbuf("ws00", [128, 8], F32, ph)
                sb2 = fw.sbuf("sb2", [128, 2], F32, ph)
                dg = fw.sbuf("dg", [16, 16], BF16, ph)
                vnb = fw.sbuf("vnb", [128, 17, 256], BF16, ph)
                vsf = fw.sbuf("vsf", [16, 256], F32, ph)
                uT = fw.sbuf("uT", [128, 2, T], F32, ph)
                umb2 = [fw.sbuf("umb%d" % i, [128, 2, T], BF16, ph) for i in range(2)]
                mtmp = [fw.sbuf("mtmp%d" % i, [128, 512], F32, ph) for i in range(2)]
                rtmp = fw.sbuf("rtmpb", [128, 16], F32, ph)
                fw.op(POOL, lambda e: e.memset(mskU[:], 1.0), writes=[R("mskU")])
                fw.op(POOL, lambda e: e.affine_select(out=mskU[:], in_=mskU[:], pattern=[[1, 128]], compare_op=ALU.is_ge, fill=0.0, base=0,
                                                      channel_multiplier=-1), reads=[R("mskU")], writes=[R("mskU")])
                fw.dma(SP, "b0", lambda e: e.dma_start(out=bs0[:], in_=b_b_s0[li].partition_broadcast(128)), writes=[R("bs0")])
                fw.dma(SP, "b1", lambda e: e.dma_start(out=ws00[:], in_=b_w_s00[li].partition_broadcast(128)), writes=[R("ws00")])
                mc = 0
                for g in range(8):
                    load_group_w(g + 1)
                    wvg, wug, wog = wvg2[g % 2], wug2[g % 2], wog2[g % 2]
                    gq = g % 2
                    umb = umb2[gq]
                    fw.dma(SP, "b2", lambda e, g=g: e.dma_start(out=wsf[:], in_=b_w_sT[li, g]), writes=[R("wsf")])
                    fw.dma(SP, "b3", lambda e, g=g: e.dma_start(out=lnw[:], in_=b_vnorm_w[li, g * 256:(g + 1) * 256].partition_broadcast(128)), writes=[R("lnw")])
                    fw.dma(SP, "b4", lambda e, g=g: e.dma_start(out=lnb[:], in_=b_vnorm_b[li, g * 256:(g + 1) * 256].partition_broadcast(128)), writes=[R("lnb")])
                    fw.dma(SP, "b5", lambda e, g=g: e.dma_start(out=bsbc[:], in_=b_b_s[li, g].partition_broadcast(128)), writes=[R("bsbc")])
                    fw.op(DVE, lambda e: e.tensor_tensor(out=wsb[:], in0=wsf[:], in1=mskU[:], op=ALU.mult), reads=[R("wsf"), R("mskU")], writes=[R("wsb")])
                    for cc_ in range(2):
                        a0 = g * 256 + cc_ * 128
                        fw.dma(SP, "b6_%d" % cc_, lambda e: e.dma_start(out=lncol[:, cc_, 0:1], in_=b_vnorm_w[li, a0:a0 + 128].rearrange("(p o) -> p o", o=1)), writes=[R("lncol", cc_)])
                        fw.dma(SP, "b7_%d" % cc_, lambda e: e.dma_start(out=lncol[:, cc_, 1:2], in_=b_vnorm_b[li, a0:a0 + 128].rearrange("(p o) -> p o", o=1)), writes=[R("lncol", cc_)])
                    for cc_ in range(2):
                        fw.op(DVE, lambda e: e.scalar_tensor_tensor(out=sb2[:, cc_:cc_ + 1], in0=ws00[:, g:g + 1], scalar=lncol[:, cc_, 1:2], in1=bs0[:, g:g + 1], op0=ALU.mult, op1=ALU.add),
                              reads=[R("ws00"), R("lncol", cc_), R("bs0")], writes=[R("sb2")])
                    brs = bank([6, 7])
                    fw.op(PE, lambda e: e.matmul(ps[brs][:, 0:128], ones_b[:], wsb[:], start=True, stop=True), reads=[R("wsb"), cR], writes=[R("ps", brs)])
                    for cc_ in range(2):
                        fw.op(DVE, lambda e: e.scalar_tensor_tensor(out=bias2[:, cc_, :], in0=ps[brs][:, 0:128], scalar=lncol[:, cc_, 1:2], in1=bsbc[:], op0=ALU.mult, op1=ALU.add),
                              reads=[R("ps", brs), R("lncol", cc_), R("bsbc")], writes=[R("bias2", cc_)])
                    fw.op(DVE, lambda e, g=g: e.tensor_scalar(out=dg[:], in0=ident_f[0:16, 0:16], scalar1=ws00[0:16, g:g + 1], scalar2=None, op0=ALU.mult),
                          reads=[R("ws00"), cR], writes=[R("dg")])
                    for ti in range(17):
                        c0 = ti * 128
                        np_ = 128 if ti < 16 else 16
                        t5 = min(ti // 4, 4)
                        b = bank([0, 1])

                        def mm(e, b=b, c0=c0, np_=np_):
                            ins = None
                            for k in range(8):
                                ins = e.matmul(ps[b][0:np_, 0:256], hT[:, k, c0:c0 + np_], wvg[:, k, :], start=(k == 0), stop=(k == 7))
                            return ins
                        fw.op(PE, mm, reads=[R("h", k, t5) for k in range(8)] + [R("wvg", gq)], writes=[R("ps", b)])
                        q = gc % 2
                        gc += 1
                        fw.op(ACT, lambda e, q=q, b=b, np_=np_: e.activation(out=gv[q][0:np_, 0:256], in_=ps[b][0:np_, 0:256], func=AF.Gelu_apprx_tanh),
                              reads=[R("ps", b)], writes=[R("gv", q)])
                        if ti < 16:
                            fw.op(ACT, lambda e, q=q, ti=ti: e.activation(out=vnb[:, ti, :], in_=gv[q][:, 0:256], func=AF.Identity,
                                                                          scale=rstd[:, ti:ti + 1], bias=nmr[:, ti:ti + 1]),
                                  reads=[R("gv", q), stR], writes=[R("vnb", ti)])
                        else:
                            fw.op(DVE, lambda e, q=q, np_=np_, ti=ti: e.tensor_scalar(
                                out=gv[q][0:np_, 0:256], in0=gv[q][0:np_, 0:256], scalar1=rstd[0:np_, ti:ti + 1], scalar2=nmr[0:np_, ti:ti + 1],
                                op0=ALU.mult, op1=ALU.add), reads=[R("gv", q), stR], writes=[R("gv", q)])
                            fw.op(ACT, lambda e, q=q: e.copy(out=vnb[0:16, 16, :], in_=gv[q][0:16, 0:256]), reads=[R("gv", q)], writes=[R("vnb", 16)])
                            fw.op(DVE, lambda e, q=q, np_=np_: e.tensor_tensor(out=gv[q][0:np_, 0:256], in0=gv[q][0:np_, 0:256], in1=lnw[0:np_, :], op=ALU.mult),
                                  reads=[R("gv", q), R("lnw")], writes=[R("gv", q)])
                            fw.op(DVE, lambda e, q=q: e.tensor_tensor(out=vsf[:], in0=gv[q][0:16, 0:256], in1=lnb[0:16, :], op=ALU.add),
                                  reads=[R("gv", q), R("lnb")], writes=[R("vsf")])
                            fw.dma(SP, "vs", lambda e, g=g: e.dma_start(out=vS_out[li, :, g * 256:(g + 1) * 256], in_=vsf[:]), reads=[R("vsf")])
                    for cc in range(2):
                        for t in range(5):
                            cs, n = tcols(t)
                            b = bank([2, 3])

                            def mm(e, b=b, cc=cc, cs=cs, n=n):
                                ins = None
                                for k in range(8):
                                    ins = e.matmul(ps[b][:, :n], wug[:, k, cc * 128:(cc + 1) * 128], hT[:, k, cs], start=(k == 0), stop=(k == 7))
                                return ins
                            fw.op(PE, mm, reads=[R("h", k, t) for k in range(8)] + [R("wug", gq)], writes=[R("ps", b)])
                            fw.op(ACT, lambda e, b=b, cc=cc, cs=cs, n=n: e.activation(out=uT[:, cc, cs], in_=ps[b][:, :n], func=AF.Gelu_apprx_tanh),
                                  reads=[R("ps", b)], writes=[R("uT", cc, t)])
                    for cc in range(2):
                        for t in range(5):
                            cs, n = tcols(t)
                            b = bank([6, 7])
                            q = mc % 2
                            mc += 1
                            if t < 4:
                                def mm(e, b=b, cc=cc, t=t):
                                    ins = None
                                    for j in range(4):
                                        ins = e.matmul(ps[b][:, j * 128:(j + 1) * 128], vnb[:, t * 4 + j, cc * 128:(cc + 1) * 128], wsb[:], start=True, stop=True)
                                    return ins
                                fw.op(PE, mm, reads=[R("vnb", t * 4 + j) for j in range(4)] + [R("wsb")], writes=[R("ps", b)])
                                fw.op(DVE, lambda e, b=b, q=q, cc=cc: e.scalar_tensor_tensor(
                                    out=mtmp[q][:].rearrange("p (a n) -> p a n", a=4), in0=ps[b][:].rearrange("p (a n) -> p a n", a=4),
                                    scalar=lncol[:, cc, 0:1], in1=bias2[:, cc, :].unsqueeze(1).broadcast_to([128, 4, 128]), op0=ALU.mult, op1=ALU.add),
                                    reads=[R("ps", b), R("bias2", cc), R("lncol", cc)], writes=[R("mtmp", q)])
                            else:
                                fw.op(PE, lambda e, b=b, cc=cc: e.matmul(ps[b][:, 0:16], vnb[0:16, 16, cc * 128:(cc + 1) * 128], dg[:], start=True, stop=True),
                                      reads=[R("vnb", 16), R("dg")], writes=[R("ps", b)])
                                fw.op(DVE, lambda e, b=b, q=q, g=g, cc=cc: e.tensor_scalar(out=mtmp[q][:, 0:16], in0=ps[b][:, 0:16], scalar1=lncol[:, cc, 0:1], scalar2=sb2[:, cc:cc + 1],
                                                                                     op0=ALU.mult, op1=ALU.add),
                                      reads=[R("ps", b), R("sb2"), R("lncol", cc)], writes=[R("mtmp", q)])
                            fw.op(DVE, lambda e, q=q, cc=cc, cs=cs, n=n: e.tensor_tensor(out=umb[:, cc, cs], in0=mtmp[q][:, :n], in1=uT[:, cc, cs], op=ALU.mult),
                                  reads=[R("mtmp", q), R("uT", cc, t)], writes=[R("umb", gq, cc, t)])
                    if g % 2 == 1:
                        for m in range(8):
                            for t in range(5):
                                cs, n = tcols(t)
                                b = bank([4, 5])

                                def mm(e, b=b, m=m, cs=cs, n=n):
                                    ins = None
                                    for i4 in range(4):
                                        gg, cc = i4 // 2, i4 % 2
                                        ins = e.matmul(ps[b][:, :n], wog2[gg][:, cc, m * 128:(m + 1) * 128], umb2[gg][:, cc, cs], start=(i4 == 0), stop=(i4 == 3))
                                    return ins
                                fw.op(PE, mm, reads=[R("umb", gg, cc, t) for gg in range(2) for cc in range(2)] + [R("wog", 0), R("wog", 1)], writes=[R("ps", b)])
                                resid_update(rtmp, s, m, t, b)
                        load_group_wo(g + 1)
                        load_group_wo(g + 2)
                fw.barrier()

        def mixer_a(l, li):
            s = 1
            with contextlib.ExitStack() as ph:
                norm_to_h(ph, modA[:, s], mod[:, (s * 3) * 8:(s * 3 + 1) * 8, :])
                triU = fw.sbuf("triU", [128, 128], F32, ph)
                mposL = fw.sbuf("mposL", [128, 128], F32, ph)
                mnegU = fw.sbuf("mnegU", [128, 128], F32, ph)
                b32 = fw.sbuf("b32", [128, 128], BF16, ph)
                b1T = fw.sbuf("b1T", [128, 128], BF16, ph)
                b2 = fw.sbuf("b2", [128, 128], BF16, ph)
                cw = fw.sbuf("cw", [128, 2 * 4 * 24], F32, ph)
                won = fw.sbuf("won", [128, 2], F32, ph)
                alog = fw.sbuf("alog", [128, 8], F32, ph)
                dtb = fw.sbuf("dtb", [128, 8], F32, ph)
                mR = R("maskA")
                fw.dma(SP, "a0", lambda e: e.dma_start(out=cw[:], in_=a_w_conv_p), writes=[mR])
                fw.dma(SP, "a1", lambda e: e.dma_start(out=won[:], in_=a_w_onorm_p), writes=[mR])
                fw.dma(SP, "a2", lambda e: e.dma_start(out=alog[:], in_=a_log[li].partition_broadcast(128)), writes=[mR])
                fw.dma(SP, "a3", lambda e: e.dma_start(out=dtb[:], in_=a_dt_bias[li].partition_broadcast(128)), writes=[mR])
                fw.op(POOL, lambda e: e.memset(triU[:], 1.0), writes=[mR])
                fw.op(POOL, lambda e: e.affine_select(out=triU[:], in_=triU[:], pattern=[[1, 128]], compare_op=ALU.is_ge, fill=0.0, base=0,
                                                      channel_multiplier=-1), reads=[mR], writes=[mR])
                fw.op(POOL, lambda e: e.memset(mposL[:], 0.0), writes=[mR])
                fw.op(POOL, lambda e: e.affine_select(out=mposL[:], in_=mposL[:], pattern=[[-1, 128]], compare_op=ALU.is_gt, fill=BIG, base=0,
                                                      channel_multiplier=1), reads=[mR], writes=[mR])
                fw.op(POOL, lambda e: e.memset(mnegU[:], 0.0), writes=[mR])
                fw.op(POOL, lambda e: e.affine_select(out=mnegU[:], in_=mnegU[:], pattern=[[1, 128]], compare_op=ALU.is_ge, fill=-BIG, base=0,
                                                      channel_multiplier=-1), reads=[mR], writes=[mR])
                fw.op(POOL, lambda e: e.memset(b32[:], 0.0), writes=[mR])
                fw.op(POOL, lambda e: e.memset(b1T[:], 0.0), writes=[mR])
                fw.op(POOL, lambda e: e.memset(b2[:], 0.0), writes=[mR])
                fw.op(POOL, lambda e: e.memset(b32[0:32, 0:32], 1.0), reads=[mR], writes=[mR])
                fw.op(POOL, lambda e: e.memset(b32[32:64, 32:64], 1.0), reads=[mR], writes=[mR])
                fw.op(POOL, lambda e: e.memset(b32[64:128, 96:128], 1.0), reads=[mR], writes=[mR])
                fw.op(POOL, lambda e: e.memset(b32[64:96, 96:128], 0.0), reads=[mR], writes=[mR])
                fw.op(POOL, lambda e: e.memset(b32[64:96, 64:96], 1.0), reads=[mR], writes=[mR])
                for i in range(2):
                    fw.op(POOL, lambda e, i=i: e.memset(b1T[i * 64:i * 64 + 32, i * 64 + 32:i * 64 + 64], 1.0), reads=[mR], writes=[mR])
                fw.op(POOL, lambda e: e.memset(b2[64:128, 0:64], 1.0), reads=[mR], writes=[mR])

                wl = fw.sbuf("wl", [128, 8, 16], BF16, ph)
                lg = fw.sbuf("lg", [128, 17, 16], F32, ph)
                beta_t = fw.sbuf("beta_t", [128, 17, 8], F32, ph)
                g_t = fw.sbuf("g_t", [128, 17, 8], F32, ph)
                d_t = fw.sbuf("d_t", [128, 17, 8], F32, ph)
                dpl_t = fw.sbuf("dpl_t", [128, 17, 8], F32, ph)
                ebd_t = fw.sbuf("ebd_t", [128, 17, 8], F32, ph)
                ekd_t = fw.sbuf("ekd_t", [128, 17, 8], F32, ph)
                edl_t = fw.sbuf("edl_t", [128, 16, 8], F32, ph)
                nega = fw.sbuf("nega", [128, 8], F32, ph)
                ltmp = fw.sbuf("ltmp", [128, 17, 8], F32, ph)
                fw.op(DVE, lambda e: e.memset(lg[:], 0.0), writes=[R("lg")])
                fw.dma(POOL, "wl", lambda e: e.dma_start(out=wl[:], in_=a_w_in[li, :, 4096:4112].rearrange("(k p) n -> p k n", p=128)), writes=[R("wl")])
                for ti in range(17):
                    c0 = ti * 128
                    np_ = 128 if ti < 16 else 16
                    t5 = min(ti // 4, 4)
                    b = bank([0, 1])

                    def mm(e, b=b, c0=c0, np_=np_):
                        ins = None
                        for k in range(8):
                            ins = e.matmul(ps[b][0:np_, 0:16], hT[:, k, c0:c0 + np_], wl[:, k, :], start=(k == 0), stop=(k == 7))
                        return ins
                    fw.op(PE, mm, reads=[R("h", k, t5) for k in range(8)] + [R("wl")], writes=[R("ps", b)])
                    fw.op(ACT, lambda e, b=b, np_=np_, ti=ti: e.copy(out=lg[0:np_, ti, :], in_=ps[b][0:np_, 0:16]), reads=[R("ps", b)], writes=[R("lg")])
                gR = R("gates")
                fw.op(ACT, lambda e: e.activation(out=beta_t[:], in_=lg[:, :, 0:8], func=AF.Sigmoid), reads=[R("lg")], writes=[gR])
                fw.op(ACT, lambda e: e.activation(out=nega[:], in_=alog[:], func=AF.Exp), reads=[mR], writes=[gR])
                fw.op(DVE, lambda e: e.tensor_tensor(out=ltmp[:], in0=lg[:, :, 8:16], in1=dtb[:].unsqueeze(1).broadcast_to([128, 17, 8]), op=ALU.add),
                      reads=[R("lg"), mR], writes=[gR])
                fw.op(ACT, lambda e: e.activation(out=ltmp[:], in_=ltmp[:], func=AF.Exp), reads=[gR], writes=[gR])
                fw.op(ACT, lambda e: e.activation(out=ltmp[:], in_=ltmp[:], func=AF.Ln, bias=eps_t[:, 3:4], scale=1.0), reads=[gR, cR], writes=[gR])
                fw.op(DVE, lambda e: e.scalar_tensor_tensor(out=g_t[:], in0=ltmp[:], scalar=-1.0, in1=nega[:].unsqueeze(1).broadcast_to([128, 17, 8]),
                                                            op0=ALU.mult, op1=ALU.mult), reads=[gR], writes=[gR])
                bq = bank([2, 3])
                fw.op(PE, lambda e: e.matmul(ps[bq][:, 0:128], triU[:], g_t[:, 0:16, :].rearrange("p a b -> p (a b)"), start=True, stop=True),
                      reads=[gR, mR], writes=[R("ps", bq)])
                fw.op(DVE, lambda e: e.tensor_copy(d_t[:, 0:16, :].rearrange("p a b -> p (a b)"), ps[bq][:, 0:128]), reads=[R("ps", bq)], writes=[gR])
                fw.op(DVE, lambda e: e.tensor_copy(d_t[:, 16, :], g_t[:, 16, :]), reads=[gR], writes=[gR])
                bq2 = bank([2, 3])
                fw.op(PE, lambda e: e.matmul(ps[bq2][:, 0:128], ones_f[:], g_t[:, 0:16, :].rearrange("p a b -> p (a b)"), start=True, stop=True),
                      reads=[gR, cR], writes=[R("ps", bq2)])
                fw.op(ACT, lambda e: e.activation(out=edl_t[:].rearrange("p a b -> p (a b)"), in_=ps[bq2][:, 0:128], func=AF.Exp), reads=[R("ps", bq2)], writes=[gR])
                fw.op(DVE, lambda e: e.tensor_tensor(out=ekd_t[:, 0:16, :].rearrange("p a b -> p (a b)"), in0=ps[bq2][:, 0:128],
                                                     in1=d_t[:, 0:16, :].rearrange("p a b -> p (a b)"), op=ALU.subtract), reads=[R("ps", bq2), gR], writes=[gR])
                fw.op(ACT, lambda e: e.activation(out=ekd_t[:, 0:16, :], in_=ekd_t[:, 0:16, :], func=AF.Exp), reads=[gR], writes=[gR])
                fw.op(ACT, lambda e: e.activation(out=dpl_t[:], in_=beta_t[:], func=AF.Ln), reads=[gR], writes=[gR])
                fw.op(DVE, lambda e: e.tensor_tensor(out=dpl_t[:], in0=dpl_t[:], in1=d_t[:], op=ALU.add), reads=[gR], writes=[gR])
                fw.op(ACT, lambda e: e.activation(out=ebd_t[:], in_=d_t[:], func=AF.Exp), reads=[gR], writes=[gR])
                fw.op(DVE, lambda e: e.scalar_tensor_tensor(out=ebd_t[:], in0=ebd_t[:], scalar=-1.0, in1=beta_t[:], op0=ALU.mult, op1=ALU.mult), reads=[gR], writes=[gR])
                svals = fw.sbuf("svals", [16, 8, 2], F32, ph)
                sdg = fw.sbuf("sdg", [16, 16, 16], F32, ph)
                sbc = fw.sbuf("sbc", [128, 8, 2, 16], F32, ph)
                fw.op(DVE, lambda e: e.tensor_copy(svals[:, :, 0], beta_t[0:16, 16, :]), reads=[gR], writes=[R("svals")])
                fw.op(ACT, lambda e: e.activation(out=svals[:, :, 1], in_=g_t[0:16, 16, :], func=AF.Exp), reads=[gR], writes=[R("svals")])
                fw.op(DVE, lambda e: e.tensor_tensor(out=sdg[:], in0=ident_f[0:16, 0:16].unsqueeze(1).broadcast_to([16, 16, 16]),
                                                     in1=svals[:].rearrange("p a b -> p (a b)").unsqueeze(2).broadcast_to([16, 16, 16]), op=ALU.mult),
                      reads=[R("svals"), cR], writes=[R("sdg")])
                bq3 = bank([2, 3])
                fw.op(PE, lambda e: e.matmul(ps[bq3][:, 0:256], ones_f[0:16, :], sdg[:].rearrange("p a b -> p (a b)"), start=True, stop=True),
                      reads=[R("sdg"), cR], writes=[R("ps", bq3)])
                fw.op(DVE, lambda e: e.tensor_copy(sbc[:].rearrange("p a b c -> p (a b c)"), ps[bq3][:, 0:256]), reads=[R("ps", bq3)], writes=[R("sbc")])

                wq = fw.sbuf("wq", [128, 3, 8, 128], BF16, ph)
                wo2 = [fw.sbuf("wo%d" % i, [128, D], BF16, ph) for i in range(2)]
                wz2 = [fw.sbuf("wz%d" % i, [128, 8, 128], BF16, ph) for i in range(2)]
                preb = [fw.sbuf("preb%d" % i, [128, 516], F32, ph) for i in range(3)]
                cvb = [fw.sbuf("cvb%d" % i, [128, 512], F32, ph) for i in range(3)]
                carry = [fw.sbuf("carry%d" % i, [128, 3], F32, ph) for i in range(3)]
                pre_s3 = [fw.sbuf("pre_s%d" % i, [128, 16], F32, ph) for i in range(3)]
                cvs3 = [fw.sbuf("cvs%d" % i, [128, 16], F32, ph) for i in range(3)]
                sgs3 = [fw.sbuf("sgs%d" % i, [128, 16], F32, ph) for i in range(3)]
                sqs3 = [fw.sbuf("sqs%d" % i, [128, 16], BF16, ph) for i in range(3)]
                cbuf3 = [fw.sbuf("cbuf%d" % i, [128, 16, 3], F32, ph) for i in range(3)]
                nbuf3 = [fw.sbuf("nbuf%d" % i, [128, 16, 3], F32, ph) for i in range(3)]
                vs_f = fw.sbuf("vs_f", [128, 16], F32, ph)
                qT = fw.sbuf("qT", [128, T], BF16, ph)
                kT = fw.sbuf("kT", [128, T], BF16, ph)
                vTb = fw.sbuf("vTb", [128, TP], BF16, ph)
                rn = fw.sbuf("rn", [128, 512], F32, ph)
                qd = {}
                for nm in ["m", "mT", "X", "XT", "m1T", "m2", "P2", "PT2", "PT8", "Ra", "Rb", "T0"]:
                    qd[nm] = fw.sbuf("q_" + nm, [128, 4, 128], BF16, ph)
                QALIAS = {"P": "m", "PT4": "mT", "T0T": "X", "Y": "P2", "T1": "PT8", "T1T": "PT2", "Yp": "Ra"}
                for a_, b_ in QALIAS.items():
                    qd[a_] = qd[b_]
                Ef = [fw.sbuf("Ef%d" % i, [128, 4, 128], F32, ph) for i in range(3)]
                ring = {}
                for nm in ["T2T", "nwT", "attnT", "qeT", "vbt", "kbe", "kd"]:
                    ring[nm] = [fw.sbuf("r_%s%d" % (nm, i), [128, 4, 128], BF16, ph) for i in range(2)]
                oq = [fw.sbuf("oq%d" % i, [128, 512], F32, ph) for i in range(1)]
                o_s = fw.sbuf("o_s", [128, 16], F32, ph)
                Sf = fw.sbuf("Sf", [128, 128], F32, ph)
                Sb = fw.sbuf("Sb", [128, 128], BF16, ph)
                vnb = [fw.sbuf("vnbA%d" % i, [128, 128], BF16, ph) for i in range(2)]
                Ss = fw.sbuf("Ss", [128, 8, 128], F32, ph)
                kqf = fw.sbuf("kqf", [128, 16, 2], F32, ph)
                stmp = [fw.sbuf("sA%d" % i, [128, 16], F32, ph) for i in range(4)]
                ktok = fw.sbuf("ktok", [16, 128], F32, ph)
                vtok = fw.sbuf("vtok", [16, 128], F32, ph)
                kmask = [fw.sbuf("kmask%d" % i, [16, 128], F32, ph) for i in range(1)]
                szq = [fw.sbuf("szq%d" % i, [128, 512], F32, ph) for i in range(1)]
                ogb = [fw.sbuf("ogb%d" % i, [128, 512], BF16, ph) for i in range(1)]
                rtmp = fw.sbuf("rtmpa", [128, 16], F32, ph)
                rtmp_s = fw.sbuf("rtmps", [128, 16], F32, ph)
                sqb_s = fw.sbuf("sqb_s", [128, 16], BF16, ph)
                rn_s = fw.sbuf("rn_s", [128, 16], F32, ph)
                szq_s = fw.sbuf("szq_s", [128, 16], F32, ph)
                ogb_s = fw.sbuf("ogb_s", [128, 16], BF16, ph)
                BK_CH = [0, 1, 2]
                BK_PJ = [3]
                BK_AUX = [4, 5]
                BK_SEQ = [6, 7]
                evc = [0]

                def evac_copy(dst_ap, src_ap, reads, writes):
                    evc[0] += 1
                    if evc[0] % 4 != 0:
                        fw.op(ACT, lambda e: e.copy(out=dst_ap, in_=src_ap), reads=reads, writes=writes)
                    else:
                        fw.op(DVE, lambda e: e.tensor_copy(dst_ap, src_ap), reads=reads, writes=writes)

                def make_head(hd):
                    def load_qkv(h_):
                        for fi in range(3):
                            c0 = fi * 1024 + h_ * 128
                            fw.dma(POOL, "wq%d" % fi, lambda e: e.dma_start(
                                out=wq[:, fi], in_=a_w_in[li, :, c0:c0 + 128].rearrange("(k p) n -> p k n", p=128)), writes=[R("wq", fi)])

                    def load_zo(h_):
                        c0 = 3 * 1024 + h_ * 128
                        fw.dma(POOL, "wz%d" % (h_ % 2), lambda e: e.dma_start(
                            out=wz2[h_ % 2][:], in_=a_w_in[li, :, c0:c0 + 128].rearrange("(k p) n -> p k n", p=128)), writes=[R("wz", h_ % 2)])
                        fw.dma(POOL, "wo%d" % (h_ % 2), lambda e: e.dma_start(out=wo2[h_ % 2][:], in_=a_w_out[li, h_ * 128:(h_ + 1) * 128, :]), writes=[R("wo", h_ % 2)])
                    wz, wo = wz2[hd % 2], wo2[hd % 2]
                    wzR, woR = R("wz", hd % 2), R("wo", hd % 2)
                    BK_P = [0, 1, 2, 3, 4, 5, 6, 7]

                    def sigm(buf_ap, src_ap, rd, wr):
                        fw.op(ACT, lambda e: e.activation(out=buf_ap, in_=src_ap, func=AF.Exp, scale=-1.0), reads=rd, writes=wr)
                        yield
                        fw.op(ACT, lambda e: e.activation(out=buf_ap, in_=buf_ap, func=AF.Ln, bias=eps_t[:, 3:4], scale=1.0), reads=wr + [cR], writes=wr)
                        yield
                        fw.op(ACT, lambda e: e.activation(out=buf_ap, in_=buf_ap, func=AF.Exp, scale=-1.0), reads=wr, writes=wr)
                        yield

                    def pjob(hj, t, fi, slot, BKG):
                        cs = slice(t * 512, (t + 1) * 512)
                        ch = fi * 8 + hj
                        wbase = (li * 4) * 24 + ch
                        wc = [cw[:, wbase + j * 24:wbase + j * 24 + 1] for j in range(4)]
                        pb, cb_ = preb[slot], cvb[slot]
                        pR_, cR_ = R("preb", slot), R("cvb", slot)
                        b = yield from acq_wait(BKG)

                        for k2 in range(4):
                            def mm(e):
                                e.matmul(ps[b][:, :], wq[:, fi, 2 * k2, :], hT[:, 2 * k2, cs], start=(k2 == 0), stop=False)
                                return e.matmul(ps[b][:, :], wq[:, fi, 2 * k2 + 1, :], hT[:, 2 * k2 + 1, cs], start=False, stop=(k2 == 3))
                            fw.op(PE, mm, reads=[R("h", 2 * k2, t), R("h", 2 * k2 + 1, t), R("wq", fi)], writes=[R("ps", b)])
                            yield
                        fw.op(ACT, lambda e: e.copy(out=pb[:, 3:515], in_=ps[b][:, :]), reads=[R("ps", b)], writes=[pR_])
                        brel(b)
                        if t == 0:
                            fw.op(ACT, lambda e: e.activation(out=pb[:, 0:3], in_=pb[:, 3:6], func=AF.Copy, scale=0.0), reads=[pR_], writes=[pR_])
                        else:
                            fw.op(ACT, lambda e: e.copy(out=pb[:, 0:3], in_=carry[fi][:]), reads=[R("carry", fi)], writes=[pR_])
                        fw.op(ACT, lambda e: e.copy(out=carry[fi][:], in_=pb[:, 512:515]), reads=[pR_], writes=[R("carry", fi)])
                        if t == 3:
                            fw.dma(SP, "cpo%d" % fi, lambda e: e.dma_start(out=convP_out[li, ch * 128:(ch + 1) * 128, :], in_=carry[fi][:]), reads=[R("carry", fi)])
                        yield
                        fw.op(DVE, lambda e: e.tensor_scalar(out=cb_[:], in0=pb[:, 3:515], scalar1=wc[3], scalar2=None, op0=ALU.mult), reads=[pR_, mR], writes=[cR_])
                        for j in range(3):
                            yield
                            fw.op(DVE, lambda e: e.scalar_tensor_tensor(out=cb_[:], in0=pb[:, j:j + 512], scalar=wc[j], in1=cb_[:], op0=ALU.mult, op1=ALU.add),
                                  reads=[pR_, cR_, mR], writes=[cR_])
                        yield
                        yield from sigm(pb[:, 0:512], cb_[:], [cR_, pR_], [pR_])
                        fw.op(DVE, lambda e: e.tensor_tensor(out=cb_[:], in0=cb_[:], in1=pb[:, 0:512], op=ALU.mult), reads=[cR_, pR_], writes=[cR_])
                        yield
                        if fi == 2:
                            fw.op(ACT, lambda e: e.copy(out=vTb[:, cs], in_=cb_[:]), reads=[cR_], writes=[R("vTb", t)])
                            return
                        dst = qT if fi == 0 else kT
                        sqv = pb[:, 0:256].bitcast(BF16)
                        fw.op(ACT, lambda e: e.activation(out=sqv, in_=cb_[:], func=AF.Square), reads=[cR_, pR_], writes=[pR_])
                        b2_ = yield from acq_wait(BKG)
                        fw.op(PE, lambda e: e.matmul(ps[b2_][:, :], ones_b[:], sqv, start=True, stop=True), reads=[pR_, cR], writes=[R("ps", b2_)])
                        yield
                        if fi == 0:
                            fw.op(ACT, lambda e: e.activation(out=pb[:, 0:512], in_=ps[b2_][:, :], func=AF.Ln, bias=eps_t[:, 2:3], scale=128.0),
                                  reads=[R("ps", b2_), cR, pR_], writes=[pR_])
                        else:
                            fw.op(ACT, lambda e: e.activation(out=pb[:, 0:512], in_=ps[b2_][:, :], func=AF.Ln, bias=eps_t[:, 1:2], scale=1.0),
                                  reads=[R("ps", b2_), cR, pR_], writes=[pR_])
                        brel(b2_)
                        fw.op(ACT, lambda e: e.activation(out=pb[:, 0:512], in_=pb[:, 0:512], func=AF.Exp, scale=-0.5), reads=[pR_], writes=[pR_])
                        fw.op(DVE, lambda e: e.tensor_tensor(out=dst[:, cs], in0=cb_[:], in1=pb[:, 0:512], op=ALU.mult), reads=[cR_, pR_], writes=[R("qk", fi, t)])

                    def sjob(hj, fi, BKG):
                        ch = fi * 8 + hj
                        wbase = (li * 4) * 24 + ch
                        wc = [cw[:, wbase + j * 24:wbase + j * 24 + 1] for j in range(4)]
                        ps_, cs_, sg_s = pre_s3[fi], cvs3[fi], sgs3[fi]
                        KR_ = R("sjob", fi)
                        fw.dma(SP, "cb%d" % fi, lambda e: e.dma_start(out=cbuf3[fi][:], in_=convT_in[li, ch * 128:(ch + 1) * 128]), writes=[R("cbuf", fi)])
                        b = yield from acq_wait(BKG)

                        def mm(e):
                            ins = None
                            for k in range(8):
                                ins = e.matmul(ps[b][:, 0:16], wq[:, fi, k, :], hT[:, k, TP:T], start=(k == 0), stop=(k == 7))
                            return ins
                        fw.op(PE, mm, reads=[R("h", k, 4) for k in range(8)] + [R("wq", fi)], writes=[R("ps", b)])
                        yield
                        fw.op(ACT, lambda e: e.copy(out=ps_[:], in_=ps[b][:, 0:16]), reads=[R("ps", b)], writes=[KR_])
                        brel(b)
                        fw.op(DVE, lambda e: e.tensor_copy(nbuf3[fi][:, :, 0:2], cbuf3[fi][:, :, 1:3]), reads=[R("cbuf", fi)], writes=[R("nbuf", fi)])
                        fw.op(DVE, lambda e: e.tensor_copy(nbuf3[fi][:, :, 2], ps_[:]), reads=[KR_], writes=[R("nbuf", fi)])
                        fw.dma(SP, "cso%d" % fi, lambda e: e.dma_start(out=convS_out[li, ch * 128:(ch + 1) * 128], in_=nbuf3[fi][:]), reads=[R("nbuf", fi)])
                        fw.op(DVE, lambda e: e.tensor_scalar(out=cs_[:], in0=ps_[:], scalar1=wc[3], scalar2=None, op0=ALU.mult), reads=[KR_, mR], writes=[KR_])
                        for j in range(3):
                            fw.op(DVE, lambda e: e.scalar_tensor_tensor(out=cs_[:], in0=cbuf3[fi][:, :, j], scalar=wc[j], in1=cs_[:], op0=ALU.mult, op1=ALU.add),
                                  reads=[R("cbuf", fi), KR_, mR], writes=[KR_])
                        yield from sigm(sg_s[:], cs_[:], [KR_], [R("sjs", fi)])
                        if fi == 2:
                            fw.op(DVE, lambda e: e.tensor_tensor(out=vs_f[:], in0=cs_[:], in1=sg_s[:], op=ALU.mult), reads=[KR_, R("sjs", fi)], writes=[R("vs_f")])
                            return
                        fw.op(DVE, lambda e: e.tensor_tensor(out=cs_[:], in0=cs_[:], in1=sg_s[:], op=ALU.mult), reads=[KR_, R("sjs", fi)], writes=[KR_])
                        yield
                        dst = qT if fi == 0 else kT
                        fw.op(ACT, lambda e: e.activation(out=sqs3[fi][:], in_=cs_[:], func=AF.Square), reads=[KR_], writes=[R("sqs", fi)])
                        b2_ = yield from acq_wait(BKG)
                        fw.op(PE, lambda e: e.matmul(ps[b2_][:, 0:16], ones_b[:], sqs3[fi][:], start=True, stop=True), reads=[R("sqs", fi), cR], writes=[R("ps", b2_)])
                        yield
                        if fi == 0:
                            fw.op(ACT, lambda e: e.activation(out=sg_s[:], in_=ps[b2_][:, 0:16], func=AF.Ln, bias=eps_t[:, 2:3], scale=128.0),
                                  reads=[R("ps", b2_), cR], writes=[R("sjs", fi)])
                        else:
                            fw.op(ACT, lambda e: e.activation(out=sg_s[:], in_=ps[b2_][:, 0:16], func=AF.Ln, bias=eps_t[:, 1:2], scale=1.0),
                                  reads=[R("ps", b2_), cR], writes=[R("sjs", fi)])
                        brel(b2_)
                        fw.op(ACT, lambda e: e.activation(out=sg_s[:], in_=sg_s[:], func=AF.Exp, scale=-0.5), reads=[R("sjs", fi)], writes=[R("sjs", fi)])
                        col = 1 if fi == 0 else 0
                        fw.op(DVE, lambda e: e.tensor_tensor(out=kqf[:, :, col], in0=cs_[:], in1=sg_s[:], op=ALU.mult), reads=[KR_, R("sjs", fi)], writes=[R("kqf")])
                        fw.op(DVE, lambda e: e.tensor_tensor(out=dst[:, TP:T], in0=cs_[:], in1=sg_s[:], op=ALU.mult), reads=[KR_, R("sjs", fi)], writes=[R("qk", fi, 4)])

                    def half_chain(qi, hf):
                        rs = qi % 2
                        j0 = hf * 2
                        t0 = qi * 4 + j0
                        Rq = lambda nm: R("qd", QALIAS.get(nm, nm), hf)
                        Rr = lambda nm: R("ring", nm, rs, hf)
                        RE = lambda i: R("Ef", i, hf)
                        qv = lambda nm: qd[nm][:, j0:j0 + 2, :]
                        rv = lambda nm: ring[nm][rs][:, j0:j0 + 2, :]
                        ev = lambda i: Ef[i][:, j0:j0 + 2, :]
                        tcol = lambda j: slice((t0 + j) * 128, (t0 + j + 1) * 128)

                        def h_mm(lhs_fn, rhs_fn, reads):
                            b = bacq(BK_CH)

                            def mm(e):
                                ins = None
                                for j in range(2):
                                    ins = e.matmul(ps[b][:, j * 128:(j + 1) * 128], lhs_fn(j), rhs_fn(j), start=True, stop=True)
                                return ins
                            fw.op(PE, mm, reads=reads, writes=[R("ps", b)])
                            return b

                        def h_tr(src, reads):
                            b = bacq(BK_CH)
                            pb = ps[b][:].bitcast(BF16)

                            def mm(e):
                                ins = None
                                for j in range(2):
                                    ins = e.transpose(pb[:, j * 128:(j + 1) * 128], src(j), ident_b[:])
                                return ins
                            fw.op(PE, mm, reads=reads + [cR], writes=[R("ps", b)])
                            return b, pb[:, 0:256].rearrange("p (a n) -> p a n", a=2)

                        def p2(b):
                            return ps[b][:, 0:256].rearrange("p (a n) -> p a n", a=2)
                        bc2 = lambda tl: tl[:].unsqueeze(1).broadcast_to([128, 2, 128])

                        bt, pv = h_tr(lambda j: vTb[:, tcol(j)], [R("vTb", qi)])
                        yield
                        fw.op(DVE, lambda e: e.tensor_tensor(out=rv("vbt"), in0=pv, in1=beta_t[:, t0:t0 + 2, hd:hd + 1].broadcast_to([128, 2, 128]), op=ALU.mult),
                              reads=[R("ps", bt), gR], writes=[Rr("vbt")])
                        brel(bt)
                        bt, pk = h_tr(lambda j: kT[:, tcol(j)], [R("qk", 1, qi)])
                        yield
                        fw.op(DVE, lambda e: e.tensor_tensor(out=rv("kbe"), in0=pk, in1=ebd_t[:, t0:t0 + 2, hd:hd + 1].broadcast_to([128, 2, 128]), op=ALU.mult),
                              reads=[R("ps", bt), gR], writes=[Rr("kbe")])
                        fw.op(DVE, lambda e: e.tensor_tensor(out=rv("kd"), in0=pk, in1=ekd_t[:, t0:t0 + 2, hd:hd + 1].broadcast_to([128, 2, 128]), op=ALU.mult),
                              reads=[R("ps", bt), gR], writes=[Rr("kd")])
                        brel(bt)
                        bd = h_mm(lambda j: g_t[:, t0 + j, hd:hd + 1].broadcast_to([128, 128]), lambda j: triU[:], [gR, mR])
                        yield
                        for j in range(2):
                            fw.op(DVE, lambda e, j=j: e.scalar_tensor_tensor(out=Ef[0][:, j0 + j, :], in0=ps[bd][:, j * 128:(j + 1) * 128], scalar=dpl_t[:, t0 + j, hd:hd + 1],
                                                                             in1=mposL[:], op0=ALU.subtract, op1=ALU.add),
                                  reads=[R("ps", bd), gR, mR], writes=[RE(0)])
                        fw.op(ACT, lambda e: e.activation(out=ev(0), in_=ev(0), func=AF.Exp, scale=-1.0), reads=[RE(0)], writes=[RE(0)])
                        for j in range(2):
                            fw.op(DVE, lambda e, j=j: e.scalar_tensor_tensor(out=Ef[1][:, j0 + j, :], in0=ps[bd][:, j * 128:(j + 1) * 128], scalar=d_t[:, t0 + j, hd:hd + 1],
                                                                             in1=mnegU[:], op0=ALU.subtract, op1=ALU.add),
                                  reads=[R("ps", bd), gR, mR], writes=[RE(1)])
                        fw.op(ACT, lambda e: e.activation(out=ev(1), in_=ev(1), func=AF.Exp), reads=[RE(1)], writes=[RE(1)])
                        fw.op(ACT, lambda e: e.activation(out=ev(2), in_=p2(bd), func=AF.Exp), reads=[R("ps", bd)], writes=[RE(2)])
                        brel(bd)
                        fw.op(DVE, lambda e: e.tensor_tensor(out=rv("qeT"), in0=qT[:, t0 * 128:(t0 + 2) * 128].rearrange("p (a n) -> p a n", a=2), in1=ev(2), op=ALU.mult),
                              reads=[R("qk", 0, qi), RE(2)], writes=[Rr("qeT")])
                        bkk = h_mm(lambda j: kT[:, tcol(j)], lambda j: kT[:, tcol(j)], [R("qk", 1, qi)])
                        yield
                        fw.op(DVE, lambda e: e.tensor_tensor(out=qv("m"), in0=p2(bkk), in1=ev(0), op=ALU.mult), reads=[R("ps", bkk), RE(0)], writes=[Rq("m")])
                        brel(bkk)
                        bqk = h_mm(lambda j: kT[:, tcol(j)], lambda j: qT[:, tcol(j)], [R("qk", 1, qi), R("qk", 0, qi)])
                        yield
                        fw.op(DVE, lambda e: e.tensor_tensor(out=rv("attnT"), in0=p2(bqk), in1=ev(1), op=ALU.mult),
                              reads=[R("ps", bqk), RE(1)], writes=[Rr("attnT")])
                        brel(bqk)
                        bt, pm = h_tr(lambda j: qd["m"][:, j0 + j, :], [Rq("m")])
                        yield
                        evac_copy(qv("mT"), pm, [R("ps", bt)], [Rq("mT")])
                        brel(bt)
                        fw.op(DVE, lambda e: e.tensor_tensor(out=qv("X"), in0=qv("m"), in1=bc2(b32), op=ALU.mult), reads=[Rq("m"), mR], writes=[Rq("X")])
                        fw.op(DVE, lambda e: e.tensor_tensor(out=qv("m2"), in0=qv("m"), in1=bc2(b2), op=ALU.mult), reads=[Rq("m"), mR], writes=[Rq("m2")])
                        fw.op(DVE, lambda e: e.tensor_tensor(out=qv("XT"), in0=qv("mT"), in1=bc2(b32), op=ALU.mult), reads=[Rq("mT"), mR], writes=[Rq("XT")])
                        fw.op(DVE, lambda e: e.tensor_tensor(out=qv("m1T"), in0=qv("mT"), in1=bc2(b1T), op=ALU.mult), reads=[Rq("mT"), mR], writes=[Rq("m1T")])

                        def qmm(a, bname):
                            return h_mm(lambda j: qd[a][:, j0 + j, :], lambda j: qd[bname][:, j0 + j, :], [Rq(a), Rq(bname)])
                        for (a_, b__, dst_) in [("XT", "X", "P2"), ("X", "XT", "PT2"), ("PT2", "P2", "P"), ("P2", "PT2", "PT4"), ("PT4", "P", "P2"), ("P", "PT4", "PT8")]:
                            b_ = qmm(a_, b__)
                            yield
                            evac_copy(qv(dst_), p2(b_), [R("ps", b_)], [Rq(dst_)])
                            brel(b_)
                        b_ = qmm("PT8", "P2")
                        yield
                        fw.op(DVE, lambda e: e.tensor_tensor(out=qv("Ra"), in0=p2(b_), in1=bc2(ident_b), op=ALU.add), reads=[R("ps", b_), cR], writes=[Rq("Ra")])
                        brel(b_)
                        cur, nxt = "Ra", "Rb"
                        for pw in ["PT8", "PT4", "PT2"]:
                            b_ = qmm(pw, cur)
                            yield
                            fw.op(DVE, lambda e: e.tensor_tensor(out=qv(nxt), in0=p2(b_), in1=qv(cur), op=ALU.add),
                                  reads=[R("ps", b_), Rq(cur)], writes=[Rq(nxt)])
                            brel(b_)
                            cur, nxt = nxt, cur
                        b_ = qmm("XT", cur)
                        yield
                        fw.op(DVE, lambda e: e.tensor_tensor(out=qv("T0"), in0=qv(cur), in1=p2(b_), op=ALU.subtract),
                              reads=[R("ps", b_), Rq(cur)], writes=[Rq("T0")])
                        brel(b_)
                        bt, pm = h_tr(lambda j: qd["T0"][:, j0 + j, :], [Rq("T0")])
                        yield
                        evac_copy(qv("T0T"), pm, [R("ps", bt)], [Rq("T0T")])
                        brel(bt)
                        b_ = qmm("m1T", "T0")
                        yield
                        evac_copy(qv("Y"), p2(b_), [R("ps", b_)], [Rq("Y")])
                        brel(b_)
                        b_ = qmm("T0T", "Y")
                        yield
                        fw.op(DVE, lambda e: e.tensor_tensor(out=qv("T1"), in0=qv("T0"), in1=p2(b_), op=ALU.subtract), reads=[R("ps", b_), Rq("T0")], writes=[Rq("T1")])
                        brel(b_)
                        bt, pm = h_tr(lambda j: qd["T1"][:, j0 + j, :], [Rq("T1")])
                        yield
                        evac_copy(qv("T1T"), pm, [R("ps", bt)], [Rq("T1T")])
                        brel(bt)
                        b_ = qmm("m2", "T1T")
                        yield
                        evac_copy(qv("Yp"), p2(b_), [R("ps", b_)], [Rq("Yp")])
                        brel(b_)
                        b_ = qmm("T1", "Yp")
                        yield
                        fw.op(DVE, lambda e: e.tensor_tensor(out=rv("T2T"), in0=qv("T1T"), in1=p2(b_), op=ALU.subtract),
                              reads=[R("ps", b_), Rq("T1T")], writes=[Rr("T2T")])
                        brel(b_)
                        b_ = h_mm(lambda j: ring["kbe"][rs][:, j0 + j, :], lambda j: ring["T2T"][rs][:, j0 + j, :], [Rr("kbe"), Rr("T2T")])
                        yield
                        evac_copy(rv("nwT"), p2(b_), [R("ps", b_)], [Rr("nwT")])
                        brel(b_)

                    def seq_f(qi):
                        rs = qi % 2
                        t0 = qi * 4
                        oi = 0
                        if qi == 0:
                            fw.op(DVE, lambda e: e.memset(Sf[:], 0.0), reads=[], writes=[R("Sf")])
                            fw.op(DVE, lambda e: e.memset(Sb[:], 0.0), reads=[], writes=[R("Sb")])
                        yield from finish_o(l, li, hd, qi, oq[oi], R("oq", oi), 512, wz, wzR, wo, woR, ogb[0], rn, szq, ogb, rtmp, BK_AUX, won, mR, do_z=True, do_rest=False)
                        for j in range(4):
                            ti = t0 + j
                            vi = ti % 2
                            Rr = lambda nm: R("ring", nm, rs, j // 2)
                            b1 = bacq(BK_SEQ)

                            def mm1(e):
                                e.matmul(ps[b1][:, 0:128], ring["T2T"][rs][:, j, :], ring["vbt"][rs][:, j, :], start=True, stop=False)
                                return e.matmul(ps[b1][:, 0:128], ring["nwT"][rs][:, j, :], Sb[:], start=False, stop=True)
                            fw.op(PE, mm1, reads=[Rr("T2T"), Rr("vbt"), Rr("nwT"), R("Sb")], writes=[R("ps", b1)])
                            yield
                            fw.op(ACT, lambda e: e.copy(out=vnb[vi][:], in_=ps[b1][:, 0:128]), reads=[R("ps", b1)], writes=[R("vnbA", vi)])
                            brel(b1)
                            b2_ = bacq(BK_SEQ)

                            def mm2(e):
                                e.matmul(ps[b2_][:, 0:128], Sb[:], ring["qeT"][rs][:, j, :], start=True, stop=False)
                                e.matmul(ps[b2_][:, 0:128], vnb[vi][:], ring["attnT"][rs][:, j, :], start=False, stop=True)
                                return e.matmul(ps[b2_][:, 128:256], ring["kd"][rs][:, j, :], vnb[vi][:], start=True, stop=True)
                            fw.op(PE, mm2, reads=[R("Sb"), Rr("qeT"), R("vnbA", vi), Rr("attnT"), Rr("kd")], writes=[R("ps", b2_)])
                            yield
                            fw.op(DVE, lambda e: e.scalar_tensor_tensor(out=Sf[:], in0=Sf[:], scalar=edl_t[:, ti, hd:hd + 1], in1=ps[b2_][:, 128:256],
                                                                        op0=ALU.mult, op1=ALU.add), reads=[R("ps", b2_), R("Sf"), gR], writes=[R("Sf")])
                            fw.op(ACT, lambda e: e.copy(out=Sb[:], in_=Sf[:]), reads=[R("Sf")], writes=[R("Sb")])
                            fw.op(ACT, lambda e: e.copy(out=oq[oi][:, j * 128:(j + 1) * 128], in_=ps[b2_][:, 0:128]), reads=[R("ps", b2_)], writes=[R("oq", oi)])
                            brel(b2_)
                        if qi == 3:
                            fw.dma(SP, "spo", lambda e: e.dma_start(out=SP_out[li, hd], in_=Sf[:]), reads=[R("Sf")])
                        yield from finish_o(l, li, hd, qi, oq[oi], R("oq", oi), 512, wz, wzR, wo, woR, ogb[0], rn, szq, ogb, rtmp, BK_AUX, won, mR, do_z=False, do_rest=True)

                    def run_rr(gens):
                        gens = list(gens)
                        while gens:
                            for g_ in list(gens):
                                try:
                                    next(g_)
                                except StopIteration:
                                    gens.remove(g_)
                    def samples_gen():
                        for half in range(2):
                            s0 = half * 8
                            hs = slice(s0, s0 + 8)
                            fw.dma(SP, "ssi", lambda e: e.dma_start(out=Ss[:], in_=S_in[li, hs, hd].rearrange("s k v -> k s v")), writes=[R("Ss")])
                            b = bacq(BK_AUX)

                            def mms(e):
                                ins = None
                                for s_ in range(8):
                                    ins = e.matmul(ps[b][:, 2 * s_:2 * s_ + 2], Ss[:, s_, :], kqf[:, s0 + s_, :], start=True, stop=True)
                                return ins
                            fw.op(PE, mms, reads=[R("Ss"), R("kqf")], writes=[R("ps", b)])
                            yield
                            skq = ps[b][:, 0:16].rearrange("p (s c) -> p s c", c=2)
                            betaBC = sbc[:, hd, 0, hs]
                            egBC = sbc[:, hd, 1, hs]
                            fw.op(DVE, lambda e: e.tensor_tensor(out=stmp[1][:, 0:8], in0=skq[:, :, 0], in1=egBC, op=ALU.mult), reads=[R("ps", b), R("sbc")], writes=[R("sA", 1)])
                            fw.op(DVE, lambda e: e.tensor_tensor(out=stmp[2][:, 0:8], in0=skq[:, :, 1], in1=egBC, op=ALU.mult), reads=[R("ps", b), R("sbc")], writes=[R("sA", 2)])
                            brel(b)
                            fw.op(DVE, lambda e: e.tensor_tensor(out=stmp[1][:, 0:8], in0=vs_f[:, s0:s0 + 8], in1=stmp[1][:, 0:8], op=ALU.subtract), reads=[R("sA", 1), R("vs_f")], writes=[R("sA", 1)])
                            fw.op(DVE, lambda e: e.tensor_tensor(out=stmp[1][:, 0:8], in0=stmp[1][:, 0:8], in1=betaBC, op=ALU.mult), reads=[R("sA", 1), R("sbc")], writes=[R("sA", 1)])
                            fw.op(DVE, lambda e: e.tensor_tensor(out=stmp[0][:, 0:8], in0=kqf[:, hs, 0], in1=kqf[:, hs, 1], op=ALU.mult), reads=[R("kqf")], writes=[R("sA", 0)])
                            bqk = bacq(BK_AUX)
                            fw.op(PE, lambda e: e.matmul(ps[bqk][:, 0:8], ones_f[:], stmp[0][:, 0:8], start=True, stop=True), reads=[R("sA", 0), cR], writes=[R("ps", bqk)])
                            yield
                            fw.op(DVE, lambda e: e.tensor_tensor(out=stmp[3][:, 0:8], in0=ps[bqk][:, 0:8], in1=stmp[1][:, 0:8], op=ALU.mult), reads=[R("ps", bqk), R("sA", 1)], writes=[R("sA", 3)])
                            brel(bqk)
                            fw.op(DVE, lambda e: e.tensor_tensor(out=o_s[:, hs], in0=stmp[2][:, 0:8], in1=stmp[3][:, 0:8], op=ALU.add), reads=[R("sA", 2), R("sA", 3)], writes=[R("o_s")])
                            b3 = bacq(BK_AUX)
                            fw.op(PE, lambda e: e.transpose(ps[b3][0:8, 0:128], kqf[:, hs, 0], ident_f[:]), reads=[R("kqf"), cR], writes=[R("ps", b3)])
                            yield
                            fw.op(DVE, lambda e: e.tensor_copy(ktok[0:8, :], ps[b3][0:8, 0:128]), reads=[R("ps", b3)], writes=[R("ktok")])
                            brel(b3)
                            b4 = bacq(BK_AUX)
                            fw.op(PE, lambda e: e.transpose(ps[b4][0:8, 0:128], stmp[1][:, 0:8], ident_f[:]), reads=[R("sA", 1), cR], writes=[R("ps", b4)])
                            yield
                            fw.op(DVE, lambda e: e.tensor_copy(vtok[0:8, :], ps[b4][0:8, 0:128]), reads=[R("ps", b4)], writes=[R("vtok")])
                            brel(b4)
                            for s_ in range(8):
                                km = 0
                                fw.op(DVE, lambda e: e.tensor_scalar(out=kmask[km][0:8, :], in0=ktok[0:8, :], scalar1=ident_f[0:8, s_:s_ + 1], scalar2=None, op0=ALU.mult),
                                      reads=[R("ktok"), cR], writes=[R("kmask", km)])
                                b5 = bacq(BK_AUX)
                                fw.op(PE, lambda e: e.matmul(ps[b5][:, 0:128], kmask[km][0:8, :], vtok[0:8, :], start=True, stop=True),
                                      reads=[R("kmask", km), R("vtok")], writes=[R("ps", b5)])
                                yield
                                fw.op(DVE, lambda e: e.scalar_tensor_tensor(out=Ss[:, s_, :], in0=Ss[:, s_, :], scalar=sbc[:, hd, 1, s0 + s_:s0 + s_ + 1], in1=ps[b5][:, 0:128],
                                                                            op0=ALU.mult, op1=ALU.add), reads=[R("ps", b5), R("Ss"), R("sbc")], writes=[R("Ss")])
                                brel(b5)
                            fw.dma(SP, "sso", lambda e: e.dma_start(out=SS_out[li, hs, hd].rearrange("s k v -> k s v"), in_=Ss[:]), reads=[R("Ss")])
                        yield from finish_o(l, li, hd, 4, o_s, R("o_s"), 16, wz, wzR, wo, woR, sqb_s, rn_s, [szq_s], [ogb_s], rtmp_s, BK_AUX, won, mR, tag="s")

                    return dict(hc=half_chain, seq=seq_f, samples=samples_gen, pjob=pjob, sjob=sjob, load_qkv=load_qkv, load_zo=load_zo)

                HD = [make_head(h) for h in range(8)]
                tasks = {}

                def add(name, deps, fac):
                    tasks[name] = ([d for d in deps if d is not None], fac)

                def seqname(idx):
                    return ("SEQ", idx // 4, idx % 4) if idx >= 0 else None
                for h in range(8):
                    add(("LQ", h), ([("PJ", h - 1, 3, fi) for fi in range(3)] + [("SJ", h - 1, fi) for fi in range(3)]) if h > 0 else [],
                        lambda h=h: HD[h]["load_qkv"](h))
                    add(("LZ", h), [("SEQ", h - 2, 3), ("SAM", h - 2)] if h >= 2 else [], lambda h=h: HD[h]["load_zo"](h))
                    for t in range(4):
                        for fi in range(3):
                            deps = [("LQ", h)]
                            if h > 0:
                                deps += [("HC", h - 1, t, 0), ("HC", h - 1, t, 1)]
                            if t > 0:
                                deps.append(("PJ", h, t - 1, fi))
                            elif h > 0:
                                deps.append(("PJ", h - 1, 3, fi))
                            add(("PJ", h, t, fi), deps, lambda h=h, t=t, fi=fi: HD[h]["pjob"](h, t, fi, fi, BK_PJ))
                    for fi in range(3):
                        add(("SJ", h, fi), [("LQ", h)] + ([("SAM", h - 1), ("SJ", h - 1, fi)] if h > 0 else []),
                            lambda h=h, fi=fi: HD[h]["sjob"](h, fi, BK_PJ))
                    for q in range(4):
                        for hf in range(2):
                            deps = [("PJ", h, q, fi) for fi in range(3)]
                            if q > 0:
                                deps.append(("HC", h, q - 1, hf))
                            elif h > 0:
                                deps.append(("HC", h - 1, 3, hf))
                            deps.append(seqname(h * 4 + q - 2))
                            add(("HC", h, q, hf), deps, lambda h=h, q=q, hf=hf: HD[h]["hc"](q, hf))
                    for q in range(4):
                        add(("SEQ", h, q), [("HC", h, q, 0), ("HC", h, q, 1), ("LZ", h), seqname(h * 4 + q - 1)], lambda h=h, q=q: HD[h]["seq"](q))
                    add(("SAM", h), [("SJ", h, fi) for fi in range(3)] + [("LZ", h)] + ([("SAM", h - 1)] if h > 0 else []), lambda h=h: HD[h]["samples"]())
                prio = {"LQ": 0, "LZ": 0, "HC": 1, "SEQ": 2, "SAM": 3, "PJ": 4, "SJ": 5}
                order = sorted(tasks.keys(), key=lambda n: (n[1], prio[n[0]]) + tuple(n[2:]))
                started, done_t = set(), set()
                cyc = 0
                active = []
                while len(done_t) < len(tasks):
                    nstart = 0
                    for name in order:
                        if name in started:
                            continue
                        deps, fac = tasks[name]
                        if all(d in done_t for d in deps):
                            started.add(name)
                            nstart += 1
                            g_ = fac()
                            if g_ is None:
                                done_t.add(name)
                            else:
                                active.append((name, g_))
                    active.sort(key=lambda it: (prio[it[0][0]], it[0][1:]))
                    if not active and nstart == 0:
                        raise RuntimeError("scheduler deadlock: %s" % sorted(set(tasks) - done_t)[:5])
                    cyc += 1
                    for it in list(active):
                        if prio[it[0][0]] >= 4 and cyc % 2 == 1:
                            continue
                        try:
                            next(it[1])
                        except StopIteration:
                            active.remove(it)
                            done_t.add(it[0])
                fw.barrier()

        octr = [0]

        def finish_o(l, li, hd, t, osrc, oR, n, wz, wzR, wo, woR, sqb, rn, szq, ogb, rtmp, BK_AUX, won, mR, tag=None, do_z=True, do_rest=True):
            cs, n = tcols(t)
            q = 0
            zq = 0
            KS = ("ogb", 0) if tag is None else ("sqbA", tag)
            KR = ("rn",) if tag is None else ("rn", tag)
            KZ = ("szq", 0) if tag is None else ("szq", 0, tag)
            KO = ("ogb", 0) if tag is None else ("ogb", 0, tag)
            KT = "rtmp" if tag is None else "rtmp_" + tag
            if do_z:
                bz = bacq(BK_AUX)

                for k2 in range(4):
                    def mm(e):
                        e.matmul(ps[bz][:, :n], wz[:, 2 * k2, :], hT[:, 2 * k2, cs], start=(k2 == 0), stop=False)
                        return e.matmul(ps[bz][:, :n], wz[:, 2 * k2 + 1, :], hT[:, 2 * k2 + 1, cs], start=False, stop=(k2 == 3))
                    fw.op(PE, mm, reads=[R("h", 2 * k2, t), R("h", 2 * k2 + 1, t), wzR], writes=[R("ps", bz)])
                    yield
                fw.op(ACT, lambda e: e.activation(out=szq[zq][:, :n], in_=ps[bz][:, :n], func=AF.Exp, scale=-1.0), reads=[R("ps", bz)], writes=[R(*KZ)])
                yield
                fw.op(ACT, lambda e: e.activation(out=szq[zq][:, :n], in_=szq[zq][:, :n], func=AF.Ln, bias=eps_t[:, 3:4], scale=1.0), reads=[R(*KZ), cR], writes=[R(*KZ)])
                yield
                fw.op(ACT, lambda e: e.activation(out=szq[zq][:, :n], in_=szq[zq][:, :n], func=AF.Exp, scale=-1.0), reads=[R(*KZ)], writes=[R(*KZ)])
                yield
                fw.op(DVE, lambda e: e.tensor_tensor(out=szq[zq][:, :n], in0=ps[bz][:, :n], in1=szq[zq][:, :n], op=ALU.mult), reads=[R("ps", bz), R(*KZ)], writes=[R(*KZ)])
                brel(bz)
                yield
            if not do_rest:
                return
            b = bacq(BK_AUX)
            fw.op(ACT, lambda e: e.activation(out=sqb[:, :n], in_=osrc[:, :n], func=AF.Square), reads=[oR], writes=[R(*KS)])
            fw.op(PE, lambda e: e.matmul(ps[b][:, :n], ones_b[:], sqb[:, :n], start=True, stop=True), reads=[R(*KS), cR], writes=[R("ps", b)])
            yield
            fw.op(ACT, lambda e: e.activation(out=rn[:, :n], in_=ps[b][:, :n], func=AF.Ln, bias=eps_t[:, 1:2], scale=1.0 / 128), reads=[R("ps", b), cR], writes=[R(*KR)])
            brel(b)
            yield
            fw.op(ACT, lambda e: e.activation(out=rn[:, :n], in_=rn[:, :n], func=AF.Exp, scale=-0.5), reads=[R(*KR)], writes=[R(*KR)])
            yield
            fw.op(DVE, lambda e: e.tensor_tensor(out=rn[:, :n], in0=osrc[:, :n], in1=rn[:, :n], op=ALU.mult), reads=[oR, R(*KR)], writes=[R(*KR)])
            yield
            fw.op(DVE, lambda e: e.scalar_tensor_tensor(out=ogb[q][:, :n], in0=rn[:, :n], scalar=won[:, li:li + 1], in1=szq[zq][:, :n], op0=ALU.mult, op1=ALU.mult),
                  reads=[R(*KR), R(*KZ), mR], writes=[R(*KO)])
            for m in range(8):
                bo = bacq(BK_AUX)
                fw.op(PE, lambda e: e.matmul(ps[bo][:, :n], wo[:, m * 128:(m + 1) * 128], ogb[q][:, :n], start=True, stop=True),
                      reads=[R(*KO), woR], writes=[R("ps", bo)])
                yield
                resid_update(rtmp, 1, m, t, bo, None, KT)
                brel(bo)

        def final_norm():
            with contextlib.ExitStack() as ph:
                def dst(c, t, src, srcR):
                    if t < 4:
                        cs, n = tcols(t)
                        fw.dma(SP, "yo_" + srcR.name, lambda e: e.dma_start(out=yT_out[c * 128:(c + 1) * 128, cs], in_=src[:]), reads=[srcR])
                    else:
                        fw.dma(SP, "yo4", lambda e: e.dma_start(out=yT_out[:, TP:T].rearrange("(c p) n -> p c n", p=128), in_=src[:]), reads=[srcR])
                fnA = fw.sbuf("fnA", [128, 8, 1], F32, ph)
                fw.op(DVE, lambda e: e.tensor_scalar(out=fnA[:, :, 0], in0=fnw_s[:], scalar1=32.0, scalar2=None, op0=ALU.mult), reads=[cR], writes=[cR])
                norm_to_h(ph, fnA, None, dst)
                fw.barrier()

        def dump_x():
            for c in range(8):
                fw.dma(SP, "yo%d" % (c % 4), lambda e, c=c: e.dma_start(out=yT_out[c * 128:(c + 1) * 128, :], in_=xT[:, c, :]), reads=[R("x", c, t) for t in range(5)])
            fw.barrier()

        stage = 0

        def check_stop():
            nonlocal stage
            stage += 1
            return stop_after is not None and stage >= stop_after
        done = False
        for l in range(DEPTH):
            if stop_after is not None:
                ada(l)
                ffn(l, 0, 0)
            elif l == 0:
                with contextlib.ExitStack() as ph0:
                    for _ in ada_gen(0, ph0, 0, 6, (0,)):
                        pass
                    fw.barrier()
                ffn(0, 0, 0, ada_rest=0)
            else:
                ffn(l, 0, 0)
            if check_stop():
                done = True
                break
            if l % 2 == 0:
                mixer_a(l, l // 2)
            else:
                mixer_b(l, l // 2)
            if check_stop():
                done = True
                break
            ffn(l, 1, 2, next_ada=(l + 1 if (l + 1 < DEPTH and stop_after is None) else None))
            if check_stop():
                done = True
                break
        if done:
            dump_x()
        else:
            final_norm()
        fw.finish()
    return nc


_NC_CACHE = {}


def _prep_shared(inp):
    f = lambda a: np.ascontiguousarray(np.asarray(a, dtype=np.float32))
    sh = {}
    sh["w_ada"] = f(inp["w_ada"])
    sh["b_ada_p"] = f(inp["b_ada"].reshape(DEPTH, 72, 128).transpose(2, 0, 1).reshape(128, DEPTH * 72))
    sh["norm_w_p"] = f(inp["norm_w"].reshape(DEPTH, 3, 8, 128).transpose(3, 0, 1, 2).reshape(128, DEPTH * 24))
    sh["ffn_w_gu"] = f(inp["ffn_w_gu"])
    sh["ffn_w_down"] = f(inp["ffn_w_down"])
    sh["a_w_in"] = f(inp["a_w_in"])
    sh["a_w_conv_p"] = f(inp["a_w_conv"].reshape(2, 4, 24, 128).transpose(3, 0, 1, 2).reshape(128, 2 * 4 * 24))
    sh["a_log"] = f(inp["a_log"])
    sh["a_dt_bias"] = f(inp["a_dt_bias"])
    sh["a_w_onorm_p"] = f(inp["a_w_onorm"].T)
    sh["a_w_out"] = f(inp["a_w_out"])
    sh["b_w_in"] = f(inp["b_w_in"])
    sh["b_vnorm_w"] = f(inp["b_vnorm_w"])
    sh["b_vnorm_b"] = f(inp["b_vnorm_b"])
    sh["b_w_sT"] = f(np.asarray(inp["b_w_s"]).transpose(0, 1, 3, 2))
    sh["b_w_s00"] = f(np.asarray(inp["b_w_s"])[:, :, 0, 0])
    sh["b_b_s"] = f(inp["b_b_s"])
    sh["b_b_s0"] = f(np.asarray(inp["b_b_s"])[:, :, 0])
    sh["b_w_out"] = f(inp["b_w_out"])
    sh["fnw_p"] = f(np.asarray(inp["final_norm_w"]).reshape(8, 128).T)
    return sh


def _prep_core(inp, i):
    f = lambda a: np.ascontiguousarray(np.asarray(a, dtype=np.float32))
    xp = np.asarray(inp["x_prompt"])[i]
    xs = np.asarray(inp["x_sample"])[i * NS:(i + 1) * NS, 0]
    d = {}
    d["xT"] = f(np.concatenate([xp, xs], axis=0).T)
    d["cT"] = f(np.concatenate([np.asarray(inp["c_prompt"])[i:i + 1], np.asarray(inp["c_sample"])[i * NS:(i + 1) * NS]], axis=0).T)
    d["convT"] = f(np.asarray(inp["state_a_conv"])[:, i * NS:(i + 1) * NS].transpose(0, 3, 1, 2))
    d["S_in"] = f(np.asarray(inp["state_a_S"])[:, i * NS:(i + 1) * NS])
    return d


def kernel(**inputs):
    stop_after = inputs.pop("_stop_after", None)
    key = stop_after
    if key not in _NC_CACHE:
        _NC_CACHE[key] = build_program(stop_after)
    nc = _NC_CACHE[key]
    sh = _prep_shared(inputs)
    in_maps = []
    for i in range(8):
        d = dict(sh)
        d.update(_prep_core(inputs, i))
        in_maps.append(d)
    res = run_bass_kernel_spmd(nc, in_maps, core_ids=list(range(8)))
    rs = res.results
    y_prompt = np.stack([rs[i]["yT"][:, :TP].T for i in range(8)], axis=0).astype(np.float32)
    y_sample = np.concatenate([rs[i]["yT"][:, TP:].T for i in range(8)], axis=0)[:, None, :].astype(np.float32)
    conv_prompt = np.stack([rs[i]["convP"].transpose(0, 2, 1) for i in range(8)], axis=1).astype(np.float32)
    S_prompt = np.stack([rs[i]["S_p"] for i in range(8)], axis=1).astype(np.float32)
    conv_sample = np.concatenate([rs[i]["convS"].transpose(0, 2, 3, 1) for i in range(8)], axis=1).astype(np.float32)
    S_sample = np.concatenate([rs[i]["S_s"] for i in range(8)], axis=1).astype(np.float32)
    v_sample = np.concatenate([rs[i]["v_s"] for i in range(8)], axis=1)[:, :, None, :].astype(np.float32)
    return (np.ascontiguousarray(y_prompt), np.ascontiguousarray(y_sample), np.ascontiguousarray(conv_prompt), np.ascontiguousarray(S_prompt),
            np.ascontiguousarray(conv_sample), np.ascontiguousarray(S_sample), np.ascontiguousarray(v_sample))
```

```python
import contextlib
import numpy as np
import concourse.bass as bass
import concourse.mybir as mybir
from concourse.bass_utils import run_bass_kernel_spmd

F32 = mybir.dt.float32
BF16 = mybir.dt.bfloat16
AF = mybir.ActivationFunctionType
ALU = mybir.AluOpType
AX = mybir.AxisListType

D = 1024
TP = 2048
NS = 16
T = TP + NS
DEPTH = 4
DFF = 2816
NJ = DFF // 128
EPS = 1e-6
TILES = [(0, 512), (512, 512), (1024, 512), (1536, 512), (2048, 16)]
BIG = 1.0e4


class Res:
    __slots__ = ("name", "w", "rd")

    def __init__(self, name):
        self.name = name
        self.w = None
        self.rd = []


class Eng:
    def __init__(self, name, eng, is_pe=False):
        self.name = name
        self.eng = eng
        self.sem = None
        self.cnt = 0
        self.seen = {}
        self.is_pe = is_pe


class FW:
    def __init__(self, nc, stack):
        self.nc = nc
        self.stack = stack
        self.pe = Eng("pe", nc.tensor, True)
        self.dve = Eng("dve", nc.vector)
        self.act = Eng("act", nc.scalar)
        self.pool = Eng("pool", nc.gpsimd)
        self.sp = Eng("sp", nc.sync)
        self.engs = [self.pe, self.dve, self.act, self.pool, self.sp]
        for e in self.engs:
            e.sem = stack.enter_context(nc.semaphore("s_" + e.name))
        self.dma_sems = {}
        self.res = {}

    def R(self, *key):
        r = self.res.get(key)
        if r is None:
            r = Res(str(key))
            self.res[key] = r
        return r

    def sbuf(self, name, shape, dtype, stack=None):
        self.nsb = getattr(self, "nsb", 0) + 1
        return (stack or self.stack).enter_context(self.nc.sbuf_tensor("sb%d_%s" % (self.nsb, name), list(shape), dtype))

    def dsem(self, name):
        if name not in self.dma_sems:
            h = self.stack.enter_context(self.nc.semaphore("d%d" % len(self.dma_sems)))
            self.dma_sems[name] = [h, 0]
        return self.dma_sems[name]

    def _collect(self, E, reads, writes):
        toks = []
        for r in reads:
            if r.w is not None:
                toks.append(r.w)
        for w in writes:
            if w.w is not None:
                toks.append(w.w)
            toks.extend(w.rd)
        need = {}
        for t in toks:
            kind, key, sem, val = t
            if kind == "eng" and key is E and E.is_pe:
                continue
            if E.seen.get(id(sem), 0) >= val:
                continue
            if need.get(id(sem), (None, 0))[1] < val:
                need[id(sem)] = (sem, val)
        for sem, val in need.values():
            E.seen[id(sem)] = val
            E.eng.wait_ge(sem, val)

    def _mark(self, tok, reads, writes):
        for r in reads:
            if len(r.rd) > 64:
                r.rd = r.rd[-32:]
            r.rd.append(tok)
        for w in writes:
            w.w = tok
            w.rd = []

    def op(self, E, fn, reads=(), writes=()):
        pr = [r for r in reads if r.name.startswith("('ps'")]
        if pr:
            writes = list(writes) + pr
        self._collect(E, reads, writes)
        E.cnt += 1
        fn(E.eng).then_inc(E.sem, 1)
        tok = ("eng", E, E.sem, E.cnt)
        self._mark(tok, reads, writes)
        return tok

    def dma(self, E, stream, fn, reads=(), writes=()):
        self._collect(E, reads, writes)
        ds = self.dsem(stream)
        ds[1] += 16
        fn(E.eng).then_inc(ds[0], 16)
        tok = ("dma", stream, ds[0], ds[1])
        self._mark(tok, reads, writes)
        return tok

    def barrier(self):
        for E in self.engs:
            for O in self.engs:
                if O is not E and O.cnt > 0 and E.seen.get(id(O.sem), 0) < O.cnt:
                    E.seen[id(O.sem)] = O.cnt
                    E.eng.wait_ge(O.sem, O.cnt)
            for name, (h, v) in self.dma_sems.items():
                if v > 0 and E.seen.get(id(h), 0) < v:
                    E.seen[id(h)] = v
                    E.eng.wait_ge(h, v)
        self.res = {}

    def finish(self):
        for name, (h, v) in self.dma_sems.items():
            if v > 0 and self.sp.seen.get(id(h), 0) < v:
                self.sp.eng.wait_ge(h, v)
        for E in self.engs:
            if E is not self.sp and E.cnt > 0:
                self.sp.eng.wait_ge(E.sem, E.cnt)


def build_program(stop_after=None):
    nc = bass.Bass("TRN2", target_bir_lowering=False)

    def din(name, shape):
        return nc.dram_tensor(name, list(shape), F32, kind="ExternalInput").ap()

    def dout(name, shape):
        return nc.dram_tensor(name, list(shape), F32, kind="ExternalOutput").ap()

    xT_in = din("xT", [D, T])
    cT_in = din("cT", [D, 17])
    convT_in = din("convT", [2, 3072, NS, 3])
    S_in = din("S_in", [2, NS, 8, 128, 128])
    w_ada = din("w_ada", [DEPTH, D, 9216])
    b_ada_p = din("b_ada_p", [128, DEPTH * 72])
    norm_w_p = din("norm_w_p", [128, DEPTH * 3 * 8])
    ffn_w_gu = din("ffn_w_gu", [DEPTH, 2, D, 2 * DFF])
    ffn_w_down = din("ffn_w_down", [DEPTH, 2, DFF, D])
    a_w_in = din("a_w_in", [2, D, 4112])
    a_w_conv_p = din("a_w_conv_p", [128, 2 * 4 * 24])
    a_log = din("a_log", [2, 8])
    a_dt_bias = din("a_dt_bias", [2, 8])
    a_w_onorm_p = din("a_w_onorm_p", [128, 2])
    a_w_out = din("a_w_out", [2, D, D])
    b_w_in = din("b_w_in", [2, D, 4096])
    b_vnorm_w = din("b_vnorm_w", [2, 2048])
    b_vnorm_b = din("b_vnorm_b", [2, 2048])
    b_w_sT = din("b_w_sT", [2, 8, 128, 128])
    b_w_s00 = din("b_w_s00", [2, 8])
    b_b_s = din("b_b_s", [2, 8, 128])
    b_b_s0 = din("b_b_s0", [2, 8])
    b_w_out = din("b_w_out", [2, 2048, D])
    fnw_p = din("fnw_p", [128, 8])

    yT_out = dout("yT", [D, T])
    convP_out = dout("convP", [2, 3072, 3])
    SP_out = dout("S_p", [2, 8, 128, 128])
    convS_out = dout("convS", [2, 3072, NS, 3])
    SS_out = dout("S_s", [2, NS, 8, 128, 128])
    vS_out = dout("v_s", [2, NS, 2048])

    with contextlib.ExitStack() as st:
        fw = FW(nc, st)
        R = fw.R
        PE, DVE, ACT, POOL, SP = fw.pe, fw.dve, fw.act, fw.pool, fw.sp

        xT = fw.sbuf("xT", [128, 8, T], F32)
        hT = fw.sbuf("hT", [128, 8, T], BF16)
        mod = fw.sbuf("mod", [128, 72, 17], F32)
        modA = fw.sbuf("modA", [128, 3, 8, 17], F32)
        modG = fw.sbuf("modG", [128, 3, 8, 17], F32)
        stash = fw.sbuf("stash", [128, 3, 8, 17], F32)
        scb = fw.sbuf("scb", [128, 8, 17], BF16)
        ctmp = fw.sbuf("ctmp", [128, 8, 17], F32)
        b_ada_s = fw.sbuf("b_ada_s", [128, DEPTH * 72], F32)
        normw_s = fw.sbuf("normw_s", [128, DEPTH * 3 * 8], F32)
        fnw_s = fw.sbuf("fnw_s", [128, 8], F32)
        ones_b = fw.sbuf("ones_b", [128, 128], BF16)
        ones_f = fw.sbuf("ones_f", [128, 128], F32)
        ident_f = fw.sbuf("ident_f", [128, 128], F32)
        ident_b = fw.sbuf("ident_b", [128, 128], BF16)
        eps_t = fw.sbuf("eps_t", [128, 4], F32)
        ps = [st.enter_context(nc.psum_tensor("ps%d" % i, [128, 512], F32)) for i in range(8)]
        bankctr = [0]

        bank_ctr = {}

        def bank(group):
            key = tuple(group)
            c = bank_ctr.get(key, 0)
            bank_ctr[key] = c + 1
            return group[c % len(group)]
        held = set()
        bctr = {}

        def bacq(group):
            key = tuple(group)
            c = bctr.get(key, 0)
            for k in range(len(group)):
                i = group[(c + k) % len(group)]
                if i not in held:
                    bctr[key] = c + k + 1
                    held.add(i)
                    return i
            raise RuntimeError("no free psum bank in %s" % (group,))

        def brel(i):
            held.discard(i)

        def acq_wait(group):
            while True:
                for i in group:
                    if i not in held:
                        held.add(i)
                        return i
                yield

        cR = R("const")
        fw.op(POOL, lambda e: e.memset(ones_f[:], 1.0), writes=[cR])
        fw.op(POOL, lambda e: e.memset(ones_b[:], 1.0), writes=[cR])
        fw.op(POOL, lambda e: e.memset(ident_f[:], 1.0), writes=[cR])
        fw.op(POOL, lambda e: e.affine_select(out=ident_f[:], in_=ident_f[:], pattern=[[-1, 128]], compare_op=ALU.is_equal,
                                              fill=0.0, base=0, channel_multiplier=1), reads=[cR], writes=[cR])
        fw.op(POOL, lambda e: e.tensor_copy(ident_b[:], ident_f[:]), reads=[cR], writes=[cR])
        fw.op(POOL, lambda e: e.memset(eps_t[:, 0:1], D * EPS), writes=[cR])
        fw.op(POOL, lambda e: e.memset(eps_t[:, 1:2], EPS), writes=[cR])
        fw.op(POOL, lambda e: e.memset(eps_t[:, 2:3], 128 * EPS), writes=[cR])
        fw.op(POOL, lambda e: e.memset(eps_t[:, 3:4], 1.0), writes=[cR])

        fw.dma(SP, "ld0", lambda e: e.dma_start(out=b_ada_s[:], in_=b_ada_p), writes=[cR])
        fw.dma(SP, "ld1", lambda e: e.dma_start(out=normw_s[:], in_=norm_w_p), writes=[cR])
        fw.dma(SP, "ld2", lambda e: e.dma_start(out=fnw_s[:], in_=fnw_p), writes=[cR])
        fw.dma(SP, "ld3", lambda e: e.dma_start(out=ctmp[:], in_=cT_in.rearrange("(c p) n -> p c n", p=128)), writes=[R("ctmp")])
        for c in range(8):
            fw.dma(SP, "ldx%d" % c, lambda e, c=c: e.dma_start(out=xT[:, c, :], in_=xT_in[c * 128:(c + 1) * 128, :]),
                   writes=[R("x", c, t) for t in range(5)])
        fw.op(ACT, lambda e: e.activation(out=scb[:], in_=ctmp[:], func=AF.Silu), reads=[R("ctmp")], writes=[R("scb")])

        def tcols(t):
            s0, n = TILES[t]
            return slice(s0, s0 + n), n

        def ada_derive(l, s):
            sc = mod[:, (s * 3 + 1) * 8:(s * 3 + 2) * 8, :]
            gt = mod[:, (s * 3 + 2) * 8:(s * 3 + 3) * 8, :]
            fw.op(DVE, lambda e: e.tensor_scalar(out=modA[:, s], in0=sc, scalar1=1.0, scalar2=32.0, op0=ALU.add, op1=ALU.mult),
                  reads=[R("mod")], writes=[R("modA")])
            fw.op(DVE, lambda e: e.tensor_tensor(
                out=modA[:, s], in0=modA[:, s],
                in1=normw_s[:, (l * 3 + s) * 8:(l * 3 + s + 1) * 8].unsqueeze(2).broadcast_to([128, 8, 17]), op=ALU.mult),
                reads=[R("modA"), cR], writes=[R("modA")])
            res = 1.0 if s == 1 else 0.5
            fw.op(DVE, lambda e: e.tensor_scalar(out=modG[:, s], in0=gt, scalar1=1.0, scalar2=res, op0=ALU.add, op1=ALU.mult),
                  reads=[R("mod")], writes=[R("modG")])

        def ada_gen(l, ph, b0=0, b1=18, derive=(0, 1, 2)):
            wa = [fw.sbuf("wa%d" % i, [128, 8, 512], BF16, ph) for i in range(2)]

            def load(blk):
                sl = blk % 2
                fw.dma(POOL, "wa%d" % sl, lambda e: e.dma_start(
                    out=wa[sl][:], in_=w_ada[l, :, blk * 512:(blk + 1) * 512].rearrange("(k p) n -> p k n", p=128)), writes=[R("wa", sl)])
            load(b0)
            for blk in range(b0, b1):
                sl = blk % 2
                wR = R("wa", sl)
                if blk + 1 < b1:
                    load(blk + 1)
                b = bank([6, 7])
                pR = R("ps", b)

                def mm(e):
                    ins = None
                    for j in range(4):
                        for k in range(8):
                            ins = e.matmul(ps[b][:, j * 17:(j + 1) * 17], wa[sl][:, k, j * 128:(j + 1) * 128], scb[:, k, :],
                                           start=(k == 0), stop=(k == 7))
                    return ins
                fw.op(PE, mm, reads=[wR, R("scb")], writes=[pR])
                mc0 = blk * 4
                fw.op(DVE, lambda e: e.tensor_tensor(
                    out=mod[:, mc0:mc0 + 4, :], in0=ps[b][:, 0:68].rearrange("p (a n) -> p a n", a=4),
                    in1=b_ada_s[:, l * 72 + mc0:l * 72 + mc0 + 4].unsqueeze(2).broadcast_to([128, 4, 17]), op=ALU.add),
                    reads=[pR, cR], writes=[R("mod")])
                yield
            for s in derive:
                sc = mod[:, (s * 3 + 1) * 8:(s * 3 + 2) * 8, :]
                gt = mod[:, (s * 3 + 2) * 8:(s * 3 + 3) * 8, :]
                fw.op(DVE, lambda e: e.tensor_scalar(out=modA[:, s], in0=sc, scalar1=1.0, scalar2=32.0, op0=ALU.add, op1=ALU.mult),
                      reads=[R("mod")], writes=[R("modA")])
                fw.op(DVE, lambda e: e.tensor_tensor(
                    out=modA[:, s], in0=modA[:, s],
                    in1=normw_s[:, (l * 3 + s) * 8:(l * 3 + s + 1) * 8].unsqueeze(2).broadcast_to([128, 8, 17]), op=ALU.mult),
                    reads=[R("modA"), cR], writes=[R("modA")])
                res = 1.0 if s == 1 else 0.5
                fw.op(DVE, lambda e: e.tensor_scalar(out=modG[:, s], in0=gt, scalar1=1.0, scalar2=res, op0=ALU.add, op1=ALU.mult),
                      reads=[R("mod")], writes=[R("modG")])

        def ada(l):
            with contextlib.ExitStack() as ph:
                for _ in ada_gen(l, ph):
                    pass
                fw.barrier()

        def norm_to_h(ph_outer, Aview, Bview, dst_fn=None, keep=False):
            if keep:
                _norm_to_h(ph_outer, Aview, Bview, dst_fn)
                return
            with contextlib.ExitStack() as ph:
                _norm_to_h(ph, Aview, Bview, dst_fn)
                fw.barrier()

        def _norm_to_h(ph, Aview, Bview, dst_fn=None):
            sqb = [fw.sbuf("sqb%d" % i, [128, 512], BF16, ph) for i in range(4)]
            rinv = fw.sbuf("rinv", [128, T], F32, ph)
            ntmp = [fw.sbuf("ntmp%d" % i, [128, 512], F32, ph) for i in range(4)]
            stmp = fw.sbuf("stmp", [128, 8, 16], F32, ph)
            ctr = 0
            for t in range(5):
                cs, n = tcols(t)
                b = bank([6, 7])
                pR = R("ps", b)
                for c in range(8):
                    q = ctr % 4
                    ctr += 1
                    if c % 2 == 0:
                        fw.op(ACT, lambda e: e.activation(out=sqb[q][:, :n], in_=xT[:, c, cs], func=AF.Square),
                              reads=[R("x", c, t)], writes=[R("sqb", q)])
                    else:
                        fw.op(DVE, lambda e: e.tensor_tensor(out=sqb[q][:, :n], in0=xT[:, c, cs], in1=xT[:, c, cs], op=ALU.mult),
                              reads=[R("x", c, t)], writes=[R("sqb", q)])
                    fw.op(PE, lambda e: e.matmul(ps[b][:, :n], ones_b[:], sqb[q][:, :n], start=(c == 0), stop=(c == 7)),
                          reads=[R("sqb", q), cR], writes=[pR])
                fw.op(ACT, lambda e: e.activation(out=rinv[:, cs], in_=ps[b][:, :n], func=AF.Ln, bias=eps_t[:, 0:1], scale=1.0),
                      reads=[pR, cR], writes=[R("rinv")])
            fw.op(ACT, lambda e: e.activation(out=rinv[:], in_=rinv[:], func=AF.Exp, scale=-0.5), reads=[R("rinv")], writes=[R("rinv")])
            for t in range(5):
                cs, n = tcols(t)
                if t < 4:
                    for c in range(8):
                        q = ctr % 4
                        ctr += 1
                        fw.op(DVE, lambda e: e.scalar_tensor_tensor(
                            out=ntmp[q][:], in0=xT[:, c, cs], scalar=Aview[:, c, 0:1], in1=rinv[:, cs], op0=ALU.mult, op1=ALU.mult),
                            reads=[R("x", c, t), R("rinv"), R("modA"), cR], writes=[R("ntmp", q)])
                        if Bview is not None:
                            if c % 4 != 3:
                                fw.op(ACT, lambda e: e.activation(out=hT[:, c, cs], in_=ntmp[q][:], func=AF.Identity, bias=Bview[:, c, 0:1], scale=1.0),
                                      reads=[R("ntmp", q), R("mod"), R("stash")], writes=[R("h", c, t)])
                            else:
                                fw.op(DVE, lambda e: e.tensor_scalar(out=hT[:, c, cs], in0=ntmp[q][:], scalar1=Bview[:, c, 0:1], scalar2=None, op0=ALU.add),
                                      reads=[R("ntmp", q), R("mod"), R("stash")], writes=[R("h", c, t)])
                        else:
                            dst_fn(c, t, ntmp[q], R("ntmp", q))
                else:
                    fw.op(DVE, lambda e: e.tensor_tensor(out=stmp[:], in0=xT[:, :, TP:T], in1=rinv[:, TP:T].unsqueeze(1).broadcast_to([128, 8, 16]), op=ALU.mult),
                          reads=[R("x", c, 4) for c in range(8)] + [R("rinv")], writes=[R("stmp")])
                    if Bview is not None:
                        fw.op(DVE, lambda e: e.tensor_tensor(out=stmp[:], in0=stmp[:], in1=Aview[:, :, 1:17], op=ALU.mult),
                              reads=[R("stmp"), R("modA")], writes=[R("stmp")])
                        fw.op(DVE, lambda e: e.tensor_tensor(out=hT[:, :, TP:T], in0=stmp[:], in1=Bview[:, :, 1:17], op=ALU.add),
                              reads=[R("stmp"), R("mod")], writes=[R("h", c, 4) for c in range(8)])
                    else:
                        fw.op(DVE, lambda e: e.tensor_tensor(out=stmp[:], in0=stmp[:], in1=Aview[:, :, 0:1].broadcast_to([128, 8, 16]), op=ALU.mult),
                              reads=[R("stmp"), cR], writes=[R("stmp")])
                        dst_fn(None, 4, stmp, R("stmp"))

        def resid_update(ph_tmp, s, m, t, b, Gv=None, tkey="rtmp"):
            cs, n = tcols(t)
            pR = R("ps", b)
            if Gv is None:
                Gv = modG[:, s]
            if t < 4:
                fw.op(DVE, lambda e: e.scalar_tensor_tensor(out=xT[:, m, cs], in0=ps[b][:, :n], scalar=Gv[:, m, 0:1], in1=xT[:, m, cs],
                                                            op0=ALU.mult, op1=ALU.add),
                      reads=[pR, R("modG"), R("stash"), R("x", m, t)], writes=[R("x", m, t)])
            else:
                fw.op(DVE, lambda e: e.tensor_tensor(out=ph_tmp[:], in0=ps[b][:, :16], in1=Gv[:, m, 1:17], op=ALU.mult),
                      reads=[pR, R("modG"), R("stash")], writes=[R(tkey)])
                fw.op(DVE, lambda e: e.tensor_tensor(out=xT[:, m, cs], in0=xT[:, m, cs], in1=ph_tmp[:], op=ALU.add),
                      reads=[R(tkey), R("x", m, t)], writes=[R("x", m, t)])

        GROUPS = [(0, 6), (6, 6), (12, 6), (18, 4)]

        def ffn(l, f, s, next_ada=None, ada_rest=None):
            with contextlib.ExitStack() as ph:
                Av, Bv, Gv = modA[:, s], mod[:, (s * 3) * 8:(s * 3 + 1) * 8, :], modG[:, s]
                agen = None
                if next_ada is not None:
                    fw.op(DVE, lambda e: e.tensor_copy(stash[:, 0], Av), reads=[R("modA")], writes=[R("stash")])
                    fw.op(DVE, lambda e: e.tensor_copy(stash[:, 1], Bv), reads=[R("mod")], writes=[R("stash")])
                    fw.op(DVE, lambda e: e.tensor_copy(stash[:, 2], Gv), reads=[R("modG")], writes=[R("stash")])
                    Av, Bv, Gv = stash[:, 0], stash[:, 1], stash[:, 2]
                act = fw.sbuf("act", [128, 6, T], BF16, ph)
                wgu = [fw.sbuf("wgu%d" % i, [128, 2, 8, 256], BF16, ph) for i in range(3)]
                wd = [fw.sbuf("wd%d" % i, [128, 6, D], BF16, ph) for i in range(2)]
                sg = [fw.sbuf("sg%d" % i, [128, 512], F32, ph) for i in range(2)]
                rtmp = fw.sbuf("rtmp", [128, 16], F32, ph)
                PAIRS = [j0 + pj * 2 for (j0, gn) in GROUPS for pj in range(gn // 2)]

                def issue_pair(k):
                    if k >= len(PAIRS):
                        return
                    jj_, sl_ = PAIRS[k], k % 3
                    for gu in range(2):
                        c0 = gu * DFF + jj_ * 128
                        fw.dma(POOL, "wgu%d_%d" % (sl_, gu), lambda e: e.dma_start(
                            out=wgu[sl_][:, gu], in_=ffn_w_gu[l, f, :, c0:c0 + 256].rearrange("(k p) n -> p k n", p=128)),
                            writes=[R("wgu", sl_, gu)])

                def issue_wd(gi_):
                    if gi_ >= len(GROUPS):
                        return
                    j0_, gn_ = GROUPS[gi_]
                    ws_ = gi_ % 2
                    fw.dma(POOL, "wd%d" % ws_, lambda e: e.dma_start(
                        out=wd[ws_][:, 0:gn_, :], in_=ffn_w_down[l, f, j0_ * 128:(j0_ + gn_) * 128, :].rearrange("(j p) n -> p j n", p=128)),
                        writes=[R("wd", ws_)])
                issue_pair(0)
                issue_pair(1)
                issue_wd(0)
                norm_to_h(ph, Av, Bv, keep=(next_ada is None and ada_rest is None))
                if next_ada is not None:
                    agen = ada_gen(next_ada, ph)
                if ada_rest is not None:
                    agen = ada_gen(ada_rest, ph, 6, 18, (1, 2))
                pair_ctr = 0
                sgc = 0
                for gi, (j0, gn) in enumerate(GROUPS):
                    ws = gi % 2
                    issue_wd(gi + 1)
                    for pj in range(gn // 2):
                        jj = j0 + pj * 2
                        sl = pair_ctr % 3
                        issue_pair(pair_ctr + 2)
                        pair_ctr += 1
                        for jl in range(2):
                            ja = pj * 2 + jl
                            for t in range(5):
                                cs, n = tcols(t)
                                bg = bank([0, 1])
                                bu = bank([2, 3])

                                def mmg(e, gu, bnk, sl=sl, jl=jl, cs=cs, n=n):
                                    ins = None
                                    for k in range(8):
                                        ins = e.matmul(ps[bnk][:, :n], wgu[sl][:, gu, k, jl * 128:(jl + 1) * 128], hT[:, k, cs], start=(k == 0), stop=(k == 7))
                                    return ins
                                hr = [R("h", k, t) for k in range(8)]
                                fw.op(PE, lambda e, bg=bg: mmg(e, 0, bg), reads=hr + [R("wgu", sl, 0)], writes=[R("ps", bg)])
                                fw.op(PE, lambda e, bu=bu: mmg(e, 1, bu), reads=hr + [R("wgu", sl, 1)], writes=[R("ps", bu)])
                                q = sgc % 2
                                sgc += 1
                                fw.op(ACT, lambda e, q=q, bg=bg, n=n: e.activation(out=sg[q][:, :n], in_=ps[bg][:, :n], func=AF.Silu),
                                      reads=[R("ps", bg)], writes=[R("sg", q)])
                                fw.op(DVE, lambda e, q=q, bu=bu, n=n, ja=ja, cs=cs: e.tensor_tensor(out=act[:, ja, cs], in0=sg[q][:, :n], in1=ps[bu][:, :n], op=ALU.mult),
                                      reads=[R("sg", q), R("ps", bu)], writes=[R("act", ja, t)])
                    for m in range(8):
                        for t in range(5):
                            cs, n = tcols(t)
                            b = bank([4, 5])

                            def mmd(e, b=b, m=m, cs=cs, n=n, ws=ws, gn=gn):
                                ins = None
                                for ja in range(gn):
                                    ins = e.matmul(ps[b][:, :n], wd[ws][:, ja, m * 128:(m + 1) * 128], act[:, ja, cs], start=(ja == 0), stop=(ja == gn - 1))
                                return ins
                            fw.op(PE, mmd, reads=[R("act", ja, t) for ja in range(gn)] + [R("wd", ws)], writes=[R("ps", b)])
                            resid_update(rtmp, s, m, t, b, Gv)
                    if agen is not None:
                        for _ in range(5):
                            next(agen, None)
                if agen is not None:
                    for _ in agen:
                        pass
                fw.barrier()

        def mixer_b(l, li):
            s = 1
            with contextlib.ExitStack() as ph:
                norm_to_h(ph, modA[:, s], mod[:, (s * 3) * 8:(s * 3 + 1) * 8, :])
                wv1 = [fw.sbuf("wv1_%d" % i, [128, 8, 512], BF16, ph) for i in range(2)]
                gv = [fw.sbuf("gv%d" % i, [128, 512], F32, ph) for i in range(2)]
                ssum = fw.sbuf("ssum", [128, 17, 4], F32, ph)
                ssq = fw.sbuf("ssq", [128, 17, 4], F32, ph)
                mean = fw.sbuf("mean", [128, 17], F32, ph)
                rstd = fw.sbuf("rstd", [128, 17], F32, ph)
                nmr = fw.sbuf("nmr", [128, 17], F32, ph)
                vtmp = fw.sbuf("vtmp", [128, 17], F32, ph)
                fw.op(DVE, lambda e: e.memset(ssum[:], 0.0), writes=[R("ssum")])
                fw.op(DVE, lambda e: e.memset(ssq[:], 0.0), writes=[R("ssq")])
                gc = 0
                for vb in range(4):
                    sl = vb % 2
                    fw.dma(POOL, "wv1_%d" % sl, lambda e, sl=sl, vb=vb: e.dma_start(
                        out=wv1[sl][:], in_=b_w_in[li, :, 2048 + vb * 512:2048 + (vb + 1) * 512].rearrange("(k p) n -> p k n", p=128)),
                        writes=[R("wv1", sl)])
                    for ti in range(17):
                        c0 = ti * 128
                        np_ = 128 if ti < 16 else 16
                        t5 = min(ti // 4, 4)
                        b = bank([0, 1, 2, 3])

                        def mm(e, b=b, c0=c0, np_=np_, sl=sl):
                            ins = None
                            for k in range(8):
                                ins = e.matmul(ps[b][0:np_, :], hT[:, k, c0:c0 + np_], wv1[sl][:, k, :], start=(k == 0), stop=(k == 7))
                            return ins
                        fw.op(PE, mm, reads=[R("h", k, t5) for k in range(8)] + [R("wv1", sl)], writes=[R("ps", b)])
                        q = gc % 2
                        gc += 1
                        fw.op(ACT, lambda e, q=q, b=b, np_=np_, ti=ti, vb=vb: e.activation(
                            out=gv[q][0:np_, :], in_=ps[b][0:np_, :], func=AF.Gelu_apprx_tanh, accum_out=ssum[0:np_, ti, vb:vb + 1]),
                            reads=[R("ps", b)], writes=[R("gv", q), R("ssum")])
                        fw.op(ACT, lambda e, q=q, np_=np_, ti=ti, vb=vb: e.activation(
                            out=gv[q][0:np_, :], in_=gv[q][0:np_, :], func=AF.Square, accum_out=ssq[0:np_, ti, vb:vb + 1]),
                            reads=[R("gv", q)], writes=[R("gv", q), R("ssq")])
                stR = R("stats")
                fw.op(DVE, lambda e: e.tensor_reduce(out=mean[:], in_=ssum[:], axis=AX.X, op=ALU.add), reads=[R("ssum")], writes=[stR])
                fw.op(DVE, lambda e: e.tensor_reduce(out=rstd[:], in_=ssq[:], axis=AX.X, op=ALU.add), reads=[R("ssq")], writes=[stR])
                fw.op(DVE, lambda e: e.tensor_scalar(out=mean[:], in0=mean[:], scalar1=1.0 / 2048, scalar2=None, op0=ALU.mult), reads=[stR], writes=[stR])
                fw.op(DVE, lambda e: e.tensor_tensor(out=vtmp[:], in0=mean[:], in1=mean[:], op=ALU.mult), reads=[stR], writes=[stR])
                fw.op(DVE, lambda e: e.scalar_tensor_tensor(out=rstd[:], in0=rstd[:], scalar=1.0 / 2048, in1=vtmp[:], op0=ALU.mult, op1=ALU.subtract),
                      reads=[stR], writes=[stR])
                fw.op(ACT, lambda e: e.activation(out=rstd[:], in_=rstd[:], func=AF.Sqrt, bias=eps_t[:, 1:2], scale=1.0), reads=[stR, cR], writes=[stR])
                fw.op(DVE, lambda e: e.reciprocal(out=rstd[:], in_=rstd[:]), reads=[stR], writes=[stR])
                fw.op(DVE, lambda e: e.scalar_tensor_tensor(out=nmr[:], in0=mean[:], scalar=-1.0, in1=rstd[:], op0=ALU.mult, op1=ALU.mult),
                      reads=[stR], writes=[stR])
                wvg2 = [fw.sbuf("wvg%d" % i, [128, 8, 256], BF16, ph) for i in range(2)]
                wug2 = [fw.sbuf("wug%d" % i, [128, 8, 256], BF16, ph) for i in range(2)]
                wog2 = [fw.sbuf("wog%d" % i, [128, 2, D], BF16, ph) for i in range(2)]

                def load_group_w(g_):
                    if g_ >= 8:
                        return
                    q_ = g_ % 2
                    fw.dma(POOL, "wvg%d" % q_, lambda e: e.dma_start(out=wvg2[q_][:], in_=b_w_in[li, :, 2048 + g_ * 256:2048 + (g_ + 1) * 256].rearrange("(k p) n -> p k n", p=128)),
                           writes=[R("wvg", q_)])
                    fw.dma(POOL, "wug%d" % q_, lambda e: e.dma_start(out=wug2[q_][:], in_=b_w_in[li, :, g_ * 256:(g_ + 1) * 256].rearrange("(k p) n -> p k n", p=128)),
                           writes=[R("wug", q_)])

                def load_group_wo(g_):
                    if g_ >= 8:
                        return
                    q_ = g_ % 2
                    fw.dma(POOL, "wog%d" % q_, lambda e: e.dma_start(out=wog2[q_][:], in_=b_w_out[li, g_ * 256:(g_ + 1) * 256, :].rearrange("(c p) n -> p c n", p=128)),
                           writes=[R("wog", q_)])
                load_group_w(0)
                load_group_wo(0)
                load_group_wo(1)
                wsf = fw.sbuf("wsf", [128, 128], F32, ph)
                wsb = fw.sbuf("wsb", [128, 128], BF16, ph)
                mskU = fw.sbuf("mskU", [128, 128], F32, ph)
                lnw = fw.sbuf("lnw", [128, 256], F32, ph)
                lnb = fw.sbuf("lnb", [128, 256], F32, ph)
                bsbc = fw.sbuf("bsbc", [128, 128], F32, ph)
                lncol = fw.sbuf("lncol", [128, 2, 2], F32, ph)
                bias2 = fw.sbuf("bias2", [128, 2, 128], F32, ph)
                bs0 = fw.sbuf("bs0", [128, 8], F32, ph)
                ws00 = fw.sbuf("ws00", [128, 8], F32, ph)
                sb2 = fw.sbuf("sb2", [128, 2], F32, ph)
                dg = fw.sbuf("dg", [16, 16], BF16, ph)
                vnb = fw.sbuf("vnb", [128, 17, 256], BF16, ph)
                vsf = fw.sbuf("vsf", [16, 256], F32, ph)
                uT = fw.sbuf("uT", [128, 2, T], F32, ph)
                umb2 = [fw.sbuf("umb%d" % i, [128, 2, T], BF16, ph) for i in range(2)]
                mtmp = [fw.sbuf("mtmp%d" % i, [128, 512], F32, ph) for i in range(2)]
                rtmp = fw.sbuf("rtmpb", [128, 16], F32, ph)
                fw.op(POOL, lambda e: e.memset(mskU[:], 1.0), writes=[R("mskU")])
                fw.op(POOL, lambda e: e.affine_select(out=mskU[:], in_=mskU[:], pattern=[[1, 128]], compare_op=ALU.is_ge, fill=0.0, base=0,
                                                      channel_multiplier=-1), reads=[R("mskU")], writes=[R("mskU")])
                fw.dma(SP, "b0", lambda e: e.dma_start(out=bs0[:], in_=b_b_s0[li].partition_broadcast(128)), writes=[R("bs0")])
                fw.dma(SP, "b1", lambda e: e.dma_start(out=ws00[:], in_=b_w_s00[li].partition_broadcast(128)), writes=[R("ws00")])
                mc = 0
                for g in range(8):
                    load_group_w(g + 1)
                    wvg, wug, wog = wvg2[g % 2], wug2[g % 2], wog2[g % 2]
                    gq = g % 2
                    umb = umb2[gq]
                    fw.dma(SP, "b2", lambda e, g=g: e.dma_start(out=wsf[:], in_=b_w_sT[li, g]), writes=[R("wsf")])
                    fw.dma(SP, "b3", lambda e, g=g: e.dma_start(out=lnw[:], in_=b_vnorm_w[li, g * 256:(g + 1) * 256].partition_broadcast(128)), writes=[R("lnw")])
                    fw.dma(SP, "b4", lambda e, g=g: e.dma_start(out=lnb[:], in_=b_vnorm_b[li, g * 256:(g + 1) * 256].partition_broadcast(128)), writes=[R("lnb")])
                    fw.dma(SP, "b5", lambda e, g=g: e.dma_start(out=bsbc[:], in_=b_b_s[li, g].partition_broadcast(128)), writes=[R("bsbc")])
                    fw.op(DVE, lambda e: e.tensor_tensor(out=wsb[:], in0=wsf[:], in1=mskU[:], op=ALU.mult), reads=[R("wsf"), R("mskU")], writes=[R("wsb")])
                    for cc_ in range(2):
                        a0 = g * 256 + cc_ * 128
                        fw.dma(SP, "b6_%d" % cc_, lambda e: e.dma_start(out=lncol[:, cc_, 0:1], in_=b_vnorm_w[li, a0:a0 + 128].rearrange("(p o) -> p o", o=1)), writes=[R("lncol", cc_)])
                        fw.dma(SP, "b7_%d" % cc_, lambda e: e.dma_start(out=lncol[:, cc_, 1:2], in_=b_vnorm_b[li, a0:a0 + 128].rearrange("(p o) -> p o", o=1)), writes=[R("lncol", cc_)])
                    for cc_ in range(2):
                        fw.op(DVE, lambda e: e.scalar_tensor_tensor(out=sb2[:, cc_:cc_ + 1], in0=ws00[:, g:g + 1], scalar=lncol[:, cc_, 1:2], in1=bs0[:, g:g + 1], op0=ALU.mult, op1=ALU.add),
                              reads=[R("ws00"), R("lncol", cc_), R("bs0")], writes=[R("sb2")])
                    brs = bank([6, 7])
                    fw.op(PE, lambda e: e.matmul(ps[brs][:, 0:128], ones_b[:], wsb[:], start=True, stop=True), reads=[R("wsb"), cR], writes=[R("ps", brs)])
                    for cc_ in range(2):
                        fw.op(DVE, lambda e: e.scalar_tensor_tensor(out=bias2[:, cc_, :], in0=ps[brs][:, 0:128], scalar=lncol[:, cc_, 1:2], in1=bsbc[:], op0=ALU.mult, op1=ALU.add),
                              reads=[R("ps", brs), R("lncol", cc_), R("bsbc")], writes=[R("bias2", cc_)])
                    fw.op(DVE, lambda e, g=g: e.tensor_scalar(out=dg[:], in0=ident_f[0:16, 0:16], scalar1=ws00[0:16, g:g + 1], scalar2=None, op0=ALU.mult),
                          reads=[R("ws00"), cR], writes=[R("dg")])
                    for ti in range(17):
                        c0 = ti * 128
                        np_ = 128 if ti < 16 else 16
                        t5 = min(ti // 4, 4)
                        b = bank([0, 1])

                        def mm(e, b=b, c0=c0, np_=np_):
                            ins = None
                            for k in range(8):
                                ins = e.matmul(ps[b][0:np_, 0:256], hT[:, k, c0:c0 + np_], wvg[:, k, :], start=(k == 0), stop=(k == 7))
                            return ins
                        fw.op(PE, mm, reads=[R("h", k, t5) for k in range(8)] + [R("wvg", gq)], writes=[R("ps", b)])
                        q = gc % 2
                        gc += 1
                        fw.op(ACT, lambda e, q=q, b=b, np_=np_: e.activation(out=gv[q][0:np_, 0:256], in_=ps[b][0:np_, 0:256], func=AF.Gelu_apprx_tanh),
                              reads=[R("ps", b)], writes=[R("gv", q)])
                        if ti < 16:
                            fw.op(ACT, lambda e, q=q, ti=ti: e.activation(out=vnb[:, ti, :], in_=gv[q][:, 0:256], func=AF.Identity,
                                                                          scale=rstd[:, ti:ti + 1], bias=nmr[:, ti:ti + 1]),
                                  reads=[R("gv", q), stR], writes=[R("vnb", ti)])
                        else:
                            fw.op(DVE, lambda e, q=q, np_=np_, ti=ti: e.tensor_scalar(
                                out=gv[q][0:np_, 0:256], in0=gv[q][0:np_, 0:256], scalar1=rstd[0:np_, ti:ti + 1], scalar2=nmr[0:np_, ti:ti + 1],
                                op0=ALU.mult, op1=ALU.add), reads=[R("gv", q), stR], writes=[R("gv", q)])
                            fw.op(ACT, lambda e, q=q: e.copy(out=vnb[0:16, 16, :], in_=gv[q][0:16, 0:256]), reads=[R("gv", q)], writes=[R("vnb", 16)])
                            fw.op(DVE, lambda e, q=q, np_=np_: e.tensor_tensor(out=gv[q][0:np_, 0:256], in0=gv[q][0:np_, 0:256], in1=lnw[0:np_, :], op=ALU.mult),
                                  reads=[R("gv", q), R("lnw")], writes=[R("gv", q)])
                            fw.op(DVE, lambda e, q=q: e.tensor_tensor(out=vsf[:], in0=gv[q][0:16, 0:256], in1=lnb[0:16, :], op=ALU.add),
                                  reads=[R("gv", q), R("lnb")], writes=[R("vsf")])
                            fw.dma(SP, "vs", lambda e, g=g: e.dma_start(out=vS_out[li, :, g * 256:(g + 1) * 256], in_=vsf[:]), reads=[R("vsf")])
                    for cc in range(2):
                        for t in range(5):
                            cs, n = tcols(t)
                            b = bank([2, 3])

                            def mm(e, b=b, cc=cc, cs=cs, n=n):
                                ins = None
                                for k in range(8):
                                    ins = e.matmul(ps[b][:, :n], wug[:, k, cc * 128:(cc + 1) * 128], hT[:, k, cs], start=(k == 0), stop=(k == 7))
                                return ins
                            fw.op(PE, mm, reads=[R("h", k, t) for k in range(8)] + [R("wug", gq)], writes=[R("ps", b)])
                            fw.op(ACT, lambda e, b=b, cc=cc, cs=cs, n=n: e.activation(out=uT[:, cc, cs], in_=ps[b][:, :n], func=AF.Gelu_apprx_tanh),
                                  reads=[R("ps", b)], writes=[R("uT", cc, t)])
                    for cc in range(2):
                        for t in range(5):
                            cs, n = tcols(t)
                            b = bank([6, 7])
                            q = mc % 2
                            mc += 1
                            if t < 4:
                                def mm(e, b=b, cc=cc, t=t):
                                    ins = None
                                    for j in range(4):
                                        ins = e.matmul(ps[b][:, j * 128:(j + 1) * 128], vnb[:, t * 4 + j, cc * 128:(cc + 1) * 128], wsb[:], start=True, stop=True)
                                    return ins
                                fw.op(PE, mm, reads=[R("vnb", t * 4 + j) for j in range(4)] + [R("wsb")], writes=[R("ps", b)])
                                fw.op(DVE, lambda e, b=b, q=q, cc=cc: e.scalar_tensor_tensor(
                                    out=mtmp[q][:].rearrange("p (a n) -> p a n", a=4), in0=ps[b][:].rearrange("p (a n) -> p a n", a=4),
                                    scalar=lncol[:, cc, 0:1], in1=bias2[:, cc, :].unsqueeze(1).broadcast_to([128, 4, 128]), op0=ALU.mult, op1=ALU.add),
                                    reads=[R("ps", b), R("bias2", cc), R("lncol", cc)], writes=[R("mtmp", q)])
                            else:
                                fw.op(PE, lambda e, b=b, cc=cc: e.matmul(ps[b][:, 0:16], vnb[0:16, 16, cc * 128:(cc + 1) * 128], dg[:], start=True, stop=True),
                                      reads=[R("vnb", 16), R("dg")], writes=[R("ps", b)])
                                fw.op(DVE, lambda e, b=b, q=q, g=g, cc=cc: e.tensor_scalar(out=mtmp[q][:, 0:16], in0=ps[b][:, 0:16], scalar1=lncol[:, cc, 0:1], scalar2=sb2[:, cc:cc + 1],
                                                                                     op0=ALU.mult, op1=ALU.add),
                                      reads=[R("ps", b), R("sb2"), R("lncol", cc)], writes=[R("mtmp", q)])
                            fw.op(DVE, lambda e, q=q, cc=cc, cs=cs, n=n: e.tensor_tensor(out=umb[:, cc, cs], in0=mtmp[q][:, :n], in1=uT[:, cc, cs], op=ALU.mult),
                                  reads=[R("mtmp", q), R("uT", cc, t)], writes=[R("umb", gq, cc, t)])
                    if g % 2 == 1:
                        for m in range(8):
                            for t in range(5):
                                cs, n = tcols(t)
                                b = bank([4, 5])

                                def mm(e, b=b, m=m, cs=cs, n=n):
                                    ins = None
                                    for i4 in range(4):
                                        gg, cc = i4 // 2, i4 % 2
                                        ins = e.matmul(ps[b][:, :n], wog2[gg][:, cc, m * 128:(m + 1) * 128], umb2[gg][:, cc, cs], start=(i4 == 0), stop=(i4 == 3))
                                    return ins
                                fw.op(PE, mm, reads=[R("umb", gg, cc, t) for gg in range(2) for cc in range(2)] + [R("wog", 0), R("wog", 1)], writes=[R("ps", b)])
                                resid_update(rtmp, s, m, t, b)
                        load_group_wo(g + 1)
                        load_group_wo(g + 2)
                fw.barrier()

        def mixer_a(l, li):
            s = 1
            with contextlib.ExitStack() as ph:
                norm_to_h(ph, modA[:, s], mod[:, (s * 3) * 8:(s * 3 + 1) * 8, :])
                triU = fw.sbuf("triU", [128, 128], F32, ph)
                mposL = fw.sbuf("mposL", [128, 128], F32, ph)
                mnegU = fw.sbuf("mnegU", [128, 128], F32, ph)
                b32 = fw.sbuf("b32", [128, 128], BF16, ph)
                b1T = fw.sbuf("b1T", [128, 128], BF16, ph)
                b2 = fw.sbuf("b2", [128, 128], BF16, ph)
                cw = fw.sbuf("cw", [128, 2 * 4 * 24], F32, ph)
                won = fw.sbuf("won", [128, 2], F32, ph)
                alog = fw.sbuf("alog", [128, 8], F32, ph)
                dtb = fw.sbuf("dtb", [128, 8], F32, ph)
                mR = R("maskA")
                fw.dma(SP, "a0", lambda e: e.dma_start(out=cw[:], in_=a_w_conv_p), writes=[mR])
                fw.dma(SP, "a1", lambda e: e.dma_start(out=won[:], in_=a_w_onorm_p), writes=[mR])
                fw.dma(SP, "a2", lambda e: e.dma_start(out=alog[:], in_=a_log[li].partition_broadcast(128)), writes=[mR])
                fw.dma(SP, "a3", lambda e: e.dma_start(out=dtb[:], in_=a_dt_bias[li].partition_broadcast(128)), writes=[mR])
                fw.op(POOL, lambda e: e.memset(triU[:], 1.0), writes=[mR])
                fw.op(POOL, lambda e: e.affine_select(out=triU[:], in_=triU[:], pattern=[[1, 128]], compare_op=ALU.is_ge, fill=0.0, base=0,
                                                      channel_multiplier=-1), reads=[mR], writes=[mR])
                fw.op(POOL, lambda e: e.memset(mposL[:], 0.0), writes=[mR])
                fw.op(POOL, lambda e: e.affine_select(out=mposL[:], in_=mposL[:], pattern=[[-1, 128]], compare_op=ALU.is_gt, fill=BIG, base=0,
                                                      channel_multiplier=1), reads=[mR], writes=[mR])
                fw.op(POOL, lambda e: e.memset(mnegU[:], 0.0), writes=[mR])
                fw.op(POOL, lambda e: e.affine_select(out=mnegU[:], in_=mnegU[:], pattern=[[1, 128]], compare_op=ALU.is_ge, fill=-BIG, base=0,
                                                      channel_multiplier=-1), reads=[mR], writes=[mR])
                fw.op(POOL, lambda e: e.memset(b32[:], 0.0), writes=[mR])
                fw.op(POOL, lambda e: e.memset(b1T[:], 0.0), writes=[mR])
                fw.op(POOL, lambda e: e.memset(b2[:], 0.0), writes=[mR])
                fw.op(POOL, lambda e: e.memset(b32[0:32, 0:32], 1.0), reads=[mR], writes=[mR])
                fw.op(POOL, lambda e: e.memset(b32[32:64, 32:64], 1.0), reads=[mR], writes=[mR])
                fw.op(POOL, lambda e: e.memset(b32[64:128, 96:128], 1.0), reads=[mR], writes=[mR])
                fw.op(POOL, lambda e: e.memset(b32[64:96, 96:128], 0.0), reads=[mR], writes=[mR])
                fw.op(POOL, lambda e: e.memset(b32[64:96, 64:96], 1.0), reads=[mR], writes=[mR])
                for i in range(2):
                    fw.op(POOL, lambda e, i=i: e.memset(b1T[i * 64:i * 64 + 32, i * 64 + 32:i * 64 + 64], 1.0), reads=[mR], writes=[mR])
                fw.op(POOL, lambda e: e.memset(b2[64:128, 0:64], 1.0), reads=[mR], writes=[mR])

                wl = fw.sbuf("wl", [128, 8, 16], BF16, ph)
                lg = fw.sbuf("lg", [128, 17, 16], F32, ph)
                beta_t = fw.sbuf("beta_t", [128, 17, 8], F32, ph)
                g_t = fw.sbuf("g_t", [128, 17, 8], F32, ph)
                d_t = fw.sbuf("d_t", [128, 17, 8], F32, ph)
                dpl_t = fw.sbuf("dpl_t", [128, 17, 8], F32, ph)
                ebd_t = fw.sbuf("ebd_t", [128, 17, 8], F32, ph)
                ekd_t = fw.sbuf("ekd_t", [128, 17, 8], F32, ph)
                edl_t = fw.sbuf("edl_t", [128, 16, 8], F32, ph)
                nega = fw.sbuf("nega", [128, 8], F32, ph)
                ltmp = fw.sbuf("ltmp", [128, 17, 8], F32, ph)
                fw.op(DVE, lambda e: e.memset(lg[:], 0.0), writes=[R("lg")])
                fw.dma(POOL, "wl", lambda e: e.dma_start(out=wl[:], in_=a_w_in[li, :, 4096:4112].rearrange("(k p) n -> p k n", p=128)), writes=[R("wl")])
                for ti in range(17):
                    c0 = ti * 128
                    np_ = 128 if ti < 16 else 16
                    t5 = min(ti // 4, 4)
                    b = bank([0, 1])

                    def mm(e, b=b, c0=c0, np_=np_):
                        ins = None
                        for k in range(8):
                            ins = e.matmul(ps[b][0:np_, 0:16], hT[:, k, c0:c0 + np_], wl[:, k, :], start=(k == 0), stop=(k == 7))
                        return ins
                    fw.op(PE, mm, reads=[R("h", k, t5) for k in range(8)] + [R("wl")], writes=[R("ps", b)])
                    fw.op(ACT, lambda e, b=b, np_=np_, ti=ti: e.copy(out=lg[0:np_, ti, :], in_=ps[b][0:np_, 0:16]), reads=[R("ps", b)], writes=[R("lg")])
                gR = R("gates")
                fw.op(ACT, lambda e: e.activation(out=beta_t[:], in_=lg[:, :, 0:8], func=AF.Sigmoid), reads=[R("lg")], writes=[gR])
                fw.op(ACT, lambda e: e.activation(out=nega[:], in_=alog[:], func=AF.Exp), reads=[mR], writes=[gR])
                fw.op(DVE, lambda e: e.tensor_tensor(out=ltmp[:], in0=lg[:, :, 8:16], in1=dtb[:].unsqueeze(1).broadcast_to([128, 17, 8]), op=ALU.add),
                      reads=[R("lg"), mR], writes=[gR])
                fw.op(ACT, lambda e: e.activation(out=ltmp[:], in_=ltmp[:], func=AF.Exp), reads=[gR], writes=[gR])
                fw.op(ACT, lambda e: e.activation(out=ltmp[:], in_=ltmp[:], func=AF.Ln, bias=eps_t[:, 3:4], scale=1.0), reads=[gR, cR], writes=[gR])
                fw.op(DVE, lambda e: e.scalar_tensor_tensor(out=g_t[:], in0=ltmp[:], scalar=-1.0, in1=nega[:].unsqueeze(1).broadcast_to([128, 17, 8]),
                                                            op0=ALU.mult, op1=ALU.mult), reads=[gR], writes=[gR])
                bq = bank([2, 3])
                fw.op(PE, lambda e: e.matmul(ps[bq][:, 0:128], triU[:], g_t[:, 0:16, :].rearrange("p a b -> p (a b)"), start=True, stop=True),
                      reads=[gR, mR], writes=[R("ps", bq)])
                fw.op(DVE, lambda e: e.tensor_copy(d_t[:, 0:16, :].rearrange("p a b -> p (a b)"), ps[bq][:, 0:128]), reads=[R("ps", bq)], writes=[gR])
                fw.op(DVE, lambda e: e.tensor_copy(d_t[:, 16, :], g_t[:, 16, :]), reads=[gR], writes=[gR])
                bq2 = bank([2, 3])
                fw.op(PE, lambda e: e.matmul(ps[bq2][:, 0:128], ones_f[:], g_t[:, 0:16, :].rearrange("p a b -> p (a b)"), start=True, stop=True),
                      reads=[gR, cR], writes=[R("ps", bq2)])
                fw.op(ACT, lambda e: e.activation(out=edl_t[:].rearrange("p a b -> p (a b)"), in_=ps[bq2][:, 0:128], func=AF.Exp), reads=[R("ps", bq2)], writes=[gR])
                fw.op(DVE, lambda e: e.tensor_tensor(out=ekd_t[:, 0:16, :].rearrange("p a b -> p (a b)"), in0=ps[bq2][:, 0:128],
                                                     in1=d_t[:, 0:16, :].rearrange("p a b -> p (a b)"), op=ALU.subtract), reads=[R("ps", bq2), gR], writes=[gR])
                fw.op(ACT, lambda e: e.activation(out=ekd_t[:, 0:16, :], in_=ekd_t[:, 0:16, :], func=AF.Exp), reads=[gR], writes=[gR])
                fw.op(ACT, lambda e: e.activation(out=dpl_t[:], in_=beta_t[:], func=AF.Ln), reads=[gR], writes=[gR])
                fw.op(DVE, lambda e: e.tensor_tensor(out=dpl_t[:], in0=dpl_t[:], in1=d_t[:], op=ALU.add), reads=[gR], writes=[gR])
                fw.op(ACT, lambda e: e.activation(out=ebd_t[:], in_=d_t[:], func=AF.Exp), reads=[gR], writes=[gR])
                fw.op(DVE, lambda e: e.scalar_tensor_tensor(out=ebd_t[:], in0=ebd_t[:], scalar=-1.0, in1=beta_t[:], op0=ALU.mult, op1=ALU.mult), reads=[gR], writes=[gR])
                svals = fw.sbuf("svals", [16, 8, 2], F32, ph)
                sdg = fw.sbuf("sdg", [16, 16, 16], F32, ph)
                sbc = fw.sbuf("sbc", [128, 8, 2, 16], F32, ph)
                fw.op(DVE, lambda e: e.tensor_copy(svals[:, :, 0], beta_t[0:16, 16, :]), reads=[gR], writes=[R("svals")])
                fw.op(ACT, lambda e: e.activation(out=svals[:, :, 1], in_=g_t[0:16, 16, :], func=AF.Exp), reads=[gR], writes=[R("svals")])
                fw.op(DVE, lambda e: e.tensor_tensor(out=sdg[:], in0=ident_f[0:16, 0:16].unsqueeze(1).broadcast_to([16, 16, 16]),
                                                     in1=svals[:].rearrange("p a b -> p (a b)").unsqueeze(2).broadcast_to([16, 16, 16]), op=ALU.mult),
                      reads=[R("svals"), cR], writes=[R("sdg")])
                bq3 = bank([2, 3])
                fw.op(PE, lambda e: e.matmul(ps[bq3][:, 0:256], ones_f[0:16, :], sdg[:].rearrange("p a b -> p (a b)"), start=True, stop=True),
                      reads=[R("sdg"), cR], writes=[R("ps", bq3)])
                fw.op(DVE, lambda e: e.tensor_copy(sbc[:].rearrange("p a b c -> p (a b c)"), ps[bq3][:, 0:256]), reads=[R("ps", bq3)], writes=[R("sbc")])

                wq = fw.sbuf("wq", [128, 3, 8, 128], BF16, ph)
                wo2 = [fw.sbuf("wo%d" % i, [128, D], BF16, ph) for i in range(2)]
                wz2 = [fw.sbuf("wz%d" % i, [128, 8, 128], BF16, ph) for i in range(2)]
                preb = [fw.sbuf("preb%d" % i, [128, 516], F32, ph) for i in range(3)]
                cvb = [fw.sbuf("cvb%d" % i, [128, 512], F32, ph) for i in range(3)]
                carry = [fw.sbuf("carry%d" % i, [128, 3], F32, ph) for i in range(3)]
                pre_s3 = [fw.sbuf("pre_s%d" % i, [128, 16], F32, ph) for i in range(3)]
                cvs3 = [fw.sbuf("cvs%d" % i, [128, 16], F32, ph) for i in range(3)]
                sgs3 = [fw.sbuf("sgs%d" % i, [128, 16], F32, ph) for i in range(3)]
                sqs3 = [fw.sbuf("sqs%d" % i, [128, 16], BF16, ph) for i in range(3)]
                cbuf3 = [fw.sbuf("cbuf%d" % i, [128, 16, 3], F32, ph) for i in range(3)]
                nbuf3 = [fw.sbuf("nbuf%d" % i, [128, 16, 3], F32, ph) for i in range(3)]
                vs_f = fw.sbuf("vs_f", [128, 16], F32, ph)
                qT = fw.sbuf("qT", [128, T], BF16, ph)
                kT = fw.sbuf("kT", [128, T], BF16, ph)
                vTb = fw.sbuf("vTb", [128, TP], BF16, ph)
                rn = fw.sbuf("rn", [128, 512], F32, ph)
                qd = {}
                for nm in ["m", "mT", "X", "XT", "m1T", "m2", "P2", "PT2", "PT8", "Ra", "Rb", "T0"]:
                    qd[nm] = fw.sbuf("q_" + nm, [128, 4, 128], BF16, ph)
                QALIAS = {"P": "m", "PT4": "mT", "T0T": "X", "Y": "P2", "T1": "PT8", "T1T": "PT2", "Yp": "Ra"}
                for a_, b_ in QALIAS.items():
                    qd[a_] = qd[b_]
                Ef = [fw.sbuf("Ef%d" % i, [128, 4, 128], F32, ph) for i in range(3)]
                ring = {}
                for nm in ["T2T", "nwT", "attnT", "qeT", "vbt", "kbe", "kd"]:
                    ring[nm] = [fw.sbuf("r_%s%d" % (nm, i), [128, 4, 128], BF16, ph) for i in range(2)]
                oq = [fw.sbuf("oq%d" % i, [128, 512], F32, ph) for i in range(1)]
                o_s = fw.sbuf("o_s", [128, 16], F32, ph)
                Sf = fw.sbuf("Sf", [128, 128], F32, ph)
                Sb = fw.sbuf("Sb", [128, 128], BF16, ph)
                vnb = [fw.sbuf("vnbA%d" % i, [128, 128], BF16, ph) for i in range(2)]
                Ss = fw.sbuf("Ss", [128, 8, 128], F32, ph)
                kqf = fw.sbuf("kqf", [128, 16, 2], F32, ph)
                stmp = [fw.sbuf("sA%d" % i, [128, 16], F32, ph) for i in range(4)]
                ktok = fw.sbuf("ktok", [16, 128], F32, ph)
                vtok = fw.sbuf("vtok", [16, 128], F32, ph)
                kmask = [fw.sbuf("kmask%d" % i, [16, 128], F32, ph) for i in range(1)]
                szq = [fw.sbuf("szq%d" % i, [128, 512], F32, ph) for i in range(1)]
                ogb = [fw.sbuf("ogb%d" % i, [128, 512], BF16, ph) for i in range(1)]
                rtmp = fw.sbuf("rtmpa", [128, 16], F32, ph)
                rtmp_s = fw.sbuf("rtmps", [128, 16], F32, ph)
                sqb_s = fw.sbuf("sqb_s", [128, 16], BF16, ph)
                rn_s = fw.sbuf("rn_s", [128, 16], F32, ph)
                szq_s = fw.sbuf("szq_s", [128, 16], F32, ph)
                ogb_s = fw.sbuf("ogb_s", [128, 16], BF16, ph)
                BK_CH = [0, 1, 2]
                BK_PJ = [3]
                BK_AUX = [4, 5]
                BK_SEQ = [6, 7]
                evc = [0]

                def evac_copy(dst_ap, src_ap, reads, writes):
                    evc[0] += 1
                    if evc[0] % 4 != 0:
                        fw.op(ACT, lambda e: e.copy(out=dst_ap, in_=src_ap), reads=reads, writes=writes)
                    else:
                        fw.op(DVE, lambda e: e.tensor_copy(dst_ap, src_ap), reads=reads, writes=writes)

                def make_head(hd):
                    def load_qkv(h_):
                        for fi in range(3):
                            c0 = fi * 1024 + h_ * 128
                            fw.dma(POOL, "wq%d" % fi, lambda e: e.dma_start(
                                out=wq[:, fi], in_=a_w_in[li, :, c0:c0 + 128].rearrange("(k p) n -> p k n", p=128)), writes=[R("wq", fi)])

                    def load_zo(h_):
                        c0 = 3 * 1024 + h_ * 128
                        fw.dma(POOL, "wz%d" % (h_ % 2), lambda e: e.dma_start(
                            out=wz2[h_ % 2][:], in_=a_w_in[li, :, c0:c0 + 128].rearrange("(k p) n -> p k n", p=128)), writes=[R("wz", h_ % 2)])
                        fw.dma(POOL, "wo%d" % (h_ % 2), lambda e: e.dma_start(out=wo2[h_ % 2][:], in_=a_w_out[li, h_ * 128:(h_ + 1) * 128, :]), writes=[R("wo", h_ % 2)])
                    wz, wo = wz2[hd % 2], wo2[hd % 2]
                    wzR, woR = R("wz", hd % 2), R("wo", hd % 2)
                    BK_P = [0, 1, 2, 3, 4, 5, 6, 7]

                    def sigm(buf_ap, src_ap, rd, wr):
                        fw.op(ACT, lambda e: e.activation(out=buf_ap, in_=src_ap, func=AF.Exp, scale=-1.0), reads=rd, writes=wr)
                        yield
                        fw.op(ACT, lambda e: e.activation(out=buf_ap, in_=buf_ap, func=AF.Ln, bias=eps_t[:, 3:4], scale=1.0), reads=wr + [cR], writes=wr)
                        yield
                        fw.op(ACT, lambda e: e.activation(out=buf_ap, in_=buf_ap, func=AF.Exp, scale=-1.0), reads=wr, writes=wr)
                        yield

                    def pjob(hj, t, fi, slot, BKG):
                        cs = slice(t * 512, (t + 1) * 512)
                        ch = fi * 8 + hj
                        wbase = (li * 4) * 24 + ch
                        wc = [cw[:, wbase + j * 24:wbase + j * 24 + 1] for j in range(4)]
                        pb, cb_ = preb[slot], cvb[slot]
                        pR_, cR_ = R("preb", slot), R("cvb", slot)
                        b = yield from acq_wait(BKG)

                        for k2 in range(4):
                            def mm(e):
                                e.matmul(ps[b][:, :], wq[:, fi, 2 * k2, :], hT[:, 2 * k2, cs], start=(k2 == 0), stop=False)
                                return e.matmul(ps[b][:, :], wq[:, fi, 2 * k2 + 1, :], hT[:, 2 * k2 + 1, cs], start=False, stop=(k2 == 3))
                            fw.op(PE, mm, reads=[R("h", 2 * k2, t), R("h", 2 * k2 + 1, t), R("wq", fi)], writes=[R("ps", b)])
                            yield
                        fw.op(ACT, lambda e: e.copy(out=pb[:, 3:515], in_=ps[b][:, :]), reads=[R("ps", b)], writes=[pR_])
                        brel(b)
                        if t == 0:
                            fw.op(ACT, lambda e: e.activation(out=pb[:, 0:3], in_=pb[:, 3:6], func=AF.Copy, scale=0.0), reads=[pR_], writes=[pR_])
                        else:
                            fw.op(ACT, lambda e: e.copy(out=pb[:, 0:3], in_=carry[fi][:]), reads=[R("carry", fi)], writes=[pR_])
                        fw.op(ACT, lambda e: e.copy(out=carry[fi][:], in_=pb[:, 512:515]), reads=[pR_], writes=[R("carry", fi)])
                        if t == 3:
                            fw.dma(SP, "cpo%d" % fi, lambda e: e.dma_start(out=convP_out[li, ch * 128:(ch + 1) * 128, :], in_=carry[fi][:]), reads=[R("carry", fi)])
                        yield
                        fw.op(DVE, lambda e: e.tensor_scalar(out=cb_[:], in0=pb[:, 3:515], scalar1=wc[3], scalar2=None, op0=ALU.mult), reads=[pR_, mR], writes=[cR_])
                        for j in range(3):
                            yield
                            fw.op(DVE, lambda e: e.scalar_tensor_tensor(out=cb_[:], in0=pb[:, j:j + 512], scalar=wc[j], in1=cb_[:], op0=ALU.mult, op1=ALU.add),
                                  reads=[pR_, cR_, mR], writes=[cR_])
                        yield
                        yield from sigm(pb[:, 0:512], cb_[:], [cR_, pR_], [pR_])
                        fw.op(DVE, lambda e: e.tensor_tensor(out=cb_[:], in0=cb_[:], in1=pb[:, 0:512], op=ALU.mult), reads=[cR_, pR_], writes=[cR_])
                        yield
                        if fi == 2:
                            fw.op(ACT, lambda e: e.copy(out=vTb[:, cs], in_=cb_[:]), reads=[cR_], writes=[R("vTb", t)])
                            return
                        dst = qT if fi == 0 else kT
                        sqv = pb[:, 0:256].bitcast(BF16)
                        fw.op(ACT, lambda e: e.activation(out=sqv, in_=cb_[:], func=AF.Square), reads=[cR_, pR_], writes=[pR_])
                        b2_ = yield from acq_wait(BKG)
                        fw.op(PE, lambda e: e.matmul(ps[b2_][:, :], ones_b[:], sqv, start=True, stop=True), reads=[pR_, cR], writes=[R("ps", b2_)])
                        yield
                        if fi == 0:
                            fw.op(ACT, lambda e: e.activation(out=pb[:, 0:512], in_=ps[b2_][:, :], func=AF.Ln, bias=eps_t[:, 2:3], scale=128.0),
                                  reads=[R("ps", b2_), cR, pR_], writes=[pR_])
                        else:
                            fw.op(ACT, lambda e: e.activation(out=pb[:, 0:512], in_=ps[b2_][:, :], func=AF.Ln, bias=eps_t[:, 1:2], scale=1.0),
                                  reads=[R("ps", b2_), cR, pR_], writes=[pR_])
                        brel(b2_)
                        fw.op(ACT, lambda e: e.activation(out=pb[:, 0:512], in_=pb[:, 0:512], func=AF.Exp, scale=-0.5), reads=[pR_], writes=[pR_])
                        fw.op(DVE, lambda e: e.tensor_tensor(out=dst[:, cs], in0=cb_[:], in1=pb[:, 0:512], op=ALU.mult), reads=[cR_, pR_], writes=[R("qk", fi, t)])

                    def sjob(hj, fi, BKG):
                        ch = fi * 8 + hj
                        wbase = (li * 4) * 24 + ch
                        wc = [cw[:, wbase + j * 24:wbase + j * 24 + 1] for j in range(4)]
                        ps_, cs_, sg_s = pre_s3[fi], cvs3[fi], sgs3[fi]
                        KR_ = R("sjob", fi)
                        fw.dma(SP, "cb%d" % fi, lambda e: e.dma_start(out=cbuf3[fi][:], in_=convT_in[li, ch * 128:(ch + 1) * 128]), writes=[R("cbuf", fi)])
                        b = yield from acq_wait(BKG)

                        def mm(e):
                            ins = None
                            for k in range(8):
                                ins = e.matmul(ps[b][:, 0:16], wq[:, fi, k, :], hT[:, k, TP:T], start=(k == 0), stop=(k == 7))
                            return ins
                        fw.op(PE, mm, reads=[R("h", k, 4) for k in range(8)] + [R("wq", fi)], writes=[R("ps", b)])
                        yield
                        fw.op(ACT, lambda e: e.copy(out=ps_[:], in_=ps[b][:, 0:16]), reads=[R("ps", b)], writes=[KR_])
                        brel(b)
                        fw.op(DVE, lambda e: e.tensor_copy(nbuf3[fi][:, :, 0:2], cbuf3[fi][:, :, 1:3]), reads=[R("cbuf", fi)], writes=[R("nbuf", fi)])
                        fw.op(DVE, lambda e: e.tensor_copy(nbuf3[fi][:, :, 2], ps_[:]), reads=[KR_], writes=[R("nbuf", fi)])
                        fw.dma(SP, "cso%d" % fi, lambda e: e.dma_start(out=convS_out[li, ch * 128:(ch + 1) * 128], in_=nbuf3[fi][:]), reads=[R("nbuf", fi)])
                        fw.op(DVE, lambda e: e.tensor_scalar(out=cs_[:], in0=ps_[:], scalar1=wc[3], scalar2=None, op0=ALU.mult), reads=[KR_, mR], writes=[KR_])
                        for j in range(3):
                            fw.op(DVE, lambda e: e.scalar_tensor_tensor(out=cs_[:], in0=cbuf3[fi][:, :, j], scalar=wc[j], in1=cs_[:], op0=ALU.mult, op1=ALU.add),
                                  reads=[R("cbuf", fi), KR_, mR], writes=[KR_])
                        yield from sigm(sg_s[:], cs_[:], [KR_], [R("sjs", fi)])
                        if fi == 2:
                            fw.op(DVE, lambda e: e.tensor_tensor(out=vs_f[:], in0=cs_[:], in1=sg_s[:], op=ALU.mult), reads=[KR_, R("sjs", fi)], writes=[R("vs_f")])
                            return
                        fw.op(DVE, lambda e: e.tensor_tensor(out=cs_[:], in0=cs_[:], in1=sg_s[:], op=ALU.mult), reads=[KR_, R("sjs", fi)], writes=[KR_])
                        yield
                        dst = qT if fi == 0 else kT
                        fw.op(ACT, lambda e: e.activation(out=sqs3[fi][:], in_=cs_[:], func=AF.Square), reads=[KR_], writes=[R("sqs", fi)])
                        b2_ = yield from acq_wait(BKG)
                        fw.op(PE, lambda e: e.matmul(ps[b2_][:, 0:16], ones_b[:], sqs3[fi][:], start=True, stop=True), reads=[R("sqs", fi), cR], writes=[R("ps", b2_)])
                        yield
                        if fi == 0:
                            fw.op(ACT, lambda e: e.activation(out=sg_s[:], in_=ps[b2_][:, 0:16], func=AF.Ln, bias=eps_t[:, 2:3], scale=128.0),
                                  reads=[R("ps", b2_), cR], writes=[R("sjs", fi)])
                        else:
                            fw.op(ACT, lambda e: e.activation(out=sg_s[:], in_=ps[b2_][:, 0:16], func=AF.Ln, bias=eps_t[:, 1:2], scale=1.0),
                                  reads=[R("ps", b2_), cR], writes=[R("sjs", fi)])
                        brel(b2_)
                        fw.op(ACT, lambda e: e.activation(out=sg_s[:], in_=sg_s[:], func=AF.Exp, scale=-0.5), reads=[R("sjs", fi)], writes=[R("sjs", fi)])
                        col = 1 if fi == 0 else 0
                        fw.op(DVE, lambda e: e.tensor_tensor(out=kqf[:, :, col], in0=cs_[:], in1=sg_s[:], op=ALU.mult), reads=[KR_, R("sjs", fi)], writes=[R("kqf")])
                        fw.op(DVE, lambda e: e.tensor_tensor(out=dst[:, TP:T], in0=cs_[:], in1=sg_s[:], op=ALU.mult), reads=[KR_, R("sjs", fi)], writes=[R("qk", fi, 4)])

                    def half_chain(qi, hf):
                        rs = qi % 2
                        j0 = hf * 2
                        t0 = qi * 4 + j0
                        Rq = lambda nm: R("qd", QALIAS.get(nm, nm), hf)
                        Rr = lambda nm: R("ring", nm, rs, hf)
                        RE = lambda i: R("Ef", i, hf)
                        qv = lambda nm: qd[nm][:, j0:j0 + 2, :]
                        rv = lambda nm: ring[nm][rs][:, j0:j0 + 2, :]
                        ev = lambda i: Ef[i][:, j0:j0 + 2, :]
                        tcol = lambda j: slice((t0 + j) * 128, (t0 + j + 1) * 128)

                        def h_mm(lhs_fn, rhs_fn, reads):
                            b = bacq(BK_CH)

                            def mm(e):
                                ins = None
                                for j in range(2):
                                    ins = e.matmul(ps[b][:, j * 128:(j + 1) * 128], lhs_fn(j), rhs_fn(j), start=True, stop=True)
                                return ins
                            fw.op(PE, mm, reads=reads, writes=[R("ps", b)])
                            return b

                        def h_tr(src, reads):
                            b = bacq(BK_CH)
                            pb = ps[b][:].bitcast(BF16)

                            def mm(e):
                                ins = None
                                for j in range(2):
                                    ins = e.transpose(pb[:, j * 128:(j + 1) * 128], src(j), ident_b[:])
                                return ins
                            fw.op(PE, mm, reads=reads + [cR], writes=[R("ps", b)])
                            return b, pb[:, 0:256].rearrange("p (a n) -> p a n", a=2)

                        def p2(b):
                            return ps[b][:, 0:256].rearrange("p (a n) -> p a n", a=2)
                        bc2 = lambda tl: tl[:].unsqueeze(1).broadcast_to([128, 2, 128])

                        bt, pv = h_tr(lambda j: vTb[:, tcol(j)], [R("vTb", qi)])
                        yield
                        fw.op(DVE, lambda e: e.tensor_tensor(out=rv("vbt"), in0=pv, in1=beta_t[:, t0:t0 + 2, hd:hd + 1].broadcast_to([128, 2, 128]), op=ALU.mult),
                              reads=[R("ps", bt), gR], writes=[Rr("vbt")])
                        brel(bt)
                        bt, pk = h_tr(lambda j: kT[:, tcol(j)], [R("qk", 1, qi)])
                        yield
                        fw.op(DVE, lambda e: e.tensor_tensor(out=rv("kbe"), in0=pk, in1=ebd_t[:, t0:t0 + 2, hd:hd + 1].broadcast_to([128, 2, 128]), op=ALU.mult),
                              reads=[R("ps", bt), gR], writes=[Rr("kbe")])
                        fw.op(DVE, lambda e: e.tensor_tensor(out=rv("kd"), in0=pk, in1=ekd_t[:, t0:t0 + 2, hd:hd + 1].broadcast_to([128, 2, 128]), op=ALU.mult),
                              reads=[R("ps", bt), gR], writes=[Rr("kd")])
                        brel(bt)
                        bd = h_mm(lambda j: g_t[:, t0 + j, hd:hd + 1].broadcast_to([128, 128]), lambda j: triU[:], [gR, mR])
                        yield
                        for j in range(2):
                            fw.op(DVE, lambda e, j=j: e.scalar_tensor_tensor(out=Ef[0][:, j0 + j, :], in0=ps[bd][:, j * 128:(j + 1) * 128], scalar=dpl_t[:, t0 + j, hd:hd + 1],
                                                                             in1=mposL[:], op0=ALU.subtract, op1=ALU.add),
                                  reads=[R("ps", bd), gR, mR], writes=[RE(0)])
                        fw.op(ACT, lambda e: e.activation(out=ev(0), in_=ev(0), func=AF.Exp, scale=-1.0), reads=[RE(0)], writes=[RE(0)])
                        for j in range(2):
                            fw.op(DVE, lambda e, j=j: e.scalar_tensor_tensor(out=Ef[1][:, j0 + j, :], in0=ps[bd][:, j * 128:(j + 1) * 128], scalar=d_t[:, t0 + j, hd:hd + 1],
                                                                             in1=mnegU[:], op0=ALU.subtract, op1=ALU.add),
                                  reads=[R("ps", bd), gR, mR], writes=[RE(1)])
                        fw.op(ACT, lambda e: e.activation(out=ev(1), in_=ev(1), func=AF.Exp), reads=[RE(1)], writes=[RE(1)])
                        fw.op(ACT, lambda e: e.activation(out=ev(2), in_=p2(bd), func=AF.Exp), reads=[R("ps", bd)], writes=[RE(2)])
                        brel(bd)
                        fw.op(DVE, lambda e: e.tensor_tensor(out=rv("qeT"), in0=qT[:, t0 * 128:(t0 + 2) * 128].rearrange("p (a n) -> p a n", a=2), in1=ev(2), op=ALU.mult),
                              reads=[R("qk", 0, qi), RE(2)], writes=[Rr("qeT")])
                        bkk = h_mm(lambda j: kT[:, tcol(j)], lambda j: kT[:, tcol(j)], [R("qk", 1, qi)])
                        yield
                        fw.op(DVE, lambda e: e.tensor_tensor(out=qv("m"), in0=p2(bkk), in1=ev(0), op=ALU.mult), reads=[R("ps", bkk), RE(0)], writes=[Rq("m")])
                        brel(bkk)
                        bqk = h_mm(lambda j: kT[:, tcol(j)], lambda j: qT[:, tcol(j)], [R("qk", 1, qi), R("qk", 0, qi)])
                        yield
                        fw.op(DVE, lambda e: e.tensor_tensor(out=rv("attnT"), in0=p2(bqk), in1=ev(1), op=ALU.mult),
                              reads=[R("ps", bqk), RE(1)], writes=[Rr("attnT")])
                        brel(bqk)
                        bt, pm = h_tr(lambda j: qd["m"][:, j0 + j, :], [Rq("m")])
                        yield
                        evac_copy(qv("mT"), pm, [R("ps", bt)], [Rq("mT")])
                        brel(bt)
                        fw.op(DVE, lambda e: e.tensor_tensor(out=qv("X"), in0=qv("m"), in1=bc2(b32), op=ALU.mult), reads=[Rq("m"), mR], writes=[Rq("X")])
                        fw.op(DVE, lambda e: e.tensor_tensor(out=qv("m2"), in0=qv("m"), in1=bc2(b2), op=ALU.mult), reads=[Rq("m"), mR], writes=[Rq("m2")])
                        fw.op(DVE, lambda e: e.tensor_tensor(out=qv("XT"), in0=qv("mT"), in1=bc2(b32), op=ALU.mult), reads=[Rq("mT"), mR], writes=[Rq("XT")])
                        fw.op(DVE, lambda e: e.tensor_tensor(out=qv("m1T"), in0=qv("mT"), in1=bc2(b1T), op=ALU.mult), reads=[Rq("mT"), mR], writes=[Rq("m1T")])

                        def qmm(a, bname):
                            return h_mm(lambda j: qd[a][:, j0 + j, :], lambda j: qd[bname][:, j0 + j, :], [Rq(a), Rq(bname)])
                        for (a_, b__, dst_) in [("XT", "X", "P2"), ("X", "XT", "PT2"), ("PT2", "P2", "P"), ("P2", "PT2", "PT4"), ("PT4", "P", "P2"), ("P", "PT4", "PT8")]:
                            b_ = qmm(a_, b__)
                            yield
                            evac_copy(qv(dst_), p2(b_), [R("ps", b_)], [Rq(dst_)])
                            brel(b_)
                        b_ = qmm("PT8", "P2")
                        yield
                        fw.op(DVE, lambda e: e.tensor_tensor(out=qv("Ra"), in0=p2(b_), in1=bc2(ident_b), op=ALU.add), reads=[R("ps", b_), cR], writes=[Rq("Ra")])
                        brel(b_)
                        cur, nxt = "Ra", "Rb"
                        for pw in ["PT8", "PT4", "PT2"]:
                            b_ = qmm(pw, cur)
                            yield
                            fw.op(DVE, lambda e: e.tensor_tensor(out=qv(nxt), in0=p2(b_), in1=qv(cur), op=ALU.add),
                                  reads=[R("ps", b_), Rq(cur)], writes=[Rq(nxt)])
                            brel(b_)
                            cur, nxt = nxt, cur
                        b_ = qmm("XT", cur)
                        yield
                        fw.op(DVE, lambda e: e.tensor_tensor(out=qv("T0"), in0=qv(cur), in1=p2(b_), op=ALU.subtract),
                              reads=[R("ps", b_), Rq(cur)], writes=[Rq("T0")])
                        brel(b_)
                        bt, pm = h_tr(lambda j: qd["T0"][:, j0 + j, :], [Rq("T0")])
                        yield
                        evac_copy(qv("T0T"), pm, [R("ps", bt)], [Rq("T0T")])
                        brel(bt)
                        b_ = qmm("m1T", "T0")
                        yield
                        evac_copy(qv("Y"), p2(b_), [R("ps", b_)], [Rq("Y")])
                        brel(b_)
                        b_ = qmm("T0T", "Y")
                        yield
                        fw.op(DVE, lambda e: e.tensor_tensor(out=qv("T1"), in0=qv("T0"), in1=p2(b_), op=ALU.subtract), reads=[R("ps", b_), Rq("T0")], writes=[Rq("T1")])
                        brel(b_)
                        bt, pm = h_tr(lambda j: qd["T1"][:, j0 + j, :], [Rq("T1")])
                        yield
                        evac_copy(qv("T1T"), pm, [R("ps", bt)], [Rq("T1T")])
                        brel(bt)
                        b_ = qmm("m2", "T1T")
                        yield
                        evac_copy(qv("Yp"), p2(b_), [R("ps", b_)], [Rq("Yp")])
                        brel(b_)
                        b_ = qmm("T1", "Yp")
                        yield
                        fw.op(DVE, lambda e: e.tensor_tensor(out=rv("T2T"), in0=qv("T1T"), in1=p2(b_), op=ALU.subtract),
                              reads=[R("ps", b_), Rq("T1T")], writes=[Rr("T2T")])
                        brel(b_)
                        b_ = h_mm(lambda j: ring["kbe"][rs][:, j0 + j, :], lambda j: ring["T2T"][rs][:, j0 + j, :], [Rr("kbe"), Rr("T2T")])
                        yield
                        evac_copy(rv("nwT"), p2(b_), [R("ps", b_)], [Rr("nwT")])
                        brel(b_)

                    def seq_f(qi):
                        rs = qi % 2
                        t0 = qi * 4
                        oi = 0
                        if qi == 0:
                            fw.op(DVE, lambda e: e.memset(Sf[:], 0.0), reads=[], writes=[R("Sf")])
                            fw.op(DVE, lambda e: e.memset(Sb[:], 0.0), reads=[], writes=[R("Sb")])
                        yield from finish_o(l, li, hd, qi, oq[oi], R("oq", oi), 512, wz, wzR, wo, woR, ogb[0], rn, szq, ogb, rtmp, BK_AUX, won, mR, do_z=True, do_rest=False)
                        for j in range(4):
                            ti = t0 + j
                            vi = ti % 2
                            Rr = lambda nm: R("ring", nm, rs, j // 2)
                            b1 = bacq(BK_SEQ)

                            def mm1(e):
                                e.matmul(ps[b1][:, 0:128], ring["T2T"][rs][:, j, :], ring["vbt"][rs][:, j, :], start=True, stop=False)
                                return e.matmul(ps[b1][:, 0:128], ring["nwT"][rs][:, j, :], Sb[:], start=False, stop=True)
                            fw.op(PE, mm1, reads=[Rr("T2T"), Rr("vbt"), Rr("nwT"), R("Sb")], writes=[R("ps", b1)])
                            yield
                            fw.op(ACT, lambda e: e.copy(out=vnb[vi][:], in_=ps[b1][:, 0:128]), reads=[R("ps", b1)], writes=[R("vnbA", vi)])
                            brel(b1)
                            b2_ = bacq(BK_SEQ)

                            def mm2(e):
                                e.matmul(ps[b2_][:, 0:128], Sb[:], ring["qeT"][rs][:, j, :], start=True, stop=False)
                                e.matmul(ps[b2_][:, 0:128], vnb[vi][:], ring["attnT"][rs][:, j, :], start=False, stop=True)
                                return e.matmul(ps[b2_][:, 128:256], ring["kd"][rs][:, j, :], vnb[vi][:], start=True, stop=True)
                            fw.op(PE, mm2, reads=[R("Sb"), Rr("qeT"), R("vnbA", vi), Rr("attnT"), Rr("kd")], writes=[R("ps", b2_)])
                            yield
                            fw.op(DVE, lambda e: e.scalar_tensor_tensor(out=Sf[:], in0=Sf[:], scalar=edl_t[:, ti, hd:hd + 1], in1=ps[b2_][:, 128:256],
                                                                        op0=ALU.mult, op1=ALU.add), reads=[R("ps", b2_), R("Sf"), gR], writes=[R("Sf")])
                            fw.op(ACT, lambda e: e.copy(out=Sb[:], in_=Sf[:]), reads=[R("Sf")], writes=[R("Sb")])
                            fw.op(ACT, lambda e: e.copy(out=oq[oi][:, j * 128:(j + 1) * 128], in_=ps[b2_][:, 0:128]), reads=[R("ps", b2_)], writes=[R("oq", oi)])
                            brel(b2_)
                        if qi == 3:
                            fw.dma(SP, "spo", lambda e: e.dma_start(out=SP_out[li, hd], in_=Sf[:]), reads=[R("Sf")])
                        yield from finish_o(l, li, hd, qi, oq[oi], R("oq", oi), 512, wz, wzR, wo, woR, ogb[0], rn, szq, ogb, rtmp, BK_AUX, won, mR, do_z=False, do_rest=True)

                    def run_rr(gens):
                        gens = list(gens)
                        while gens:
                            for g_ in list(gens):
                                try:
                                    next(g_)
                                except StopIteration:
                                    gens.remove(g_)
                    def samples_gen():
                        for half in range(2):
                            s0 = half * 8
                            hs = slice(s0, s0 + 8)
                            fw.dma(SP, "ssi", lambda e: e.dma_start(out=Ss[:], in_=S_in[li, hs, hd].rearrange("s k v -> k s v")), writes=[R("Ss")])
                            b = bacq(BK_AUX)

                            def mms(e):
                                ins = None
                                for s_ in range(8):
                                    ins = e.matmul(ps[b][:, 2 * s_:2 * s_ + 2], Ss[:, s_, :], kqf[:, s0 + s_, :], start=True, stop=True)
                                return ins
                            fw.op(PE, mms, reads=[R("Ss"), R("kqf")], writes=[R("ps", b)])
                            yield
                            skq = ps[b][:, 0:16].rearrange("p (s c) -> p s c", c=2)
                            betaBC = sbc[:, hd, 0, hs]
                            egBC = sbc[:, hd, 1, hs]
                            fw.op(DVE, lambda e: e.tensor_tensor(out=stmp[1][:, 0:8], in0=skq[:, :, 0], in1=egBC, op=ALU.mult), reads=[R("ps", b), R("sbc")], writes=[R("sA", 1)])
                            fw.op(DVE, lambda e: e.tensor_tensor(out=stmp[2][:, 0:8], in0=skq[:, :, 1], in1=egBC, op=ALU.mult), reads=[R("ps", b), R("sbc")], writes=[R("sA", 2)])
                            brel(b)
                            fw.op(DVE, lambda e: e.tensor_tensor(out=stmp[1][:, 0:8], in0=vs_f[:, s0:s0 + 8], in1=stmp[1][:, 0:8], op=ALU.subtract), reads=[R("sA", 1), R("vs_f")], writes=[R("sA", 1)])
                            fw.op(DVE, lambda e: e.tensor_tensor(out=stmp[1][:, 0:8], in0=stmp[1][:, 0:8], in1=betaBC, op=ALU.mult), reads=[R("sA", 1), R("sbc")], writes=[R("sA", 1)])
                            fw.op(DVE, lambda e: e.tensor_tensor(out=stmp[0][:, 0:8], in0=kqf[:, hs, 0], in1=kqf[:, hs, 1], op=ALU.mult), reads=[R("kqf")], writes=[R("sA", 0)])
                            bqk = bacq(BK_AUX)
                            fw.op(PE, lambda e: e.matmul(ps[bqk][:, 0:8], ones_f[:], stmp[0][:, 0:8], start=True, stop=True), reads=[R("sA", 0), cR], writes=[R("ps", bqk)])
                            yield
                            fw.op(DVE, lambda e: e.tensor_tensor(out=stmp[3][:, 0:8], in0=ps[bqk][:, 0:8], in1=stmp[1][:, 0:8], op=ALU.mult), reads=[R("ps", bqk), R("sA", 1)], writes=[R("sA", 3)])
                            brel(bqk)
                            fw.op(DVE, lambda e: e.tensor_tensor(out=o_s[:, hs], in0=stmp[2][:, 0:8], in1=stmp[3][:, 0:8], op=ALU.add), reads=[R("sA", 2), R("sA", 3)], writes=[R("o_s")])
                            b3 = bacq(BK_AUX)
                            fw.op(PE, lambda e: e.transpose(ps[b3][0:8, 0:128], kqf[:, hs, 0], ident_f[:]), reads=[R("kqf"), cR], writes=[R("ps", b3)])
                            yield
                            fw.op(DVE, lambda e: e.tensor_copy(ktok[0:8, :], ps[b3][0:8, 0:128]), reads=[R("ps", b3)], writes=[R("ktok")])
                            brel(b3)
                            b4 = bacq(BK_AUX)
                            fw.op(PE, lambda e: e.transpose(ps[b4][0:8, 0:128], stmp[1][:, 0:8], ident_f[:]), reads=[R("sA", 1), cR], writes=[R("ps", b4)])
                            yield
                            fw.op(DVE, lambda e: e.tensor_copy(vtok[0:8, :], ps[b4][0:8, 0:128]), reads=[R("ps", b4)], writes=[R("vtok")])
                            brel(b4)
                            for s_ in range(8):
                                km = 0
                                fw.op(DVE, lambda e: e.tensor_scalar(out=kmask[km][0:8, :], in0=ktok[0:8, :], scalar1=ident_f[0:8, s_:s_ + 1], scalar2=None, op0=ALU.mult),
                                      reads=[R("ktok"), cR], writes=[R("kmask", km)])
                                b5 = bacq(BK_AUX)
                                fw.op(PE, lambda e: e.matmul(ps[b5][:, 0:128], kmask[km][0:8, :], vtok[0:8, :], start=True, stop=True),
                                      reads=[R("kmask", km), R("vtok")], writes=[R("ps", b5)])
                                yield
                                fw.op(DVE, lambda e: e.scalar_tensor_tensor(out=Ss[:, s_, :], in0=Ss[:, s_, :], scalar=sbc[:, hd, 1, s0 + s_:s0 + s_ + 1], in1=ps[b5][:, 0:128],
                                                                            op0=ALU.mult, op1=ALU.add), reads=[R("ps", b5), R("Ss"), R("sbc")], writes=[R("Ss")])
                                brel(b5)
                            fw.dma(SP, "sso", lambda e: e.dma_start(out=SS_out[li, hs, hd].rearrange("s k v -> k s v"), in_=Ss[:]), reads=[R("Ss")])
                        yield from finish_o(l, li, hd, 4, o_s, R("o_s"), 16, wz, wzR, wo, woR, sqb_s, rn_s, [szq_s], [ogb_s], rtmp_s, BK_AUX, won, mR, tag="s")

                    return dict(hc=half_chain, seq=seq_f, samples=samples_gen, pjob=pjob, sjob=sjob, load_qkv=load_qkv, load_zo=load_zo)

                HD = [make_head(h) for h in range(8)]
                tasks = {}

                def add(name, deps, fac):
                    tasks[name] = ([d for d in deps if d is not None], fac)

                def seqname(idx):
                    return ("SEQ", idx // 4, idx % 4) if idx >= 0 else None
                for h in range(8):
                    add(("LQ", h), ([("PJ", h - 1, 3, fi) for fi in range(3)] + [("SJ", h - 1, fi) for fi in range(3)]) if h > 0 else [],
                        lambda h=h: HD[h]["load_qkv"](h))
                    add(("LZ", h), [("SEQ", h - 2, 3), ("SAM", h - 2)] if h >= 2 else [], lambda h=h: HD[h]["load_zo"](h))
                    for t in range(4):
                        for fi in range(3):
                            deps = [("LQ", h)]
                            if h > 0:
                                deps += [("HC", h - 1, t, 0), ("HC", h - 1, t, 1)]
                            if t > 0:
                                deps.append(("PJ", h, t - 1, fi))
                            elif h > 0:
                                deps.append(("PJ", h - 1, 3, fi))
                            add(("PJ", h, t, fi), deps, lambda h=h, t=t, fi=fi: HD[h]["pjob"](h, t, fi, fi, BK_PJ))
                    for fi in range(3):
                        add(("SJ", h, fi), [("LQ", h)] + ([("SAM", h - 1), ("SJ", h - 1, fi)] if h > 0 else []),
                            lambda h=h, fi=fi: HD[h]["sjob"](h, fi, BK_PJ))
                    for q in range(4):
                        for hf in range(2):
                            deps = [("PJ", h, q, fi) for fi in range(3)]
                            if q > 0:
                                deps.append(("HC", h, q - 1, hf))
                            elif h > 0:
                                deps.append(("HC", h - 1, 3, hf))
                            deps.append(seqname(h * 4 + q - 2))
                            add(("HC", h, q, hf), deps, lambda h=h, q=q, hf=hf: HD[h]["hc"](q, hf))
                    for q in range(4):
                        add(("SEQ", h, q), [("HC", h, q, 0), ("HC", h, q, 1), ("LZ", h), seqname(h * 4 + q - 1)], lambda h=h, q=q: HD[h]["seq"](q))
                    add(("SAM", h), [("SJ", h, fi) for fi in range(3)] + [("LZ", h)] + ([("SAM", h - 1)] if h > 0 else []), lambda h=h: HD[h]["samples"]())
                prio = {"LQ": 0, "LZ": 0, "HC": 1, "SEQ": 2, "SAM": 3, "PJ": 4, "SJ": 5}
                order = sorted(tasks.keys(), key=lambda n: (n[1], prio[n[0]]) + tuple(n[2:]))
                started, done_t = set(), set()
                active = []
                while len(done_t) < len(tasks):
                    nstart = 0
                    for name in order:
                        if name in started:
                            continue
                        deps, fac = tasks[name]
                        if all(d in done_t for d in deps):
                            started.add(name)
                            nstart += 1
                            g_ = fac()
                            if g_ is None:
                                done_t.add(name)
                            else:
                                active.append((name, g_))
                    active.sort(key=lambda it: (prio[it[0][0]], it[0][1:]))
                    if not active and nstart == 0:
                        raise RuntimeError("scheduler deadlock: %s" % sorted(set(tasks) - done_t)[:5])
                    for it in list(active):
                        try:
                            next(it[1])
                        except StopIteration:
                            active.remove(it)
                            done_t.add(it[0])
                fw.barrier()

        octr = [0]

        def finish_o(l, li, hd, t, osrc, oR, n, wz, wzR, wo, woR, sqb, rn, szq, ogb, rtmp, BK_AUX, won, mR, tag=None, do_z=True, do_rest=True):
            cs, n = tcols(t)
            q = 0
            zq = 0
            KS = ("ogb", 0) if tag is None else ("sqbA", tag)
            KR = ("rn",) if tag is None else ("rn", tag)
            KZ = ("szq", 0) if tag is None else ("szq", 0, tag)
            KO = ("ogb", 0) if tag is None else ("ogb", 0, tag)
            KT = "rtmp" if tag is None else "rtmp_" + tag
            if do_z:
                bz = bacq(BK_AUX)

                for k2 in range(4):
                    def mm(e):
                        e.matmul(ps[bz][:, :n], wz[:, 2 * k2, :], hT[:, 2 * k2, cs], start=(k2 == 0), stop=False)
                        return e.matmul(ps[bz][:, :n], wz[:, 2 * k2 + 1, :], hT[:, 2 * k2 + 1, cs], start=False, stop=(k2 == 3))
                    fw.op(PE, mm, reads=[R("h", 2 * k2, t), R("h", 2 * k2 + 1, t), wzR], writes=[R("ps", bz)])
                    yield
                fw.op(ACT, lambda e: e.activation(out=szq[zq][:, :n], in_=ps[bz][:, :n], func=AF.Exp, scale=-1.0), reads=[R("ps", bz)], writes=[R(*KZ)])
                yield
                fw.op(ACT, lambda e: e.activation(out=szq[zq][:, :n], in_=szq[zq][:, :n], func=AF.Ln, bias=eps_t[:, 3:4], scale=1.0), reads=[R(*KZ), cR], writes=[R(*KZ)])
                yield
                fw.op(ACT, lambda e: e.activation(out=szq[zq][:, :n], in_=szq[zq][:, :n], func=AF.Exp, scale=-1.0), reads=[R(*KZ)], writes=[R(*KZ)])
                yield
                fw.op(DVE, lambda e: e.tensor_tensor(out=szq[zq][:, :n], in0=ps[bz][:, :n], in1=szq[zq][:, :n], op=ALU.mult), reads=[R("ps", bz), R(*KZ)], writes=[R(*KZ)])
                brel(bz)
                yield
            if not do_rest:
                return
            b = bacq(BK_AUX)
            fw.op(ACT, lambda e: e.activation(out=sqb[:, :n], in_=osrc[:, :n], func=AF.Square), reads=[oR], writes=[R(*KS)])
            fw.op(PE, lambda e: e.matmul(ps[b][:, :n], ones_b[:], sqb[:, :n], start=True, stop=True), reads=[R(*KS), cR], writes=[R("ps", b)])
            yield
            fw.op(ACT, lambda e: e.activation(out=rn[:, :n], in_=ps[b][:, :n], func=AF.Ln, bias=eps_t[:, 1:2], scale=1.0 / 128), reads=[R("ps", b), cR], writes=[R(*KR)])
            brel(b)
            yield
            fw.op(ACT, lambda e: e.activation(out=rn[:, :n], in_=rn[:, :n], func=AF.Exp, scale=-0.5), reads=[R(*KR)], writes=[R(*KR)])
            yield
            fw.op(DVE, lambda e: e.tensor_tensor(out=rn[:, :n], in0=osrc[:, :n], in1=rn[:, :n], op=ALU.mult), reads=[oR, R(*KR)], writes=[R(*KR)])
            yield
            fw.op(DVE, lambda e: e.scalar_tensor_tensor(out=ogb[q][:, :n], in0=rn[:, :n], scalar=won[:, li:li + 1], in1=szq[zq][:, :n], op0=ALU.mult, op1=ALU.mult),
                  reads=[R(*KR), R(*KZ), mR], writes=[R(*KO)])
            for m in range(8):
                bo = bacq(BK_AUX)
                fw.op(PE, lambda e: e.matmul(ps[bo][:, :n], wo[:, m * 128:(m + 1) * 128], ogb[q][:, :n], start=True, stop=True),
                      reads=[R(*KO), woR], writes=[R("ps", bo)])
                yield
                resid_update(rtmp, 1, m, t, bo, None, KT)
                brel(bo)

        def final_norm():
            with contextlib.ExitStack() as ph:
                def dst(c, t, src, srcR):
                    if t < 4:
                        cs, n = tcols(t)
                        fw.dma(SP, "yo_" + srcR.name, lambda e: e.dma_start(out=yT_out[c * 128:(c + 1) * 128, cs], in_=src[:]), reads=[srcR])
                    else:
                        fw.dma(SP, "yo4", lambda e: e.dma_start(out=yT_out[:, TP:T].rearrange("(c p) n -> p c n", p=128), in_=src[:]), reads=[srcR])
                fnA = fw.sbuf("fnA", [128, 8, 1], F32, ph)
                fw.op(DVE, lambda e: e.tensor_scalar(out=fnA[:, :, 0], in0=fnw_s[:], scalar1=32.0, scalar2=None, op0=ALU.mult), reads=[cR], writes=[cR])
                norm_to_h(ph, fnA, None, dst)
                fw.barrier()

        def dump_x():
            for c in range(8):
                fw.dma(SP, "yo%d" % (c % 4), lambda e, c=c: e.dma_start(out=yT_out[c * 128:(c + 1) * 128, :], in_=xT[:, c, :]), reads=[R("x", c, t) for t in range(5)])
            fw.barrier()

        stage = 0

        def check_stop():
            nonlocal stage
            stage += 1
            return stop_after is not None and stage >= stop_after
        done = False
        for l in range(DEPTH):
            if stop_after is not None:
                ada(l)
                ffn(l, 0, 0)
            elif l == 0:
                with contextlib.ExitStack() as ph0:
                    for _ in ada_gen(0, ph0, 0, 6, (0,)):
                        pass
                    fw.barrier()
                ffn(0, 0, 0, ada_rest=0)
            else:
                ffn(l, 0, 0)
            if check_stop():
                done = True
                break
            if l % 2 == 0:
                mixer_a(l, l // 2)
            else:
                mixer_b(l, l // 2)
            if check_stop():
                done = True
                break
            ffn(l, 1, 2, next_ada=(l + 1 if (l + 1 < DEPTH and stop_after is None) else None))
            if check_stop():
                done = True
                break
        if done:
            dump_x()
        else:
            final_norm()
        fw.finish()
    return nc


_NC_CACHE = {}


def _prep_shared(inp):
    f = lambda a: np.ascontiguousarray(np.asarray(a, dtype=np.float32))
    sh = {}
    sh["w_ada"] = f(inp["w_ada"])
    sh["b_ada_p"] = f(inp["b_ada"].reshape(DEPTH, 72, 128).transpose(2, 0, 1).reshape(128, DEPTH * 72))
    sh["norm_w_p"] = f(inp["norm_w"].reshape(DEPTH, 3, 8, 128).transpose(3, 0, 1, 2).reshape(128, DEPTH * 24))
    sh["ffn_w_gu"] = f(inp["ffn_w_gu"])
    sh["ffn_w_down"] = f(inp["ffn_w_down"])
    sh["a_w_in"] = f(inp["a_w_in"])
    sh["a_w_conv_p"] = f(inp["a_w_conv"].reshape(2, 4, 24, 128).transpose(3, 0, 1, 2).reshape(128, 2 * 4 * 24))
    sh["a_log"] = f(inp["a_log"])
    sh["a_dt_bias"] = f(inp["a_dt_bias"])
    sh["a_w_onorm_p"] = f(inp["a_w_onorm"].T)
    sh["a_w_out"] = f(inp["a_w_out"])
    sh["b_w_in"] = f(inp["b_w_in"])
    sh["b_vnorm_w"] = f(inp["b_vnorm_w"])
    sh["b_vnorm_b"] = f(inp["b_vnorm_b"])
    sh["b_w_sT"] = f(np.asarray(inp["b_w_s"]).transpose(0, 1, 3, 2))
    sh["b_w_s00"] = f(np.asarray(inp["b_w_s"])[:, :, 0, 0])
    sh["b_b_s"] = f(inp["b_b_s"])
    sh["b_b_s0"] = f(np.asarray(inp["b_b_s"])[:, :, 0])
    sh["b_w_out"] = f(inp["b_w_out"])
    sh["fnw_p"] = f(np.asarray(inp["final_norm_w"]).reshape(8, 128).T)
    return sh


def _prep_core(inp, i):
    f = lambda a: np.ascontiguousarray(np.asarray(a, dtype=np.float32))
    xp = np.asarray(inp["x_prompt"])[i]
    xs = np.asarray(inp["x_sample"])[i * NS:(i + 1) * NS, 0]
    d = {}
    d["xT"] = f(np.concatenate([xp, xs], axis=0).T)
    d["cT"] = f(np.concatenate([np.asarray(inp["c_prompt"])[i:i + 1], np.asarray(inp["c_sample"])[i * NS:(i + 1) * NS]], axis=0).T)
    d["convT"] = f(np.asarray(inp["state_a_conv"])[:, i * NS:(i + 1) * NS].transpose(0, 3, 1, 2))
    d["S_in"] = f(np.asarray(inp["state_a_S"])[:, i * NS:(i + 1) * NS])
    return d


def kernel(**inputs):
    stop_after = inputs.pop("_stop_after", None)
    key = stop_after
    if key not in _NC_CACHE:
        _NC_CACHE[key] = build_program(stop_after)
    nc = _NC_CACHE[key]
    sh = _prep_shared(inputs)
    in_maps = []
    for i in range(8):
        d = dict(sh)
        d.update(_prep_core(inputs, i))
        in_maps.append(d)
    res = run_bass_kernel_spmd(nc, in_maps, core_ids=list(range(8)))
    rs = res.results
    y_prompt = np.stack([rs[i]["yT"][:, :TP].T for i in range(8)], axis=0).astype(np.float32)
    y_sample = np.concatenate([rs[i]["yT"][:, TP:].T for i in range(8)], axis=0)[:, None, :].astype(np.float32)
    conv_prompt = np.stack([rs[i]["convP"].transpose(0, 2, 1) for i in range(8)], axis=1).astype(np.float32)
    S_prompt = np.stack([rs[i]["S_p"] for i in range(8)], axis=1).astype(np.float32)
    conv_sample = np.concatenate([rs[i]["convS"].transpose(0, 2, 3, 1) for i in range(8)], axis=1).astype(np.float32)
    S_sample = np.concatenate([rs[i]["S_s"] for i in range(8)], axis=1).astype(np.float32)
    v_sample = np.concatenate([rs[i]["v_s"] for i in range(8)], axis=1)[:, :, None, :].astype(np.float32)
    return (np.ascontiguousarray(y_prompt), np.ascontiguousarray(y_sample), np.ascontiguousarray(conv_prompt), np.ascontiguousarray(S_prompt),
            np.ascontiguousarray(conv_sample), np.ascontiguousarray(S_sample), np.ascontiguousarray(v_sample))
```
